# Optimizing a Trainium2 kernel written in Bass

```python
import math
import jax, jax.numpy as jnp
from jax import lax
import numpy as np

D_MODEL = 1024
BATCH = 2
SEQ = 8192
DEPTH = 2

N_A_LAYERS = (DEPTH + 1) // 2
N_B_LAYERS = DEPTH // 2
DEEPNORM_ALPHA = (2.0 * DEPTH) ** 0.25
DEEPNORM_BETA = (8.0 * DEPTH) ** -0.25
LN_EPS = 1e-5
RMS_EPS = 1e-5

GLA_HEADS = 4
GLA_DK_TOTAL = D_MODEL // 2
GLA_DV_TOTAL = D_MODEL
GLA_DK = GLA_DK_TOTAL // GLA_HEADS
GLA_DV = GLA_DV_TOTAL // GLA_HEADS
GLA_GATE_RANK = 16
GLA_GATE_TAU = 16.0
GLA_CHUNK = 64
GLA_IN = 2 * GLA_DK_TOTAL + 2 * GLA_DV_TOTAL + GLA_GATE_RANK

SWA_GROUPS = ((128, 1), (512, 4), (2048, 16))
SWA_N_GROUPS = len(SWA_GROUPS)
SWA_HEAD_DIM = 128
SWA_Q_HEADS = 8
SWA_KV_HEADS = 2
SWA_REP = SWA_Q_HEADS // SWA_KV_HEADS
SWA_OUT = SWA_Q_HEADS * SWA_HEAD_DIM
SWA_Q_TOTAL = SWA_N_GROUPS * SWA_OUT
SWA_IN = SWA_Q_TOTAL + SWA_OUT
SWA_KV_HALF = SWA_N_GROUPS * SWA_KV_HEADS * SWA_HEAD_DIM
SWA_KV_TOTAL = 2 * SWA_KV_HALF
SWA_BLOCK = 128
ROPE_THETA = 10000.0

kernel_name = "yoco_gla_dilated_window_hybrid"


def layer_norm(x, g, b):
    x32 = x.astype(jnp.float32)
    mu = jnp.mean(x32, axis=-1, keepdims=True)
    xc = x32 - mu
    var = jnp.mean(xc * xc, axis=-1, keepdims=True)
    return (xc * lax.rsqrt(var + LN_EPS) * g.astype(jnp.float32) + b.astype(jnp.float32)).astype(x.dtype)


def rope(x, pos):
    e = x.shape[-1]
    half = e // 2
    inv = ROPE_THETA ** (-(jnp.arange(half, dtype=jnp.float32) * 2.0) / e)
    ang = pos.astype(jnp.float32)[:, None] * inv[None, :]
    cos = jnp.cos(ang)[None, :, None, :]
    sin = jnp.sin(ang)[None, :, None, :]
    x32 = x.astype(jnp.float32)
    x1, x2 = x32[..., :half], x32[..., half:]
    return jnp.concatenate([x1 * cos - x2 * sin, x2 * cos + x1 * sin], axis=-1)


def gla_chunked(q, k, v, log_a):
    bsz, s, h, dk = q.shape
    dv = v.shape[-1]
    c = GLA_CHUNK
    n = s // c

    def to_chunks(t):
        return t.reshape(bsz, n, c, h, t.shape[-1]).transpose(1, 0, 3, 2, 4)

    qc_all, kc_all, vc_all = to_chunks(q), to_chunks(k), to_chunks(v)
    b_all = jnp.cumsum(to_chunks(log_a), axis=3)
    causal = jnp.tril(jnp.ones((c, c), dtype=bool))[None, None, :, :, None]

    def step(state, inp):
        qc, kc, vc, bc = inp
        o_inter = jnp.einsum('bhcd,bhde->bhce', qc * jnp.exp(bc), state)
        diff = bc[:, :, :, None, :] - bc[:, :, None, :, :]
        decay = jnp.where(causal, jnp.exp(jnp.where(causal, diff, 0.0)), 0.0)
        att = jnp.einsum('bhid,bhjd,bhijd->bhij', qc, kc, decay)
        o_intra = jnp.einsum('bhij,bhje->bhie', att, vc)
        b_last = bc[:, :, -1:, :]
        new_state = jnp.exp(b_last[:, :, 0, :])[..., None] * state + \
            jnp.einsum('bhcd,bhce->bhde', kc * jnp.exp(b_last - bc), vc)
        return new_state, o_inter + o_intra

    state0 = jnp.zeros((bsz, h, dk, dv), dtype=jnp.float32)
    _, o = lax.scan(step, state0, (qc_all, kc_all, vc_all, b_all))
    return o.transpose(1, 0, 3, 2, 4).reshape(bsz, s, h, dv)


def gla_mixer(x, w_in, w_a2, b_a2, norm_g, w_out):
    bsz, s, _ = x.shape
    proj = x @ w_in
    q, k, v, g, a_lr = jnp.split(
        proj, [GLA_DK_TOTAL, 2 * GLA_DK_TOTAL, 2 * GLA_DK_TOTAL + GLA_DV_TOTAL,
               2 * GLA_DK_TOTAL + 2 * GLA_DV_TOTAL], axis=-1)
    q = q.astype(jnp.float32).reshape(bsz, s, GLA_HEADS, GLA_DK) * (GLA_DK ** -0.5)
    k = k.astype(jnp.float32).reshape(bsz, s, GLA_HEADS, GLA_DK)
    v = v.astype(jnp.float32).reshape(bsz, s, GLA_HEADS, GLA_DV)
    log_a = jax.nn.log_sigmoid((a_lr @ w_a2 + b_a2).astype(jnp.float32)) / GLA_GATE_TAU
    log_a = log_a.reshape(bsz, s, GLA_HEADS, GLA_DK)
    o = gla_chunked(q, k, v, log_a)
    o = o * lax.rsqrt(jnp.mean(o * o, axis=-1, keepdims=True) + RMS_EPS)
    o = o.reshape(bsz, s, GLA_DV_TOTAL) * norm_g.astype(jnp.float32)
    return (o.astype(x.dtype) * jax.nn.silu(g)) @ w_out


def shared_kv(x, w_kv):
    bsz, s, _ = x.shape
    pos = jnp.arange(s)
    kv = x @ w_kv
    k, v = jnp.split(kv, [SWA_KV_HALF], axis=-1)
    k = rope(k.reshape(bsz, s, SWA_N_GROUPS * SWA_KV_HEADS, SWA_HEAD_DIM), pos)
    k = k.reshape(bsz, s, SWA_N_GROUPS, SWA_KV_HEADS, SWA_HEAD_DIM)
    v = v.astype(jnp.float32).reshape(bsz, s, SWA_N_GROUPS, SWA_KV_HEADS, SWA_HEAD_DIM)
    return k, v


def dilated_window_attention(q, k, v, window, dilation):
    bsz, s, hq, e = q.shape
    hkv = k.shape[2]
    rep = hq // hkv
    blk = SWA_BLOCK
    w_sub = window // dilation
    span = dilation * blk
    seq_pad = -(-s // span) * span
    n_sub = seq_pad // dilation
    nb = n_sub // blk

    def to_strided(t):
        t = jnp.pad(t, ((0, 0), (0, seq_pad - s), (0, 0), (0, 0)))
        t = t.reshape(bsz, n_sub, dilation, t.shape[2], e).transpose(0, 2, 1, 3, 4)
        return t.reshape(bsz, dilation, nb, blk, t.shape[3], e)

    qb = to_strided(q).reshape(bsz, dilation, nb, blk, hkv, rep, e)
    kb, vb = to_strided(k), to_strided(v)

    def with_prev(t):
        prev = jnp.pad(t, ((0, 0), (0, 0), (1, 0), (0, 0), (0, 0), (0, 0)))[:, :, :-1]
        return jnp.concatenate([prev, t], axis=3)

    kcat, vcat = with_prev(kb), with_prev(vb)
    scores = jnp.einsum('brnqgpe,brnkge->brngpqk', qb, kcat)
    qi = jnp.arange(blk)[:, None]
    kj = jnp.arange(2 * blk)[None, :]
    rel = blk + qi - kj
    band = (rel >= 0) & (rel <= w_sub)
    mask = band[None] & ((jnp.arange(nb)[:, None, None] > 0) | (kj[None] >= blk))
    scores = jnp.where(mask[None, None, :, None, None], scores, -jnp.inf)
    m = jnp.max(scores, axis=-1, keepdims=True)
    p = jnp.exp(scores - m)
    den = jnp.sum(p, axis=-1)
    o = jnp.einsum('brngpqk,brnkge->brnqgpe', p, vcat) / jnp.moveaxis(den, -1, 3)[..., None]
    lse = jnp.moveaxis(m[..., 0] + jnp.log(den), -1, 3)

    def from_strided(t):
        t = t.reshape((bsz, dilation, n_sub) + t.shape[6 - 2 + 1:] if False else (bsz, dilation, n_sub, -1))
        return t

    o = o.reshape(bsz, dilation, n_sub, hq, e).transpose(0, 2, 1, 3, 4).reshape(bsz, seq_pad, hq, e)[:, :s]
    lse = lse.reshape(bsz, dilation, n_sub, hq).transpose(0, 2, 1, 3).reshape(bsz, seq_pad, hq)[:, :s]
    return o, lse


def dilated_mixer(x, w_in, w_out, k_sh, v_sh):
    bsz, s, _ = x.shape
    pos = jnp.arange(s)
    proj = x @ w_in
    q, g = jnp.split(proj, [SWA_Q_TOTAL], axis=-1)
    q = rope(q.reshape(bsz, s, SWA_N_GROUPS * SWA_Q_HEADS, SWA_HEAD_DIM), pos) * (SWA_HEAD_DIM ** -0.5)
    q = q.reshape(bsz, s, SWA_N_GROUPS, SWA_Q_HEADS, SWA_HEAD_DIM)
    outs, lses = [], []
    for gi, (window, dilation) in enumerate(SWA_GROUPS):
        o_g, lse_g = dilated_window_attention(q[:, :, gi], k_sh[:, :, gi], v_sh[:, :, gi], window, dilation)
        outs.append(o_g)
        lses.append(lse_g)
    wts = jax.nn.softmax(jnp.stack(lses, axis=0), axis=0)
    o = jnp.sum(wts[..., None] * jnp.stack(outs, axis=0), axis=0)
    o = o.reshape(bsz, s, SWA_OUT)
    return (o.astype(x.dtype) * jax.nn.silu(g)) @ w_out


def setup_inputs(seed: int = 0) -> dict:
    key = jax.random.key(seed)
    ks = jax.random.split(key, 12)
    f32 = jnp.float32
    x = jax.random.normal(ks[0], (BATCH, SEQ, D_MODEL), f32)
    gla_w_in = jax.random.normal(ks[1], (N_A_LAYERS, D_MODEL, GLA_IN), f32) * D_MODEL ** -0.5
    gla_w_a2 = jax.random.normal(ks[2], (N_A_LAYERS, GLA_GATE_RANK, GLA_DK_TOTAL), f32) * GLA_GATE_RANK ** -0.5
    gla_b_a2 = jax.random.normal(ks[3], (N_A_LAYERS, GLA_DK_TOTAL), f32) * 0.1
    gla_norm_g = 1.0 + 0.02 * jax.random.normal(ks[4], (N_A_LAYERS, GLA_DV_TOTAL), f32)
    gla_w_out = jax.random.normal(ks[5], (N_A_LAYERS, GLA_DV_TOTAL, D_MODEL), f32) * (GLA_DV_TOTAL ** -0.5 * DEEPNORM_BETA)
    w_kv = jax.random.normal(ks[6], (D_MODEL, SWA_KV_TOTAL), f32) * D_MODEL ** -0.5
    swa_w_in = jax.random.normal(ks[7], (N_B_LAYERS, D_MODEL, SWA_IN), f32) * D_MODEL ** -0.5
    swa_w_out = jax.random.normal(ks[8], (N_B_LAYERS, SWA_OUT, D_MODEL), f32) * (SWA_OUT ** -0.5 * DEEPNORM_BETA)
    ln_g = 1.0 + 0.02 * jax.random.normal(ks[9], (DEPTH, D_MODEL), f32)
    ln_b = 0.02 * jax.random.normal(ks[10], (DEPTH, D_MODEL), f32)
    return {"x": x, "gla_w_in": gla_w_in, "gla_w_a2": gla_w_a2, "gla_b_a2": gla_b_a2,
            "gla_norm_g": gla_norm_g, "gla_w_out": gla_w_out, "w_kv": w_kv,
            "swa_w_in": swa_w_in, "swa_w_out": swa_w_out, "ln_g": ln_g, "ln_b": ln_b}


def reference(x, gla_w_in, gla_w_a2, gla_b_a2, gla_norm_g, gla_w_out, w_kv,
              swa_w_in, swa_w_out, ln_g, ln_b):
    k_sh = None
    v_sh = None
    for layer in range(DEPTH):
        if layer < N_A_LAYERS:
            y = gla_mixer(x, gla_w_in[layer], gla_w_a2[layer], gla_b_a2[layer],
                          gla_norm_g[layer], gla_w_out[layer])
        else:
            if layer == N_A_LAYERS:
                k_sh, v_sh = shared_kv(x, w_kv)
            i = layer - N_A_LAYERS
            y = dilated_mixer(x, swa_w_in[i], swa_w_out[i], k_sh, v_sh)
        x = layer_norm(DEEPNORM_ALPHA * x + y, ln_g[layer], ln_b[layer])
    return x
```

```python
import contextlib
import numpy as np
import ml_dtypes
import concourse.bass as bass
import concourse.mybir as mybir
from concourse.bass_utils import run_bass_kernel_spmd

F32 = mybir.dt.float32
BF16 = mybir.dt.bfloat16
ALU = mybir.AluOpType
AF = mybir.ActivationFunctionType
AX = mybir.AxisListType

D = 1024
SEQ = 8192
BATCH = 2
KC = 8
ALPHA = (2.0 * 2) ** 0.25
LN_EPS = 1e-5
RMS_EPS = 1e-5


class Buf:
    __slots__ = ("name", "w", "r", "dsem", "dcnt", "dgen")

    def __init__(self, name):
        self.name = name
        self.w = None
        self.r = {}
        self.dsem = None
        self.dcnt = 0
        self.dgen = -1


class FW:
    def __init__(self, nc, es):
        self.nc = nc
        self.es = es
        self.eng = {"pe": nc.tensor, "act": nc.scalar, "dve": nc.vector, "pool": nc.gpsimd, "sp": nc.sync}
        self.sems = []
        self.esem = {}
        self.cnt = {}
        self.known = {}
        for e in self.eng:
            self.esem[e] = self._newsem("p_" + e)
            self.cnt[e] = 0
            self.known[e] = {}
        self.nbuf = 0
        self.final = {}
        self.dtot = {}
        self.gen = 0
        self.free_dsems = {}
        self.dq = {}

    def _newsem(self, name):
        h = self.es.enter_context(self.nc.semaphore(name))
        self.sems.append(h)
        return len(self.sems) - 1

    def buf(self, name=None):
        self.nbuf += 1
        return Buf(name or ("b%d" % self.nbuf))

    def bufs(self, n, name=None):
        return [self.buf(None if name is None else "%s%d" % (name, i)) for i in range(n)]

    def _wait(self, e, ev):
        si, val = ev
        if e == "pe" and si == self.esem["pe"]:
            return
        k = self.known[e]
        if k.get(si, 0) >= val:
            return
        self.eng[e].wait_ge(self.sems[si], val)
        k[si] = val

    def _deps(self, e, reads, writes):
        evs = {}
        for b in reads:
            if b.w is not None:
                evs[b.w[0]] = max(evs.get(b.w[0], 0), b.w[1])
        for b in writes:
            if b.w is not None:
                evs[b.w[0]] = max(evs.get(b.w[0], 0), b.w[1])
            for si, v in b.r.items():
                evs[si] = max(evs.get(si, 0), v)
        for si, v in evs.items():
            self._wait(e, (si, v))

    def _mark(self, ev, reads, writes):
        for b in reads:
            b.r[ev[0]] = max(b.r.get(ev[0], 0), ev[1])
        for b in writes:
            b.w = ev
            b.r = {}

    def op(self, e, fn, reads=(), writes=(), inc=True):
        self._deps(e, reads, writes)
        ins = fn(self.eng[e])
        si = self.esem[e]
        if inc:
            self.cnt[e] += 1
            ins.then_inc(self.sems[si], 1)
            ev = (si, self.cnt[e])
            self.known[e][si] = max(self.known[e].get(si, 0), 0)
        else:
            ev = (si, self.cnt[e] + 1)
        self._mark(ev, reads, writes)
        return ins

    def dma(self, q, out, in_, reads=(), writes=(), owner=None, final=False, **kw):
        self._deps(q, reads, writes)
        if owner is None:
            owner = writes[0] if writes else reads[0]
        if owner.dsem is None or owner.dgen != self.gen:
            if self.free_dsems.get(q):
                owner.dsem = self.free_dsems[q].pop()
                owner.dcnt = self.dtot.get(owner.dsem, 0)
            else:
                owner.dsem = self._newsem("d_" + owner.name)
                owner.dcnt = 0
                self.dq[owner.dsem] = q
            owner.dgen = self.gen
        ins = self.eng[q].dma_start(out=out, in_=in_, **kw)
        owner.dcnt += 16
        ins.then_inc(self.sems[owner.dsem], 16)
        ev = (owner.dsem, owner.dcnt)
        self.dtot[owner.dsem] = owner.dcnt
        self._mark(ev, reads, writes)
        if final:
            self.final[ev[0]] = max(self.final.get(ev[0], 0), ev[1])
        return ins

    def barrier(self):
        evs = dict(self.dtot)
        for e in self.eng:
            if self.cnt[e] > 0:
                evs[self.esem[e]] = self.cnt[e]
        for e in self.eng:
            for si, v in evs.items():
                self._wait(e, (si, v))
        self.gen += 1
        self.free_dsems = {}
        for si in sorted(self.dtot.keys(), reverse=True):
            self.free_dsems.setdefault(self.dq[si], []).append(si)

    def finish(self, e="sp"):
        for si, v in self.final.items():
            self._wait(e, (si, v))


def _sb(nc, es, name, shape, dt):
    return es.enter_context(nc.sbuf_tensor(name, shape, dt))


def _ps(nc, es, name, shape, dt):
    return es.enter_context(nc.psum_tensor(name, shape, dt))


def build_l1(S=SEQ):
    TT = 512
    NT = S // TT
    nc = bass.Bass("TRN2", target_bir_lowering=False)
    xT = nc.dram_tensor("xT", [D, S], F32, kind="ExternalInput").ap()
    wsel = nc.dram_tensor("wsel", [D, 784], F32, kind="ExternalInput").ap()
    wa2 = nc.dram_tensor("wa2", [16, 128], F32, kind="ExternalInput").ap()
    ba2 = nc.dram_tensor("ba2", [128, 1], F32, kind="ExternalInput").ap()
    ngb = nc.dram_tensor("ngb", [128, 256], F32, kind="ExternalInput").ap()
    cmask = nc.dram_tensor("cmask", [128, 128], F32, kind="ExternalInput").ap()
    ident = nc.dram_tensor("ident", [128, 128], F32, kind="ExternalInput").ap()
    og = nc.dram_tensor("og", [S, 256], BF16, kind="ExternalOutput").ap()

    es = contextlib.ExitStack()
    with es:
        fw = FW(nc, es)
        w_sb = _sb(nc, es, "w_sb", [128, KC, 784], BF16)
        wa2_sb = _sb(nc, es, "wa2_sb", [16, 128], F32)
        nb_sb = _sb(nc, es, "nb_sb", [128, 1], F32)
        ng_sb = _sb(nc, es, "ng_sb", [128, 256], F32)
        cm_sb = _sb(nc, es, "cm_sb", [128, 128], F32)
        id_sb = _sb(nc, es, "id_sb", [128, 128], BF16)
        rm_sb = _sb(nc, es, "rm_sb", [128, TT], F32)
        S_sb = _sb(nc, es, "S_sb", [128, 256], F32)
        Sb_sb = _sb(nc, es, "Sb_sb", [128, 256], BF16)
        xTb = [_sb(nc, es, "xTb%d" % i, [128, KC, TT], BF16) for i in range(2)]
        aT_sb = _sb(nc, es, "aT_sb", [16, TT], F32)
        e1_sb = _sb(nc, es, "e1_sb", [128, TT], F32)
        sp_sb = _sb(nc, es, "sp_sb", [128, TT], F32)
        cs_sb = _sb(nc, es, "cs_sb", [128, TT], F32)
        eq_sb = [_sb(nc, es, "eq_sb%d" % i, [128, TT], F32) for i in range(2)]
        ek_sb = _sb(nc, es, "ek_sb", [128, TT], F32)
        Qt = [_sb(nc, es, "Qt%d" % i, [128, TT], BF16) for i in range(2)]
        Kt = [_sb(nc, es, "Kt%d" % i, [128, TT], BF16) for i in range(2)]
        Kh = [_sb(nc, es, "Kh%d" % i, [128, TT], BF16) for i in range(2)]
        Khtok = [_sb(nc, es, "Khtok%d" % i, [128, 4, 128], BF16) for i in range(2)]
        v_sb = [_sb(nc, es, "v_sb%d" % i, [128, 4, 256], BF16) for i in range(2)]
        gs_sb = [_sb(nc, es, "gs_sb%d" % i, [128, 4, 256], F32) for i in range(2)]
        gn_sb = [_sb(nc, es, "gn_sb%d" % i, [128, 4, 256], F32) for i in range(2)]
        og_sb = [_sb(nc, es, "og_sb%d" % i, [128, 4, 256], BF16) for i in range(2)]
        att_sb = [_sb(nc, es, "att_sb%d" % i, [128, 128], BF16) for i in range(2)]
        sq_sb = _sb(nc, es, "sq_sb", [128, 256], F32)
        ss_sb = _sb(nc, es, "ss_sb", [128, 1], F32)
        rs_sb = _sb(nc, es, "rs_sb", [128, 1], F32)
        q_ps = _ps(nc, es, "q_ps", [128, TT], F32)
        k_ps = _ps(nc, es, "k_ps", [128, TT], F32)
        az_ps = _ps(nc, es, "az_ps", [128, TT], F32)
        vg_ps1 = _ps(nc, es, "vg_ps", [128, 512], F32)
        vg_ps = [vg_ps1, vg_ps1]
        at_ps = _ps(nc, es, "at_ps", [128, 512], F32)
        tr_ps = _ps(nc, es, "tr_ps", [128, 4, 256], BF16)
        o_ps = _ps(nc, es, "o_ps", [128, 512], F32)
        kv_ps = _ps(nc, es, "kv_ps", [128, 512], F32)

        B = {}
        for n in ["w", "wa2", "nb", "ng", "cm", "id", "rm", "S", "Sb", "aT", "e1", "sp", "cs", "ek",
                  "sq", "ss", "rs", "q_ps", "k_ps", "az_ps", "at_ps", "o_ps", "kv_ps", "og_out"]:
            B[n] = fw.buf(n)
        for n in ["xTb", "eq", "Qt", "Kt", "Kh", "Khtok", "og", "att"]:
            B[n] = fw.bufs(2, n)
        _vg = fw.buf("vg_ps")
        B["vg_ps"] = [_vg, _vg]
        for n in ["v", "gs", "gn"]:
            B[n] = [fw.bufs(4, n + str(i) + "_") for i in range(2)]
        B["tr_ps"] = fw.buf("tr_ps")

        fw.dma("pool", w_sb[:], wsel.rearrange("(kc p) n -> p kc n", p=128), writes=[B["w"]])
        fw.dma("sp", wa2_sb[:], wa2[:, :], writes=[B["wa2"]])
        fw.dma("sp", nb_sb[:], ba2[:, :], writes=[B["nb"]])
        fw.dma("sp", ng_sb[:], ngb[:, :], writes=[B["ng"]])
        fw.dma("sp", cm_sb[:], cmask[:, :], writes=[B["cm"]])
        fw.dma("pool", id_sb[:], ident[:, :], writes=[B["id"]])
        fw.op("dve", lambda e: e.tensor_scalar(out=nb_sb[:], in0=nb_sb[:], scalar1=-1.0, scalar2=None, op0=ALU.mult),
              reads=[B["nb"]], writes=[B["nb"]])
        fw.op("dve", lambda e: e.memset(rm_sb[:], 1.0), writes=[B["rm"]])
        for c in range(4):
            fw.op("dve", lambda e, c=c: e.memset(rm_sb[:, c * 128:c * 128 + 1], 0.0), writes=[B["rm"]])
        fw.op("dve", lambda e: e.memset(S_sb[:], 0.0), writes=[B["S"]])
        fw.op("dve", lambda e: e.memset(Sb_sb[:], 0.0), writes=[B["Sb"]])

        xT_v = xT.rearrange("(kc p) t -> p kc t", p=128)
        og_v = og.rearrange("(n c p) e -> n p c e", c=4, p=128)
        QSCALE = 128.0 ** -0.5

        for t in range(NT):
            p = t % 2
            t0 = t * TT
            fw.dma("pool", xTb[p][:], xT_v[:, :, t0:t0 + TT], writes=[B["xTb"][p]])
            for (ps, pb, c0, m) in ((q_ps, B["q_ps"], 0, 128), (k_ps, B["k_ps"], 128, 128), (az_ps, B["az_ps"], 768, 16)):
                for kc in range(KC):
                    fw.op("pe", lambda e, ps=ps, c0=c0, m=m, kc=kc: e.matmul(
                        ps[0:m, :], lhsT=w_sb[:, kc, c0:c0 + m], rhs=xTb[p][:, kc, :],
                        start=(kc == 0), stop=(kc == KC - 1)),
                        reads=[B["w"], B["xTb"][p]], writes=[pb], inc=(kc == KC - 1))
            fw.op("act", lambda e: e.activation(out=aT_sb[:], in_=az_ps[0:16, :], func=AF.Copy),
                  reads=[B["az_ps"]], writes=[B["aT"]])
            fw.op("pe", lambda e: e.matmul(az_ps[:], lhsT=wa2_sb[:], rhs=aT_sb[:], start=True, stop=True),
                  reads=[B["wa2"], B["aT"]], writes=[B["az_ps"]])
            fw.op("act", lambda e: e.activation(out=e1_sb[:], in_=az_ps[:], func=AF.Exp, bias=nb_sb[:], scale=-1.0),
                  reads=[B["az_ps"], B["nb"]], writes=[B["e1"]])
            fw.op("act", lambda e: e.activation(out=sp_sb[:], in_=e1_sb[:], func=AF.Ln, bias=1.0, scale=1.0),
                  reads=[B["e1"]], writes=[B["sp"]])
            fw.op("dve", lambda e: e.tensor_tensor_scan(out=cs_sb[:], data0=rm_sb[:], data1=sp_sb[:], initial=0.0,
                                                        op0=ALU.mult, op1=ALU.add),
                  reads=[B["rm"], B["sp"]], writes=[B["cs"]])
            fw.op("act", lambda e: e.activation(out=eq_sb[p][:], in_=cs_sb[:], func=AF.Exp, scale=-1.0 / 16.0),
                  reads=[B["cs"]], writes=[B["eq"][p]])
            fw.op("act", lambda e: e.activation(out=ek_sb[:], in_=cs_sb[:], func=AF.Exp, scale=1.0 / 16.0),
                  reads=[B["cs"]], writes=[B["ek"]])
            fw.op("dve", lambda e: e.scalar_tensor_tensor(out=Qt[p][:], in0=q_ps[:], scalar=QSCALE, in1=eq_sb[p][:],
                                                          op0=ALU.mult, op1=ALU.mult),
                  reads=[B["q_ps"], B["eq"][p]], writes=[B["Qt"][p]])
            fw.op("dve", lambda e: e.tensor_tensor(out=Kt[p][:], in0=k_ps[:], in1=ek_sb[:], op=ALU.mult),
                  reads=[B["k_ps"], B["ek"]], writes=[B["Kt"][p]])
            for c in range(4):
                cl = c * 128 + 127
                fw.op("dve", lambda e, c=c, cl=cl: e.scalar_tensor_tensor(
                    out=Kh[p][:, c * 128:(c + 1) * 128], in0=k_ps[:, c * 128:(c + 1) * 128],
                    scalar=eq_sb[p][:, cl:cl + 1], in1=ek_sb[:, c * 128:(c + 1) * 128],
                    op0=ALU.mult, op1=ALU.mult),
                    reads=[B["k_ps"], B["eq"][p], B["ek"]], writes=[B["Kh"][p]])
            for c in range(4):
                fw.op("pe", lambda e, c=c: e.transpose(tr_ps[:, c, 0:128], Kh[p][:, c * 128:(c + 1) * 128], id_sb[:]),
                      reads=[B["Kh"][p], B["id"]], writes=[B["tr_ps"]], inc=(c == 3))
            fw.op("act", lambda e: e.activation(out=Khtok[p][:], in_=tr_ps[:, :, 0:128], func=AF.Copy),
                  reads=[B["tr_ps"]], writes=[B["Khtok"][p]])
            for c in range(4):
                vp = vg_ps[c % 2]
                vb = B["vg_ps"][c % 2]
                for kc in range(KC):
                    fw.op("pe", lambda e, c=c, kc=kc, vp=vp: e.matmul(
                        vp[:], lhsT=xTb[p][:, kc, c * 128:(c + 1) * 128], rhs=w_sb[:, kc, 256:768],
                        start=(kc == 0), stop=(kc == KC - 1)),
                        reads=[B["w"], B["xTb"][p]], writes=[vb], inc=(kc == KC - 1))
                fw.op("act", lambda e, c=c, vp=vp: e.activation(out=v_sb[p][:, c, :], in_=vp[:, 0:256], func=AF.Copy),
                      reads=[vb], writes=[B["v"][p][c]])
                fw.op("act", lambda e, c=c, vp=vp: e.activation(out=gs_sb[p][:, c, :], in_=vp[:, 256:512], func=AF.Silu),
                      reads=[vb], writes=[B["gs"][p][c]])
                fw.op("pool", lambda e, c=c: e.tensor_tensor(out=gn_sb[p][:, c, :], in0=gs_sb[p][:, c, :], in1=ng_sb[:],
                                                             op=ALU.mult),
                      reads=[B["gs"][p][c], B["ng"]], writes=[B["gn"][p][c]])
            for c in range(4):
                sl = slice(c * 128, (c + 1) * 128)
                cl = c * 128 + 127
                a = c % 2
                fw.op("pe", lambda e, sl=sl: e.matmul(at_ps[:, 0:128], lhsT=Kt[p][:, sl], rhs=Qt[p][:, sl],
                                                      start=True, stop=True),
                      reads=[B["Kt"][p], B["Qt"][p]], writes=[B["at_ps"]])
                fw.op("dve", lambda e, a=a: e.tensor_tensor(out=att_sb[a][:], in0=at_ps[:, 0:128], in1=cm_sb[:], op=ALU.mult),
                      reads=[B["at_ps"], B["cm"]], writes=[B["att"][a]])
                fw.op("pe", lambda e, a=a, c=c: e.matmul(o_ps[:, 0:256], lhsT=att_sb[a][:], rhs=v_sb[p][:, c, :],
                                                         start=True, stop=False),
                      reads=[B["att"][a], B["v"][p][c]], writes=[B["o_ps"]], inc=False)
                fw.op("pe", lambda e, sl=sl: e.matmul(o_ps[:, 0:256], lhsT=Qt[p][:, sl], rhs=Sb_sb[:],
                                                      start=False, stop=True),
                      reads=[B["Qt"][p], B["Sb"]], writes=[B["o_ps"]])
                fw.op("pe", lambda e, c=c: e.matmul(kv_ps[:, 0:256], lhsT=Khtok[p][:, c, :], rhs=v_sb[p][:, c, :],
                                                    start=True, stop=True),
                      reads=[B["Khtok"][p], B["v"][p][c]], writes=[B["kv_ps"]])
                fw.op("dve", lambda e, cl=cl: e.scalar_tensor_tensor(out=S_sb[:], in0=S_sb[:], scalar=eq_sb[p][:, cl:cl + 1],
                                                                     in1=kv_ps[:, 0:256], op0=ALU.mult, op1=ALU.add),
                      reads=[B["S"], B["eq"][p], B["kv_ps"]], writes=[B["S"]])
                fw.op("act", lambda e: e.activation(out=Sb_sb[:], in_=S_sb[:], func=AF.Copy),
                      reads=[B["S"]], writes=[B["Sb"]])
                fw.op("act", lambda e: e.activation(out=sq_sb[:], in_=o_ps[:, 0:256], func=AF.Square),
                      reads=[B["o_ps"]], writes=[B["sq"]])
                fw.op("dve", lambda e: e.reduce_sum(out=ss_sb[:], in_=sq_sb[:], axis=AX.X),
                      reads=[B["sq"]], writes=[B["ss"]])
                fw.op("act", lambda e: e.activation(out=ss_sb[:], in_=ss_sb[:], func=AF.Ln, bias=RMS_EPS, scale=1.0 / 256.0),
                      reads=[B["ss"]], writes=[B["ss"]])
                fw.op("act", lambda e: e.activation(out=rs_sb[:], in_=ss_sb[:], func=AF.Exp, scale=-0.5),
                      reads=[B["ss"]], writes=[B["rs"]])
                fw.op("dve", lambda e, c=c: e.scalar_tensor_tensor(out=og_sb[p][:, c, :], in0=o_ps[:, 0:256], scalar=rs_sb[:, 0:1],
                                                                   in1=gn_sb[p][:, c, :], op0=ALU.mult, op1=ALU.mult),
                      reads=[B["o_ps"], B["rs"], B["gn"][p][c]], writes=[B["og"][p]])
            fw.dma("sp", og_v[t], og_sb[p][:], reads=[B["og"][p]], owner=B["og"][p], final=True)
        fw.finish()
    return nc


def _consts():
    j = np.arange(128)[:, None]
    i = np.arange(128)[None, :]
    cmask = (i >= j).astype(np.float32)
    ident = np.eye(128, dtype=np.float32)
    return cmask, ident


def run_l1(x, gla_w_in, gla_w_a2, gla_b_a2, gla_norm_g, S=SEQ, trace=False):
    nc = build_l1(S)
    cmask, ident = _consts()
    in_maps = []
    w = gla_w_in[0]
    for core in range(8):
        b, h = core // 4, core % 4
        wsel = np.concatenate([w[:, h * 128:(h + 1) * 128], w[:, 512 + h * 128:512 + (h + 1) * 128],
                               w[:, 1024 + h * 256:1024 + (h + 1) * 256], w[:, 2048 + h * 256:2048 + (h + 1) * 256],
                               w[:, 3072:3088]], axis=1)
        in_maps.append({
            "xT": np.ascontiguousarray(x[b, :S].T),
            "wsel": np.ascontiguousarray(wsel),
            "wa2": np.ascontiguousarray(gla_w_a2[0][:, h * 128:(h + 1) * 128]),
            "ba2": np.ascontiguousarray(gla_b_a2[0][h * 128:(h + 1) * 128].reshape(128, 1)),
            "ngb": np.ascontiguousarray(np.broadcast_to(gla_norm_g[0][h * 256:(h + 1) * 256][None, :], (128, 256))),
            "cmask": cmask, "ident": ident,
        })
    res = run_bass_kernel_spmd(nc, in_maps, core_ids=list(range(8)), trace=trace)
    og = np.zeros((BATCH, S, D), dtype=ml_dtypes.bfloat16)
    for core in range(8):
        b, h = core // 4, core % 4
        og[b, :, h * 256:(h + 1) * 256] = res.results[core]["og"]
    return og, res


CH = 2048


def build_pl(kind):
    TT = 512
    NT = CH // TT
    nc = bass.Bass("TRN2", target_bir_lowering=False)
    w = nc.dram_tensor("w", [D, D], F32, kind="ExternalInput").ap()
    lnp = nc.dram_tensor("lnp", [128, 2, KC], F32, kind="ExternalInput").ap()
    resT = nc.dram_tensor("resT", [D, CH], F32, kind="ExternalInput").ap()
    outT = nc.dram_tensor("outT", [D, CH], F32, kind="ExternalOutput").ap()
    if kind == "l2":
        aT = nc.dram_tensor("aT", [D, CH], BF16, kind="ExternalInput").ap()
    else:
        numT = nc.dram_tensor("numT", [3, D, CH], F32, kind="ExternalInput").ap()
        denT = nc.dram_tensor("denT", [3, D, CH], F32, kind="ExternalInput").ap()
        wg = nc.dram_tensor("wg", [D, D], F32, kind="ExternalInput").ap()
    es = contextlib.ExitStack()
    with es:
        fw = FW(nc, es)
        w_sb = _sb(nc, es, "w_sb", [128, KC, D], BF16)
        ln_sb = _sb(nc, es, "ln_sb", [128, 2, KC], F32)
        ones_sb = _sb(nc, es, "ones_sb", [128, 128], BF16)
        a_sb = [_sb(nc, es, "a_sb%d" % i, [128, KC, TT], BF16) for i in range(2)]
        res_sb = [_sb(nc, es, "res_sb%d" % i, [128, KC, TT], F32) for i in range(2)]
        r_sb = _sb(nc, es, "r_sb", [128, KC, TT], F32)
        rb_sb = _sb(nc, es, "rb_sb", [128, KC, TT], BF16)
        rq_sb = _sb(nc, es, "rq_sb", [128, KC, TT], BF16)
        o_sb = [_sb(nc, es, "o_sb%d" % i, [128, KC, TT], F32) for i in range(2)]
        mean_sb = _sb(nc, es, "mean_sb", [128, TT], F32)
        msq_sb = _sb(nc, es, "msq_sb", [128, TT], F32)
        var_sb = _sb(nc, es, "var_sb", [128, TT], F32)
        rstd_sb = _sb(nc, es, "rstd_sb", [128, TT], F32)
        nmr_sb = _sb(nc, es, "nmr_sb", [128, TT], F32)
        t_sb = [_sb(nc, es, "t_sb%d" % i, [128, TT], F32) for i in range(2)]
        y_ps = [_ps(nc, es, "y_ps%d" % i, [128, TT], F32) for i in range(2)]
        sum_ps = _ps(nc, es, "sum_ps", [128, TT], F32)
        sq_ps = _ps(nc, es, "sq_ps", [128, TT], F32)
        B = {}
        for n in ["w", "ln", "ones", "r", "rb", "rq", "mean", "msq", "var", "rstd", "nmr", "sum_ps", "sq_ps", "wg"]:
            B[n] = fw.buf(n)
        for n in ["a", "res", "o", "t", "y_ps", "xb", "g_ps"]:
            B[n] = fw.bufs(2, n)
        for n in ["rr", "rbb", "rqq"]:
            B[n] = fw.bufs(KC, n)
        if kind == "l4":
            wg_sb = _sb(nc, es, "wg_sb", [128, KC, D], BF16)
            xb_sb = [_sb(nc, es, "xb_sb%d" % i, [128, KC, TT], BF16) for i in range(2)]
            g_ps = [_ps(nc, es, "g_ps%d" % i, [128, TT], F32) for i in range(2)]
            n_sb = [[_sb(nc, es, "n_sb%d_%d" % (i, g), [128, TT], F32) for g in range(3)] for i in range(2)]
            d_sb = [[_sb(nc, es, "d_sb%d_%d" % (i, g), [128, TT], F32) for g in range(3)] for i in range(2)]
            sg_sb = [_sb(nc, es, "sg_sb%d" % i, [128, TT], F32) for i in range(2)]
            B["n"] = [fw.bufs(3, "n%d_" % i) for i in range(2)]
            B["d"] = [fw.bufs(3, "d%d_" % i) for i in range(2)]
            B["sg"] = fw.bufs(2, "sg")
            fw.dma("pool", wg_sb[:], wg.rearrange("(kc p) n -> p kc n", p=128), writes=[B["wg"]])
            num_v = numT.rearrange("g (h p) t -> g p h t", p=128)
            den_v = denT.rearrange("g (h p) t -> g p h t", p=128)
        fw.dma("pool", w_sb[:], w.rearrange("(kc p) n -> p kc n", p=128), writes=[B["w"]])
        fw.dma("sp", ln_sb[:], lnp[:, :, :], writes=[B["ln"]])
        fw.op("dve", lambda e: e.memset(ones_sb[:], 1.0), writes=[B["ones"]])
        res_v = resT.rearrange("(kc p) t -> p kc t", p=128)
        out_v = outT.rearrange("(kc p) t -> p kc t", p=128)
        if kind == "l2":
            a_v = aT.rearrange("(kc p) t -> p kc t", p=128)
        hcount = 0
        for t in range(NT):
            p = t % 2
            t0 = t * TT
            fw.dma("sp", res_sb[p][:], res_v[:, :, t0:t0 + TT], writes=[B["res"][p]])
            if kind == "l2":
                fw.dma("sp", a_sb[p][:], a_v[:, :, t0:t0 + TT], writes=[B["a"][p]])
            else:
                fw.dma("pool", xb_sb[p][:], res_v[:, :, t0:t0 + TT], writes=[B["xb"][p]])
                for h in range(8):
                    q = hcount % 2
                    hcount += 1
                    for g in range(3):
                        fw.dma("sp", n_sb[q][g][:], num_v[g, :, h, t0:t0 + TT], writes=[B["n"][q][g]])
                        fw.dma("sp", d_sb[q][g][:], den_v[g, :, h, t0:t0 + TT], writes=[B["d"][q][g]])
                    for kc in range(KC):
                        fw.op("pe", lambda e, kc=kc, h=h, q=q: e.matmul(
                            g_ps[q][:], lhsT=wg_sb[:, kc, h * 128:(h + 1) * 128], rhs=xb_sb[p][:, kc, :],
                            start=(kc == 0), stop=(kc == KC - 1)),
                            reads=[B["wg"], B["xb"][p]], writes=[B["g_ps"][q]], inc=(kc == KC - 1))
                    fw.op("act", lambda e, q=q: e.activation(out=sg_sb[q][:], in_=g_ps[q][:], func=AF.Silu),
                          reads=[B["g_ps"][q]], writes=[B["sg"][q]])
                    fw.op("pool", lambda e, q=q: e.tensor_tensor(out=n_sb[q][0][:], in0=n_sb[q][0][:], in1=n_sb[q][1][:], op=ALU.add),
                          reads=[B["n"][q][0], B["n"][q][1]], writes=[B["n"][q][0]])
                    fw.op("pool", lambda e, q=q: e.tensor_tensor(out=n_sb[q][0][:], in0=n_sb[q][0][:], in1=n_sb[q][2][:], op=ALU.add),
                          reads=[B["n"][q][0], B["n"][q][2]], writes=[B["n"][q][0]])
                    fw.op("pool", lambda e, q=q: e.tensor_tensor(out=d_sb[q][0][:], in0=d_sb[q][0][:], in1=d_sb[q][1][:], op=ALU.add),
                          reads=[B["d"][q][0], B["d"][q][1]], writes=[B["d"][q][0]])
                    fw.op("pool", lambda e, q=q: e.tensor_tensor(out=d_sb[q][0][:], in0=d_sb[q][0][:], in1=d_sb[q][2][:], op=ALU.add),
                          reads=[B["d"][q][0], B["d"][q][2]], writes=[B["d"][q][0]])
                    fw.op("dve", lambda e, q=q: e.reciprocal(out=d_sb[q][1][:], in_=d_sb[q][0][:]),
                          reads=[B["d"][q][0]], writes=[B["d"][q][1]])
                    fw.op("dve", lambda e, q=q: e.tensor_tensor(out=n_sb[q][1][:], in0=n_sb[q][0][:], in1=d_sb[q][1][:], op=ALU.mult),
                          reads=[B["n"][q][0], B["d"][q][1]], writes=[B["n"][q][1]])
                    fw.op("dve", lambda e, q=q, h=h: e.tensor_tensor(out=a_sb[p][:, h, :], in0=n_sb[q][1][:], in1=sg_sb[q][:], op=ALU.mult),
                          reads=[B["n"][q][1], B["sg"][q]], writes=[B["a"][p]])
            for fc in range(KC):
                yq = fc % 2
                for kc in range(KC):
                    fw.op("pe", lambda e, kc=kc, fc=fc, yq=yq: e.matmul(
                        y_ps[yq][:], lhsT=w_sb[:, kc, fc * 128:(fc + 1) * 128], rhs=a_sb[p][:, kc, :],
                        start=(kc == 0), stop=(kc == KC - 1)),
                        reads=[B["w"], B["a"][p]], writes=[B["y_ps"][yq]], inc=(kc == KC - 1))
                fw.op("dve", lambda e, fc=fc, yq=yq: e.scalar_tensor_tensor(
                    out=r_sb[:, fc, :], in0=res_sb[p][:, fc, :], scalar=float(ALPHA), in1=y_ps[yq][:], op0=ALU.mult, op1=ALU.add),
                    reads=[B["res"][p], B["y_ps"][yq]], writes=[B["rr"][fc]])
                fw.op("act", lambda e, fc=fc: e.activation(out=rb_sb[:, fc, :], in_=r_sb[:, fc, :], func=AF.Copy),
                      reads=[B["rr"][fc]], writes=[B["rbb"][fc]])
                fw.op("act", lambda e, fc=fc: e.activation(out=rq_sb[:, fc, :], in_=r_sb[:, fc, :], func=AF.Square),
                      reads=[B["rr"][fc]], writes=[B["rqq"][fc]])
            for fc in range(KC):
                fw.op("pe", lambda e, fc=fc: e.matmul(sum_ps[:], lhsT=ones_sb[:], rhs=rb_sb[:, fc, :],
                                                      start=(fc == 0), stop=(fc == KC - 1)),
                      reads=[B["ones"], B["rbb"][fc]], writes=[B["sum_ps"]], inc=(fc == KC - 1))
            for fc in range(KC):
                fw.op("pe", lambda e, fc=fc: e.matmul(sq_ps[:], lhsT=ones_sb[:], rhs=rq_sb[:, fc, :],
                                                      start=(fc == 0), stop=(fc == KC - 1)),
                      reads=[B["ones"], B["rqq"][fc]], writes=[B["sq_ps"]], inc=(fc == KC - 1))
            fw.op("dve", lambda e: e.tensor_scalar(out=mean_sb[:], in0=sum_ps[:], scalar1=1.0 / D, scalar2=None, op0=ALU.mult),
                  reads=[B["sum_ps"]], writes=[B["mean"]])
            fw.op("dve", lambda e: e.tensor_tensor(out=msq_sb[:], in0=mean_sb[:], in1=mean_sb[:], op=ALU.mult),
                  reads=[B["mean"]], writes=[B["msq"]])
            fw.op("dve", lambda e: e.scalar_tensor_tensor(out=var_sb[:], in0=sq_ps[:], scalar=1.0 / D, in1=msq_sb[:],
                                                          op0=ALU.mult, op1=ALU.subtract),
                  reads=[B["sq_ps"], B["msq"]], writes=[B["var"]])
            fw.op("act", lambda e: e.activation(out=var_sb[:], in_=var_sb[:], func=AF.Ln, bias=LN_EPS, scale=1.0),
                  reads=[B["var"]], writes=[B["var"]])
            fw.op("act", lambda e: e.activation(out=rstd_sb[:], in_=var_sb[:], func=AF.Exp, scale=-0.5),
                  reads=[B["var"]], writes=[B["rstd"]])
            fw.op("dve", lambda e: e.tensor_tensor(out=nmr_sb[:], in0=mean_sb[:], in1=rstd_sb[:], op=ALU.mult),
                  reads=[B["mean"], B["rstd"]], writes=[B["nmr"]])
            for fc in range(KC):
                tq = fc % 2
                fw.op("dve", lambda e, fc=fc, tq=tq: e.tensor_tensor(out=t_sb[tq][:], in0=r_sb[:, fc, :], in1=rstd_sb[:], op=ALU.mult),
                      reads=[B["rr"][fc], B["rstd"]], writes=[B["t"][tq]])
                fw.op("dve", lambda e, tq=tq: e.tensor_tensor(out=t_sb[tq][:], in0=t_sb[tq][:], in1=nmr_sb[:], op=ALU.subtract),
                      reads=[B["t"][tq], B["nmr"]], writes=[B["t"][tq]])
                fw.op("act", lambda e, fc=fc, tq=tq: e.activation(out=o_sb[p][:, fc, :], in_=t_sb[tq][:], func=AF.Identity,
                                                                  bias=ln_sb[:, 1, fc:fc + 1], scale=ln_sb[:, 0, fc:fc + 1]),
                      reads=[B["t"][tq], B["ln"]], writes=[B["o"][p]])
            fw.dma("sp", out_v[:, :, t0:t0 + TT], o_sb[p][:], reads=[B["o"][p]], owner=B["o"][p], final=True)
        fw.finish()
    return nc


_VERBOSE = False


def _run(nc, in_maps):
    if _VERBOSE:
        import time
        t0 = time.time()
        print("launch: input MB", sum(v.nbytes for m in in_maps for v in m.values()) / 1e6, flush=True)
    r = run_bass_kernel_spmd(nc, in_maps, core_ids=list(range(8))).results
    if _VERBOSE:
        print("launch done", time.time() - t0, flush=True)
    return r


def _ln_layout(g, b):
    return np.ascontiguousarray(np.stack([g.reshape(KC, 128).T, b.reshape(KC, 128).T], axis=1)).astype(np.float32)


SWA_GROUPS = ((128, 1), (512, 4), (2048, 16))
QS = 128.0 ** -0.5


def build_l3():
    nc = bass.Bass("TRN2", target_bir_lowering=False)
    G = []
    for g, (_, d) in enumerate(SWA_GROUPS):
        NP = (d + 16) * 128
        G.append(dict(
            d=d, NP=NP, HL=d * 128,
            x=nc.dram_tensor("xg%d" % g, [D, NP], F32, kind="ExternalInput").ap(),
            wq=nc.dram_tensor("wq%d" % g, [D, 1024], F32, kind="ExternalInput").ap(),
            wkv=nc.dram_tensor("wkv%d" % g, [D, 512], F32, kind="ExternalInput").ap(),
            cos=nc.dram_tensor("cos%d" % g, [128, NP], F32, kind="ExternalInput").ap(),
            sin=nc.dram_tensor("sin%d" % g, [128, NP], F32, kind="ExternalInput").ap(),
            num=nc.dram_tensor("numT%d" % g, [D, CH], F32, kind="ExternalOutput").ap(),
            den=nc.dram_tensor("denT%d" % g, [D, CH], F32, kind="ExternalOutput").ap(),
        ))
    mcur = nc.dram_tensor("mcur", [128, 512], F32, kind="ExternalInput").ap()
    mprev = nc.dram_tensor("mprev", [128, 512], F32, kind="ExternalInput").ap()
    mprevh = nc.dram_tensor("mprevh", [128, 512], F32, kind="ExternalInput").ap()
    es = contextlib.ExitStack()
    with es:
        fw = FW(nc, es)
        wq_sb = _sb(nc, es, "wq_sb", [128, KC, 1024], BF16)
        wkv_sb = _sb(nc, es, "wkv_sb", [128, KC, 512], BF16)
        xt = [_sb(nc, es, "xt%d" % i, [128, KC, 512], BF16) for i in range(2)]
        cs_sb = [_sb(nc, es, "cs_sb%d" % i, [128, 512], F32) for i in range(2)]
        sn_sb = [_sb(nc, es, "sn_sb%d" % i, [128, 512], F32) for i in range(2)]
        Kt_sb = _sb(nc, es, "Kt_sb", [128, 2, 4096], BF16)
        V_sb = _sb(nc, es, "V_sb", [128, 32, 256], BF16)
        Qt_sb = _sb(nc, es, "Qt_sb", [128, 8, CH], BF16)
        t1_sb = [_sb(nc, es, "t1_sb%d" % i, [128, 512], F32) for i in range(2)]
        t2_sb = [_sb(nc, es, "t2_sb%d" % i, [128, 512], F32) for i in range(2)]
        pe_sb = [[_sb(nc, es, "pe_sb%d_%d" % (k, i), [128, 4, 128], BF16) for i in range(2)] for k in range(2)]
        pm_sb = [[_sb(nc, es, "pm_sb%d_%d" % (k, i), [128, 4, 128], BF16) for i in range(2)] for k in range(2)]
        num_sb = [_sb(nc, es, "num_sb%d" % i, [128, 8, 128], F32) for i in range(2)]
        den_sb = [_sb(nc, es, "den_sb%d" % i, [128, 8, 128], F32) for i in range(2)]
        m_sb = [_sb(nc, es, "m_sb%d" % i, [128, 4, 128], BF16) for i in range(3)]
        ones_sb = _sb(nc, es, "ones_sb", [128, 128], BF16)
        pj_ps = [_ps(nc, es, "pj_ps%d" % i, [128, 512], F32) for i in range(2)]
        v_ps = _ps(nc, es, "v_ps", [128, 512], F32)
        s_ps = [_ps(nc, es, "s_ps%d" % i, [128, 4, 128], F32) for i in range(2)]
        num_ps = _ps(nc, es, "num_ps", [128, 4, 128], F32)
        den_ps = _ps(nc, es, "den_ps", [128, 4, 128], F32)
        B = {}
        for n in ["wq", "wkv", "Kt", "V", "Qt", "ones", "v_ps", "num_ps", "den_ps"]:
            B[n] = fw.buf(n)
        for n in ["xt", "cs", "sn", "t1", "t2", "pj_ps", "s_ps", "num", "den"]:
            B[n] = fw.bufs(2, n)
        B["m"] = fw.bufs(3, "m")
        B["pe"] = [fw.bufs(2, "pe%d_" % k) for k in range(2)]
        B["pm"] = [fw.bufs(2, "pm%d_" % k) for k in range(2)]
        for i, mm in enumerate((mcur, mprev, mprevh)):
            fw.dma("pool", m_sb[i][:], mm.rearrange("p (h q) -> p h q", h=4), writes=[B["m"][i]])
        fw.op("dve", lambda e: e.memset(ones_sb[:], 1.0), writes=[B["ones"]])
        pjc = [0]

        def rope(ps, wd, xp, out_ap, out_buf):
            a = pjc[0] % 2
            fw.op("dve", lambda e: e.tensor_tensor(out=t1_sb[a][:, 0:wd], in0=ps[:, 0:wd], in1=cs_sb[xp][:, 0:wd], op=ALU.mult),
                  reads=[B["pj_ps"][a], B["cs"][xp]], writes=[B["t1"][a]])
            fw.op("dve", lambda e: e.tensor_tensor(out=t2_sb[a][0:64, 0:wd], in0=ps[64:128, 0:wd], in1=sn_sb[xp][64:128, 0:wd], op=ALU.mult),
                  reads=[B["pj_ps"][a], B["sn"][xp]], writes=[B["t2"][a]])
            fw.op("dve", lambda e: e.tensor_tensor(out=t2_sb[a][64:128, 0:wd], in0=ps[0:64, 0:wd], in1=sn_sb[xp][0:64, 0:wd], op=ALU.mult),
                  reads=[B["pj_ps"][a], B["sn"][xp]], writes=[B["t2"][a]])
            fw.op("pool", lambda e: e.tensor_tensor(out=out_ap, in0=t1_sb[a][:, 0:wd], in1=t2_sb[a][:, 0:wd], op=ALU.add),
                  reads=[B["t1"][a], B["t2"][a]], writes=[out_buf])

        def proj(w_sb_, wb, c0, xp, wd):
            a = pjc[0] % 2
            for kc in range(KC):
                fw.op("pe", lambda e, kc=kc: e.matmul(pj_ps[a][:, 0:wd], lhsT=w_sb_[:, kc, c0:c0 + 128], rhs=xt[xp][:, kc, 0:wd],
                                                      start=(kc == 0), stop=(kc == KC - 1)),
                      reads=[wb, B["xt"][xp]], writes=[B["pj_ps"][a]], inc=(kc == KC - 1))
            return pj_ps[a]

        tcount = 0
        jcount = 0
        acount = 0
        for g, gi in enumerate(G):
            d, NP, HL = gi["d"], gi["NP"], gi["HL"]
            fw.dma("pool", wq_sb[:], gi["wq"].rearrange("(kc p) n -> p kc n", p=128), writes=[B["wq"]])
            fw.dma("pool", wkv_sb[:], gi["wkv"].rearrange("(kc p) n -> p kc n", p=128), writes=[B["wkv"]])
            x_v = gi["x"].rearrange("(kc p) n -> p kc n", p=128)
            if d == 1:
                tiles = [(0, 128)]
            else:
                tiles = [(i * 512, 512) for i in range(HL // 512)]
            tiles += [(HL + i * 512, 512) for i in range(4)]
            for (p0, wd) in tiles:
                xp = tcount % 2
                tcount += 1
                fw.dma("pool", xt[xp][:, :, 0:wd], x_v[:, :, p0:p0 + wd], writes=[B["xt"][xp]])
                fw.dma("sp", cs_sb[xp][:, 0:wd], gi["cos"][:, p0:p0 + wd], writes=[B["cs"][xp]])
                fw.dma("sp", sn_sb[xp][:, 0:wd], gi["sin"][:, p0:p0 + wd], writes=[B["sn"][xp]])
                for hk in range(2):
                    ps = proj(wkv_sb, B["wkv"], hk * 128, xp, wd)
                    rope(ps, wd, xp, Kt_sb[:, hk, p0:p0 + wd], B["Kt"])
                    pjc[0] += 1
                for blk in range(wd // 128):
                    for kc in range(KC):
                        fw.op("pe", lambda e, kc=kc, blk=blk: e.matmul(
                            v_ps[:, 0:256], lhsT=xt[xp][:, kc, blk * 128:(blk + 1) * 128], rhs=wkv_sb[:, kc, 256:512],
                            start=(kc == 0), stop=(kc == KC - 1)),
                            reads=[B["wkv"], B["xt"][xp]], writes=[B["v_ps"]], inc=(kc == KC - 1))
                    fw.op("act", lambda e, blk=blk: e.activation(out=V_sb[:, p0 // 128 + blk, :], in_=v_ps[:, 0:256], func=AF.Copy),
                          reads=[B["v_ps"]], writes=[B["V"]])
                if p0 >= HL:
                    q0 = p0 - HL
                    for h in range(8):
                        ps = proj(wq_sb, B["wq"], h * 128, xp, wd)
                        rope(ps, wd, xp, Qt_sb[:, h, q0:q0 + wd], B["Qt"])
                        pjc[0] += 1
            nbr = 16 // d
            num_v = gi["num"].rearrange("(h p) n -> p h n", p=128)
            den_v = gi["den"].rearrange("(h p) n -> p h n", p=128)
            for j in range(16):
                jb = jcount % 2
                jcount += 1
                if j % nbr == 0:
                    prevpos, pm_i = (j // nbr) * 128, 2
                else:
                    prevpos, pm_i = HL + (j - 1) * 128, 1
                curpos = HL + j * 128
                for hk in range(2):
                    a = acount % 2
                    acount += 1
                    for kb, (kpos, mi) in enumerate(((prevpos, pm_i), (curpos, 0))):
                        fw.op("pe", lambda e, kb=kb, kpos=kpos: e.matmul(
                            s_ps[kb][:], lhsT=Kt_sb[:, hk, kpos:kpos + 128], rhs=Qt_sb[:, 4 * hk:4 * hk + 4, j * 128:(j + 1) * 128],
                            start=True, stop=True),
                            reads=[B["Kt"], B["Qt"]], writes=[B["s_ps"][kb]])
                        fw.op("act", lambda e, kb=kb: e.activation(out=pe_sb[kb][a][:], in_=s_ps[kb][:], func=AF.Exp, scale=QS),
                              reads=[B["s_ps"][kb]], writes=[B["pe"][kb][a]])
                        fw.op("pool" if kb == 0 else "dve", lambda e, kb=kb, mi=mi: e.tensor_tensor(
                            out=pm_sb[kb][a][:], in0=pe_sb[kb][a][:], in1=m_sb[mi][:], op=ALU.mult),
                            reads=[B["pe"][kb][a], B["m"][mi]], writes=[B["pm"][kb][a]])
                    for kb, kpos in enumerate((prevpos, curpos)):
                        fw.op("pe", lambda e, kb=kb, kpos=kpos: e.matmul(
                            num_ps[:], lhsT=V_sb[:, kpos // 128, hk * 128:(hk + 1) * 128], rhs=pm_sb[kb][a][:],
                            start=(kb == 0), stop=(kb == 1)),
                            reads=[B["V"], B["pm"][kb][a]], writes=[B["num_ps"]], inc=(kb == 1))
                    for kb in range(2):
                        fw.op("pe", lambda e, kb=kb: e.matmul(den_ps[:], lhsT=ones_sb[:], rhs=pm_sb[kb][a][:],
                                                              start=(kb == 0), stop=(kb == 1)),
                              reads=[B["ones"], B["pm"][kb][a]], writes=[B["den_ps"]], inc=(kb == 1))
                    fw.op("act", lambda e: e.activation(out=num_sb[jb][:, 4 * hk:4 * hk + 4, :], in_=num_ps[:], func=AF.Copy),
                          reads=[B["num_ps"]], writes=[B["num"][jb]])
                    fw.op("dve", lambda e: e.tensor_copy(out=den_sb[jb][:, 4 * hk:4 * hk + 4, :], in_=den_ps[:]),
                          reads=[B["den_ps"]], writes=[B["den"][jb]])
                fw.dma("sp", num_v[:, :, j * 128:(j + 1) * 128], num_sb[jb][:], reads=[B["num"][jb]], owner=B["num"][jb], final=True)
                fw.dma("sp", den_v[:, :, j * 128:(j + 1) * 128], den_sb[jb][:], reads=[B["den"][jb]], owner=B["den"][jb], final=True)
        fw.finish()
    return nc


def _l3_perm(c):
    out = []
    for (_, d) in SWA_GROUPS:
        n = CH // d
        idx = []
        for r in range(d):
            i = np.arange(n - 128, n)
            t = (c - 1) * CH + r + d * i
            idx.append(t if c > 0 else np.full(128, -1))
        for r in range(d):
            i = np.arange(n)
            idx.append(c * CH + r + d * i)
        out.append(np.concatenate(idx))
    return out


def _rope_tables(pos):
    half = 64
    inv = (10000.0 ** (-(np.arange(half, dtype=np.float32) * 2.0) / 128.0)).astype(np.float32)
    ang = pos.astype(np.float32)[None, :] * inv[:, None]
    cos = np.cos(ang).astype(np.float32)
    sin = np.sin(ang).astype(np.float32)
    cosT = np.concatenate([cos, cos], axis=0)
    sinT = np.concatenate([sin, -sin], axis=0)
    return np.ascontiguousarray(cosT), np.ascontiguousarray(sinT)


def _l3_masks(c):
    k = np.arange(128)[:, None]
    q = np.arange(128)[None, :]
    mcur = np.tile((k <= q).astype(np.float32), (1, 4))
    mprev = np.tile((k >= q).astype(np.float32), (1, 4))
    mprevh = mprev if c > 0 else np.zeros_like(mprev)
    return mcur, mprev, np.ascontiguousarray(mprevh)


def _l3_inputs(x1T_b, c, swa_w_in, w_kv):
    im = {}
    perms = _l3_perm(c)
    for g, idx in enumerate(perms):
        xg = x1T_b[:, np.maximum(idx, 0)]
        xg[:, idx < 0] = 0.0
        im["xg%d" % g] = np.ascontiguousarray(xg)
        im["wq%d" % g] = np.ascontiguousarray(swa_w_in[:, g * 1024:(g + 1) * 1024])
        im["wkv%d" % g] = np.ascontiguousarray(np.concatenate(
            [w_kv[:, g * 256:(g + 1) * 256], w_kv[:, 768 + g * 256:768 + (g + 1) * 256]], axis=1))
        cosT, sinT = _rope_tables(idx)
        im["cos%d" % g] = cosT
        im["sin%d" % g] = sinT
    im["mcur"], im["mprev"], im["mprevh"] = _l3_masks(c)
    return im, perms


_NC_CACHE = {}


def _get_nc(name):
    if name not in _NC_CACHE:
        if name == "l1":
            _NC_CACHE[name] = build_l1(SEQ)
        elif name == "l3":
            _NC_CACHE[name] = build_l3()
        else:
            _NC_CACHE[name] = build_pl(name)
    return _NC_CACHE[name]


def kernel_unfused(x, gla_w_in, gla_w_a2, gla_b_a2, gla_norm_g, gla_w_out, w_kv, swa_w_in, swa_w_out, ln_g, ln_b):
    f = lambda a: np.asarray(a, dtype=np.float32)
    x, gla_w_in, gla_w_a2, gla_b_a2, gla_norm_g, gla_w_out = map(f, (x, gla_w_in, gla_w_a2, gla_b_a2, gla_norm_g, gla_w_out))
    w_kv, swa_w_in, swa_w_out, ln_g, ln_b = map(f, (w_kv, swa_w_in, swa_w_out, ln_g, ln_b))
    cmask, ident = _consts()
    w = gla_w_in[0]
    xT = [np.ascontiguousarray(x[b].T) for b in range(BATCH)]
    in_maps = []
    for core in range(8):
        b, h = core // 4, core % 4
        wsel = np.concatenate([w[:, h * 128:(h + 1) * 128], w[:, 512 + h * 128:512 + (h + 1) * 128],
                               w[:, 1024 + h * 256:1024 + (h + 1) * 256], w[:, 2048 + h * 256:2048 + (h + 1) * 256],
                               w[:, 3072:3088]], axis=1)
        in_maps.append({
            "xT": xT[b], "wsel": np.ascontiguousarray(wsel),
            "wa2": np.ascontiguousarray(gla_w_a2[0][:, h * 128:(h + 1) * 128]),
            "ba2": np.ascontiguousarray(gla_b_a2[0][h * 128:(h + 1) * 128].reshape(128, 1)),
            "ngb": np.ascontiguousarray(np.broadcast_to(gla_norm_g[0][h * 256:(h + 1) * 256][None, :], (128, 256))),
            "cmask": cmask, "ident": ident,
        })
    r1 = _run(_get_nc("l1"), in_maps)
    ogT = np.zeros((BATCH, D, SEQ), dtype=ml_dtypes.bfloat16)
    for core in range(8):
        b, h = core // 4, core % 4
        ogT[b, h * 256:(h + 1) * 256, :] = r1[core]["og"].T
    lnp0 = _ln_layout(ln_g[0], ln_b[0])
    lnp1 = _ln_layout(ln_g[1], ln_b[1])
    in_maps = []
    for core in range(8):
        b, c = core // 4, core % 4
        sl = slice(c * CH, (c + 1) * CH)
        in_maps.append({"w": gla_w_out[0], "lnp": lnp0, "resT": np.ascontiguousarray(xT[b][:, sl]),
                        "aT": np.ascontiguousarray(ogT[b][:, sl])})
    r2 = _run(_get_nc("l2"), in_maps)
    x1T = np.zeros((BATCH, D, SEQ), dtype=np.float32)
    for core in range(8):
        b, c = core // 4, core % 4
        x1T[b][:, c * CH:(c + 1) * CH] = r2[core]["outT"]
    in_maps, perms_all = [], []
    for core in range(8):
        b, c = core // 4, core % 4
        im, perms = _l3_inputs(x1T[b], c, swa_w_in[0], w_kv)
        in_maps.append(im)
        perms_all.append(perms)
    r3 = _run(_get_nc("l3"), in_maps)
    in_maps = []
    wg = np.ascontiguousarray(swa_w_in[0][:, 3072:4096])
    for core in range(8):
        b, c = core // 4, core % 4
        numT = np.zeros((3, D, CH), dtype=np.float32)
        denT = np.zeros((3, D, CH), dtype=np.float32)
        for g, (_, d) in enumerate(SWA_GROUPS):
            own = perms_all[core][g][d * 128:] - c * CH
            numT[g][:, own] = r3[core]["numT%d" % g]
            denT[g][:, own] = r3[core]["denT%d" % g]
        in_maps.append({"w": swa_w_out[0], "lnp": lnp1, "resT": np.ascontiguousarray(x1T[b][:, c * CH:(c + 1) * CH]),
                        "numT": numT, "denT": denT, "wg": wg})
    r4 = _run(_get_nc("l4"), in_maps)
    out = np.zeros((BATCH, SEQ, D), dtype=np.float32)
    for core in range(8):
        b, c = core // 4, core % 4
        out[b, c * CH:(c + 1) * CH, :] = r4[core]["outT"].T
    return out


def _pl_phase(nc, fw, w_dram, ln_sb, lidx, a_v, res_v, res_off, ntiles, first_own, x1b, out_v, out_buf_final, tag):
    TT = 512
    pes = contextlib.ExitStack()
    with pes:
        w_sb = _sb(nc, pes, tag + "w_sb", [128, KC, D], BF16)
        ones_sb = _sb(nc, pes, tag + "ones_sb", [128, 128], BF16)
        a_sb = [_sb(nc, pes, tag + "a_sb%d" % i, [128, KC, TT], BF16) for i in range(2)]
        res_sb = [_sb(nc, pes, tag + "res_sb%d" % i, [128, KC, TT], F32) for i in range(2)]
        rb_sbs = [_sb(nc, pes, tag + "rb_sb%d" % i, [128, KC, TT], BF16) for i in range(2)]
        rq_sbs = [_sb(nc, pes, tag + "rq_sb%d" % i, [128, KC, TT], BF16) for i in range(2)]
        o_sb = [_sb(nc, pes, tag + "o_sb%d" % i, [128, KC, TT], F32) for i in range(2)]
        mean_sb = _sb(nc, pes, tag + "mean_sb", [128, TT], F32)
        msq_sb = _sb(nc, pes, tag + "msq_sb", [128, TT], F32)
        var_sb = _sb(nc, pes, tag + "var_sb", [128, TT], F32)
        rstd_sb = _sb(nc, pes, tag + "rstd_sb", [128, TT], F32)
        nmr_sb = _sb(nc, pes, tag + "nmr_sb", [128, TT], F32)
        t_sb = [_sb(nc, pes, tag + "t_sb%d" % i, [128, TT], F32) for i in range(2)]
        y_ps = [_ps(nc, pes, tag + "y_ps%d" % i, [128, TT], F32) for i in range(2)]
        sum_ps = _ps(nc, pes, tag + "sum_ps", [128, TT], F32)
        sq_ps = _ps(nc, pes, tag + "sq_ps", [128, TT], F32)
        B = {}
        for n in ["w", "ones", "mean", "msq", "var", "rstd", "nmr", "sum_ps", "sq_ps", "x1b"]:
            B[n] = fw.buf(tag + n)
        for n in ["a", "res", "t", "y_ps"]:
            B[n] = fw.bufs(2, tag + n)
        B["o"] = fw.bufs(2, tag + "o")
        RR = [fw.bufs(KC, tag + "rr%d_" % i) for i in range(2)]
        RB = [fw.bufs(KC, tag + "rbb%d_" % i) for i in range(2)]
        RQ = [fw.bufs(KC, tag + "rqq%d_" % i) for i in range(2)]
        w_v = w_dram.rearrange("(kc p) n -> p kc n", p=128)
        WB = fw.bufs(KC, tag + "wblk")
        for fc in range(KC):
            fw.dma("pool", w_sb[:, :, fc * 128:(fc + 1) * 128], w_v[:, :, fc * 128:(fc + 1) * 128], writes=[WB[fc]])
        fw.op("dve", lambda e: e.memset(ones_sb[:], 1.0), writes=[B["ones"]])
        def load_a(t):
            p = t % 2
            t0 = t * TT
            fw.dma("sp", a_sb[p][:], a_v[:, :, t0:t0 + TT], writes=[B["a"][p]])

        RES = [fw.bufs(KC, tag + "res%d_" % i) for i in range(2)]

        def load_res_fc(t, fc):
            p = t % 2
            t0 = t * TT
            fw.dma("sp", res_sb[p][:, fc, :], res_v[:, fc, res_off + t0:res_off + t0 + TT], writes=[RES[p][fc], RR[p][fc]])

        def load_res(t):
            for fc in range(KC):
                load_res_fc(t, fc)

        def normalize_fc(t, fc):
            p = t % 2
            t0 = t * TT
            own = t >= first_own
            r_sb = res_sb[p]
            tq = fc % 2
            last = t == ntiles - 1
            fw.op("dve" if (fc % 4 or last) else "pool", lambda e: e.tensor_tensor(out=t_sb[tq][:], in0=r_sb[:, fc, :], in1=rstd_sb[:], op=ALU.mult),
                  reads=[RR[p][fc], B["rstd"]], writes=[B["t"][tq]])
            fw.op("dve", lambda e: e.tensor_tensor(out=t_sb[tq][:], in0=t_sb[tq][:], in1=nmr_sb[:], op=ALU.subtract),
                  reads=[B["t"][tq], B["nmr"]], writes=[B["t"][tq]])
            if own:
                fw.op("act", lambda e: e.activation(out=o_sb[p][:, fc, :], in_=t_sb[tq][:], func=AF.Identity,
                                                    bias=ln_sb[:, 2 * lidx + 1, fc:fc + 1], scale=ln_sb[:, 2 * lidx, fc:fc + 1]),
                      reads=[B["t"][tq]], writes=[B["o"][p]])
                if x1b is not None and last:
                    fw.op("act", lambda e: e.activation(out=x1b[:, fc, t0:t0 + TT], in_=t_sb[tq][:], func=AF.Identity,
                                                        bias=ln_sb[:, 2 * lidx + 1, fc:fc + 1], scale=ln_sb[:, 2 * lidx, fc:fc + 1]),
                          reads=[B["t"][tq]], writes=[B["x1b"]])
                elif x1b is not None:
                    fw.op("pool", lambda e: e.tensor_copy(out=x1b[:, fc, t0:t0 + TT], in_=o_sb[p][:, fc, :]),
                          reads=[B["o"][p]], writes=[B["x1b"]])
            elif x1b is not None:
                fw.op("act", lambda e: e.activation(out=x1b[:, fc, t0:t0 + TT], in_=t_sb[tq][:], func=AF.Identity,
                                                    bias=ln_sb[:, 2 * lidx + 1, fc:fc + 1], scale=ln_sb[:, 2 * lidx, fc:fc + 1]),
                      reads=[B["t"][tq]], writes=[B["x1b"]])
            if own and fc == KC - 1:
                c0 = (t - first_own) * TT
                wr = [out_buf_final] if out_buf_final is not None else []
                fw.dma("sp", out_v[:, :, c0:c0 + TT], o_sb[p][:], reads=[B["o"][p]], writes=wr, owner=B["o"][p],
                       final=(out_buf_final is None))

        load_a(0)
        load_res(0)
        for t in range(ntiles + 1):
            if t < ntiles:
                p = t % 2
                r_sb, rb_sb, rq_sb = res_sb[p], rb_sbs[p], rq_sbs[p]
                if t + 1 < ntiles:
                    load_a(t + 1)
            for fc in range(KC):
                if t < ntiles:
                    yq = fc % 2
                    for kc in range(KC):
                        fw.op("pe", lambda e, kc=kc, fc=fc, yq=yq: e.matmul(
                            y_ps[yq][:], lhsT=w_sb[:, kc, fc * 128:(fc + 1) * 128], rhs=a_sb[p][:, kc, :],
                            start=(kc == 0), stop=(kc == KC - 1)),
                            reads=[WB[fc], B["a"][p]], writes=[B["y_ps"][yq]], inc=(kc == KC - 1))
                    fw.op("dve", lambda e, fc=fc, yq=yq: e.scalar_tensor_tensor(
                        out=r_sb[:, fc, :], in0=res_sb[p][:, fc, :], scalar=float(ALPHA), in1=y_ps[yq][:], op0=ALU.mult, op1=ALU.add),
                        reads=[RES[p][fc], B["y_ps"][yq]], writes=[RR[p][fc]])
                    fw.op("act", lambda e, fc=fc: e.activation(out=rb_sb[:, fc, :], in_=r_sb[:, fc, :], func=AF.Copy),
                          reads=[RR[p][fc]], writes=[RB[p][fc]])
                    fw.op("act", lambda e, fc=fc: e.activation(out=rq_sb[:, fc, :], in_=r_sb[:, fc, :], func=AF.Square),
                          reads=[RR[p][fc]], writes=[RQ[p][fc]])
                if t >= 1:
                    normalize_fc(t - 1, fc)
                if t + 1 < ntiles and t >= 1:
                    load_res_fc(t + 1, fc)
            if t == 0 and ntiles > 1:
                load_res(1)
            if t < ntiles:
                for fc in range(KC):
                    fw.op("pe", lambda e, fc=fc: e.matmul(sum_ps[:], lhsT=ones_sb[:], rhs=rb_sb[:, fc, :],
                                                          start=(fc == 0), stop=(fc == KC - 1)),
                          reads=[B["ones"], RB[p][fc]], writes=[B["sum_ps"]], inc=(fc == KC - 1))
                for fc in range(KC):
                    fw.op("pe", lambda e, fc=fc: e.matmul(sq_ps[:], lhsT=ones_sb[:], rhs=rq_sb[:, fc, :],
                                                          start=(fc == 0), stop=(fc == KC - 1)),
                          reads=[B["ones"], RQ[p][fc]], writes=[B["sq_ps"]], inc=(fc == KC - 1))
                fw.op("dve", lambda e: e.tensor_scalar(out=mean_sb[:], in0=sum_ps[:], scalar1=1.0 / D, scalar2=None, op0=ALU.mult),
                      reads=[B["sum_ps"]], writes=[B["mean"]])
                fw.op("dve", lambda e: e.tensor_tensor(out=msq_sb[:], in0=mean_sb[:], in1=mean_sb[:], op=ALU.mult),
                      reads=[B["mean"]], writes=[B["msq"]])
                fw.op("dve", lambda e: e.scalar_tensor_tensor(out=var_sb[:], in0=sq_ps[:], scalar=1.0 / D, in1=msq_sb[:],
                                                              op0=ALU.mult, op1=ALU.subtract),
                      reads=[B["sq_ps"], B["msq"]], writes=[B["var"]])
                fw.op("act", lambda e: e.activation(out=var_sb[:], in_=var_sb[:], func=AF.Ln, bias=LN_EPS, scale=1.0),
                      reads=[B["var"]], writes=[B["var"]])
                fw.op("act", lambda e: e.activation(out=rstd_sb[:], in_=var_sb[:], func=AF.Exp, scale=-0.5),
                      reads=[B["var"]], writes=[B["rstd"]])
                fw.op("dve", lambda e: e.tensor_tensor(out=nmr_sb[:], in0=mean_sb[:], in1=rstd_sb[:], op=ALU.mult),
                      reads=[B["mean"], B["rstd"]], writes=[B["nmr"]])
        fw.barrier()

def build_fused():
    TT = 512
    NT1 = 16
    nc = bass.Bass("TRN2", target_bir_lowering=False)
    xT = nc.dram_tensor("xT", [D, 4 * CH], F32, kind="ExternalInput").ap()
    w_in = nc.dram_tensor("w_in", [D, 3088], F32, kind="ExternalInput").ap()
    wa2 = nc.dram_tensor("wa2", [16, 512], F32, kind="ExternalInput").ap()
    ba2 = nc.dram_tensor("ba2", [128, 4], F32, kind="ExternalInput").ap()
    ngb = nc.dram_tensor("ngb", [128, 1024], F32, kind="ExternalInput").ap()
    cmask = nc.dram_tensor("cmask", [128, 512], F32, kind="ExternalInput").ap()
    ident = nc.dram_tensor("ident", [128, 128], F32, kind="ExternalInput").ap()
    w_o1 = nc.dram_tensor("w_o1", [D, D], F32, kind="ExternalInput").ap()
    w_o2 = nc.dram_tensor("w_o2", [D, D], F32, kind="ExternalInput").ap()
    lnp = nc.dram_tensor("lnp", [128, 4, KC], F32, kind="ExternalInput").ap()
    w_kv = nc.dram_tensor("w_kv", [D, 1536], F32, kind="ExternalInput").ap()
    w_s = nc.dram_tensor("w_s", [D, 4096], F32, kind="ExternalInput").ap()
    cosd, sind = [], []
    for g, (_, d) in enumerate(SWA_GROUPS):
        NP = (d + 16) * 128
        cosd.append(nc.dram_tensor("cos%d" % g, [128, NP], F32, kind="ExternalInput").ap())
        sind.append(nc.dram_tensor("sin%d" % g, [128, NP], F32, kind="ExternalInput").ap())
    mcur = nc.dram_tensor("mcur", [128, 512], F32, kind="ExternalInput").ap()
    mprev = nc.dram_tensor("mprev", [128, 512], F32, kind="ExternalInput").ap()
    mprevh = nc.dram_tensor("mprevh", [128, 512], F32, kind="ExternalInput").ap()
    esel = nc.dram_tensor("esel", [128, 512], F32, kind="ExternalInput").ap()
    fsel = nc.dram_tensor("fsel", [128, 512], F32, kind="ExternalInput").ap()
    outT = nc.dram_tensor("outT", [D, CH], F32, kind="ExternalOutput").ap()
    ogT_s = nc.dram_tensor("ogT_s", [D, 2 * CH], BF16).ap()
    x1_s = nc.dram_tensor("x1_s", [D, CH], F32).ap()
    og2T_s = nc.dram_tensor("og2T_s", [D, CH], BF16).ap()

    es = contextlib.ExitStack()
    with es:
        fw = FW(nc, es)
        ln_sb = _sb(nc, es, "ln_sb", [128, 4, KC], F32)
        Bln = fw.buf("ln")
        fw.dma("sp", ln_sb[:], lnp[:, :, :], writes=[Bln])
        B_ogT = fw.buf("ogT_s")
        B_x1s = fw.buf("x1_s")
        B_og2 = fw.buf("og2T_s")
        xT_v = xT.rearrange("(kc p) t -> p kc t", p=128)
        ogT_v = ogT_s.rearrange("(kc p) t -> p kc t", p=128)
        x1s_v = x1_s.rearrange("(kc p) t -> p kc t", p=128)
        og2_v = og2T_s.rearrange("(kc p) t -> p kc t", p=128)
        out_v = outT.rearrange("(kc p) t -> p kc t", p=128)

        _gla_phase2(nc, fw, xT_v, w_in, wa2, ba2, ngb, cmask, ident, ogT_v, B_ogT)

        xes = contextlib.ExitStack()
        with xes:
            x1b = _sb(nc, xes, "x1b", [128, KC, 2 * CH], BF16)
            _pl_phase(nc, fw, w_o1, ln_sb, 0, ogT_v, xT_v, 2 * CH, 8, 4, x1b, x1s_v, B_x1s, "p2_")
            _attn_phase(nc, fw, x1b, w_kv, w_s, cosd, sind, (mcur, mprev, mprevh), esel, fsel, ident, og2_v, B_og2)
        _pl_phase(nc, fw, w_o2, ln_sb, 1, og2_v, x1s_v, 0, 4, 0, None, out_v, None, "p4_")
        fw.finish()
    return nc


def _attn_phase(nc, fw, x1b, w_kv, w_s, cosd, sind, masks, esel, fsel, ident, og2_v, B_og2):
    pes = contextlib.ExitStack()
    with pes:
        wq_sb = _sb(nc, pes, "a_wq_sb", [128, KC, 512], BF16)
        wk_sb = _sb(nc, pes, "a_wk_sb", [128, KC, 128], BF16)
        wv_sb = _sb(nc, pes, "a_wv_sb", [128, KC, 128], BF16)
        wg_sb = wq_sb
        xs_sb = [_sb(nc, pes, "a_xs_sb%d" % i, [128, KC, 512], BF16) for i in range(2)]
        cs_sb = [_sb(nc, pes, "a_cs_sb%d" % i, [128, 512], F32) for i in range(2)]
        sn_sb = [_sb(nc, pes, "a_sn_sb%d" % i, [128, 512], F32) for i in range(2)]
        Kt_sb = _sb(nc, pes, "a_Kt_sb", [128, 4096], BF16)
        V_sb = _sb(nc, pes, "a_V_sb", [128, 32, 128], BF16)
        Qt_sb = _sb(nc, pes, "a_Qt_sb", [128, 4, CH], BF16)
        t1_sb = [_sb(nc, pes, "a_t1_sb%d" % i, [128, 512], F32) for i in range(2)]
        t2_sb = [_sb(nc, pes, "a_t2_sb%d" % i, [128, 512], F32) for i in range(2)]
        pe_sb = [[_sb(nc, pes, "a_pe_sb%d_%d" % (k, i), [128, 4, 128], BF16) for i in range(2)] for k in range(2)]
        pm_sb = [[_sb(nc, pes, "a_pm_sb%d_%d" % (k, i), [128, 4, 128], BF16) for i in range(2)] for k in range(2)]
        m_sb = [_sb(nc, pes, "a_m_sb%d" % i, [128, 4, 128], BF16) for i in range(3)]
        aid_sb = _sb(nc, pes, "a_id_sb", [128, 128], BF16)
        E_sb = _sb(nc, pes, "a_E_sb", [128, 4, 128], BF16)
        F_sb = _sb(nc, pes, "a_F_sb", [128, 4, 128], F32)
        acc_num = _sb(nc, pes, "a_acc_num", [128, 4, CH], F32)
        acc_den = _sb(nc, pes, "a_acc_den", [128, CH], F32)
        rden_sb = _sb(nc, pes, "a_rden_sb", [128, 512], F32)
        sg_sb = [_sb(nc, pes, "a_sg_sb%d" % i, [128, 512], F32) for i in range(2)]
        tt_sb = [_sb(nc, pes, "a_tt_sb%d" % i, [128, 512], F32) for i in range(2)]
        og2_sb = [_sb(nc, pes, "a_og2_sb%d" % i, [128, 512], BF16) for i in range(2)]
        pj_ps = [_ps(nc, pes, "a_pj_ps%d" % i, [128, 512], F32) for i in range(2)]
        vd_ps = _ps(nc, pes, "a_vd_ps", [128, 512], F32)
        v_ps = vd_ps[:, 0:128]
        den_ps = vd_ps[:, 128:256]
        s_ps2 = [[_ps(nc, pes, "a_s_ps%d_%d" % (st, i), [128, 4, 128], F32) for i in range(2)] for st in range(2)]
        num_ps = _ps(nc, pes, "a_num_ps", [128, 4, 128], F32)
        B = {}
        for n in ["wq", "wk", "wv", "wg", "Kt", "V", "Qt", "E", "F", "v_ps", "num_ps", "den_ps", "accn", "accd", "rden", "x1b"]:
            B[n] = fw.buf("a_" + n)
        for n in ["cs", "sn", "t1", "t2", "pj_ps", "sg", "tt", "og2", "xs"]:
            B[n] = fw.bufs(2, "a_" + n)
        SPS = [fw.bufs(2, "a_s_ps%d_" % st) for st in range(2)]
        B["den_ps"] = B["v_ps"]
        B["m"] = fw.bufs(3, "a_m")
        B["pe"] = [fw.bufs(2, "a_pe%d_" % k) for k in range(2)]
        B["pm"] = [fw.bufs(2, "a_pm%d_" % k) for k in range(2)]
        B["aid"] = fw.buf("a_id")

        def load_consts():
            for i, mm in enumerate(masks):
                fw.dma("pool", m_sb[i][:], mm.rearrange("p (h q) -> p h q", h=4), writes=[B["m"][i]])
            fw.dma("pool", E_sb[:], esel.rearrange("p (h q) -> p h q", h=4), writes=[B["E"]])
            fw.dma("pool", aid_sb[:], ident[:, :], writes=[B["aid"]])
            fw.dma("sp", F_sb[:], fsel.rearrange("p (h q) -> p h q", h=4), writes=[B["F"]])
        pjc = [0]

        def rope(ps, wd, xp, out_ap, out_buf):
            a = pjc[0] % 2
            fw.op("dve", lambda e: e.tensor_tensor(out=t1_sb[a][:, 0:wd], in0=ps[:, 0:wd], in1=cs_sb[xp][:, 0:wd], op=ALU.mult),
                  reads=[B["pj_ps"][a], B["cs"][xp]], writes=[B["t1"][a]])
            fw.op("dve", lambda e: e.tensor_tensor(out=t2_sb[a][0:64, 0:wd], in0=ps[64:128, 0:wd], in1=sn_sb[xp][64:128, 0:wd], op=ALU.mult),
                  reads=[B["pj_ps"][a], B["sn"][xp]], writes=[B["t2"][a]])
            fw.op("dve", lambda e: e.tensor_tensor(out=t2_sb[a][64:128, 0:wd], in0=ps[0:64, 0:wd], in1=sn_sb[xp][0:64, 0:wd], op=ALU.mult),
                  reads=[B["pj_ps"][a], B["sn"][xp]], writes=[B["t2"][a]])
            fw.op("pool", lambda e: e.tensor_tensor(out=out_ap, in0=t1_sb[a][:, 0:wd], in1=t2_sb[a][:, 0:wd], op=ALU.add),
                  reads=[B["t1"][a], B["t2"][a]], writes=[out_buf])

        def load_wkv(hk_, g_):
            kcol = (2 * g_ + hk_) * 128
            fw.dma("pool", wk_sb[:], w_kv.rearrange("(kc p) n -> p kc n", p=128)[:, :, kcol:kcol + 128], writes=[B["wk"]])
            fw.dma("pool", wv_sb[:], w_kv.rearrange("(kc p) n -> p kc n", p=128)[:, :, 768 + kcol:768 + kcol + 128], writes=[B["wv"]])

        def load_wq(hk_, g_):
            fw.dma("pool", wq_sb[:], w_s.rearrange("(kc p) n -> p kc n", p=128)[:, :, g_ * 1024 + hk_ * 512:g_ * 1024 + (hk_ + 1) * 512],
                   writes=[B["wq"]])

        def load_w(hk_, g_):
            load_wkv(hk_, g_)
            load_wq(hk_, g_)

        def load_tab(g_, tile, cnt):
            p0_, wd_ = tile
            xp_ = cnt % 2
            fw.dma("sp", cs_sb[xp_][:, 0:wd_], cosd[g_][:, p0_:p0_ + wd_], writes=[B["cs"][xp_]])
            fw.dma("sp", sn_sb[xp_][:, 0:wd_], sind[g_][:, p0_:p0_ + wd_], writes=[B["sn"][xp_]])

        tcount = 0
        acount = 0
        ocount = 0
        for hk in range(2):
            for g, (_, d) in enumerate(SWA_GROUPS):
                HL = d * 128
                n = CH // d
                halo_v = lambda kc: x1b[:, kc, CH - 128 * d:CH].rearrange("p (i r) -> p r i", r=d)
                own_v = lambda kc: x1b[:, kc, CH:2 * CH].rearrange("p (i r) -> p r i", r=d)

                def xsrc(kc, p0, wd):
                    if p0 < HL:
                        r0 = p0 // 128
                        nr = wd // 128
                        return halo_v(kc)[:, r0:r0 + nr, :], nr
                    q0 = p0 - HL
                    if n >= wd:
                        r, i0 = q0 // n, q0 % n
                        return own_v(kc)[:, r, i0:i0 + wd], 1
                    r0 = q0 // n
                    nr = wd // n
                    return own_v(kc)[:, r0:r0 + nr, :], nr

                cur = {}

                def stage(tile, cnt, share=False):
                    if d == 1:
                        return
                    p0_, wd_ = tile
                    xq = cnt % 2
                    for kc in range(KC):
                        src, nr = xsrc(kc, p0_, wd_)
                        dst = xs_sb[xq][:, kc, 0:wd_]
                        if len(src.shape) == 3:
                            dst = dst.rearrange("p (r i) -> p r i", r=nr)
                        if share and kc % 2 == 1:
                            fw.op("dve", lambda e, src=src, dst=dst: e.tensor_copy(out=dst, in_=src),
                                  reads=[B["x1b"]], writes=[B["xs"][xq]])
                        else:
                            fw.op("act", lambda e, src=src, dst=dst: e.activation(out=dst, in_=src, func=AF.Copy),
                                  reads=[B["x1b"]], writes=[B["xs"][xq]])

                def xview(kc, p0, wd):
                    if d == 1:
                        return xsrc(kc, p0, wd)
                    return xs_sb[cur["xq"]][:, kc, 0:wd], 1

                def xblock(kc, pb):
                    if d == 1:
                        if pb < HL:
                            return halo_v(kc)[:, pb // 128, :]
                        q = pb - HL
                        return own_v(kc)[:, q // n, (q % n):(q % n) + 128]
                    off = pb - cur["p0"]
                    return xs_sb[cur["xq"]][:, kc, off:off + 128]

                def proj(w_ap_fn, wb, p0, wd):
                    a = pjc[0] % 2
                    for kc in range(KC):
                        rhs, nr = xview(kc, p0, wd)
                        out = pj_ps[a][:, 0:wd] if nr == 1 and len(rhs.shape) == 2 else pj_ps[a][:, 0:wd].rearrange("p (r i) -> p r i", r=nr)
                        fw.op("pe", lambda e, kc=kc, rhs=rhs, out=out: e.matmul(out, lhsT=w_ap_fn(kc), rhs=rhs,
                                                                                 start=(kc == 0), stop=(kc == KC - 1)),
                              reads=[wb, B["x1b"], B["xs"][cur.get("xq", 0)]], writes=[B["pj_ps"][a]], inc=(kc == KC - 1))
                    return pj_ps[a]

                if hk == 0 and g == 0:
                    load_w(0, 0)
                    load_consts()
                if d == 1:
                    tiles = [(0, 128)]
                else:
                    tiles = [(i * 512, 512) for i in range(HL // 512)]
                tiles += [(HL + i * 512, 512) for i in range(4)]
                if hk == 0 and g == 0:
                    load_tab(g, tiles[0], tcount)
                stage(tiles[0], tcount, share=True)
                for ti, (p0, wd) in enumerate(tiles):
                    xp = tcount % 2
                    cur["xq"] = xp
                    cur["p0"] = p0
                    tcount += 1
                    if ti + 1 < len(tiles):
                        load_tab(g, tiles[ti + 1], tcount)
                        stage(tiles[ti + 1], tcount, share=(p0 < HL))

                    def vblock(blk):
                        for kc in range(KC):
                            fw.op("pe", lambda e, kc=kc: e.matmul(
                                v_ps, lhsT=xblock(kc, p0 + blk * 128), rhs=wv_sb[:, kc, :],
                                start=(kc == 0), stop=(kc == KC - 1)),
                                reads=[B["wv"], B["x1b"], B["xs"][cur.get("xq", 0)]], writes=[B["v_ps"]], inc=(kc == KC - 1))
                        fw.op("act", lambda e: e.activation(out=V_sb[:, p0 // 128 + blk, :], in_=v_ps, func=AF.Copy),
                              reads=[B["v_ps"]], writes=[B["V"]])

                    ps = proj(lambda kc: wk_sb[:, kc, :], B["wk"], p0, wd)
                    rope(ps, wd, xp, Kt_sb[:, p0:p0 + wd], B["Kt"])
                    pjc[0] += 1
                    nblk = wd // 128
                    if p0 >= HL:
                        q0 = p0 - HL
                        for hh in range(4):
                            if hh < nblk:
                                vblock(hh)
                            ps = proj(lambda kc, hh=hh: wq_sb[:, kc, hh * 128:(hh + 1) * 128], B["wq"], p0, wd)
                            rope(ps, wd, xp, Qt_sb[:, hh, q0:q0 + wd], B["Qt"])
                            pjc[0] += 1
                    else:
                        for blk in range(nblk):
                            vblock(blk)
                nxt = (hk, g + 1) if g + 1 < 3 else None
                if nxt is not None:
                    load_w(*nxt)
                    d2 = SWA_GROUPS[nxt[1]][1]
                    load_tab(nxt[1], (0, 128) if d2 == 1 else (0, 512), tcount)
                nbr = 16 // d
                accn_v = acc_num[:, :, :].rearrange("p h (i r) -> p h r i", r=d)
                accd_v = acc_den[:, :].rearrange("p (i r) -> p r i", r=d)
                def blk_info(j):
                    if j % nbr == 0:
                        prevpos, pm_i = (j // nbr) * 128, 2
                    else:
                        prevpos, pm_i = HL + (j - 1) * 128, 1
                    return prevpos, pm_i, HL + j * 128

                def scores(j, a):
                    prevpos, pm_i, curpos = blk_info(j)
                    for kb, (kpos, mi) in enumerate(((prevpos, pm_i), (curpos, 0))):
                        fw.op("pe", lambda e, kb=kb, kpos=kpos: e.matmul(
                            s_ps2[a][kb][:], lhsT=Kt_sb[:, kpos:kpos + 128], rhs=Qt_sb[:, :, j * 128:(j + 1) * 128],
                            start=True, stop=False),
                            reads=[B["Kt"], B["Qt"]], writes=[SPS[a][kb]], inc=False)
                        fw.op("pe", lambda e, kb=kb, mi=mi: e.matmul(
                            s_ps2[a][kb][:], lhsT=aid_sb[:], rhs=m_sb[mi][:], start=False, stop=True),
                            reads=[B["aid"], B["m"][mi]], writes=[SPS[a][kb]])
                        fw.op("act", lambda e, kb=kb: e.activation(out=pm_sb[kb][a][:], in_=s_ps2[a][kb][:], func=AF.Exp, scale=QS),
                              reads=[SPS[a][kb]], writes=[B["pm"][kb][a]])

                def pv(j, a):
                    prevpos, pm_i, curpos = blk_info(j)
                    r_j, ib = j // nbr, j % nbr
                    for kb, kpos in enumerate((prevpos, curpos)):
                        fw.op("pe", lambda e, kb=kb, kpos=kpos: e.matmul(
                            num_ps[:], lhsT=V_sb[:, kpos // 128, :], rhs=pm_sb[kb][a][:],
                            start=(kb == 0), stop=(kb == 1)),
                            reads=[B["V"], B["pm"][kb][a]], writes=[B["num_ps"]], inc=(kb == 1))
                    for hh in range(4):
                        for kb in range(2):
                            fw.op("pe", lambda e, kb=kb, hh=hh: e.matmul(den_ps, lhsT=E_sb[:, hh, :], rhs=pm_sb[kb][a][:, hh, :],
                                                                         start=(hh == 0 and kb == 0), stop=(hh == 3 and kb == 1)),
                                  reads=[B["E"], B["pm"][kb][a]], writes=[B["den_ps"]], inc=(hh == 3 and kb == 1))
                    nv = accn_v[:, :, r_j, ib * 128:(ib + 1) * 128]
                    dv = accd_v[:, r_j, ib * 128:(ib + 1) * 128]
                    if g == 0:
                        fw.op("act", lambda e, nv=nv: e.activation(out=nv, in_=num_ps[:], func=AF.Copy),
                              reads=[B["num_ps"]], writes=[B["accn"]])
                        fw.op("dve", lambda e, dv=dv: e.tensor_copy(out=dv, in_=den_ps),
                              reads=[B["den_ps"]], writes=[B["accd"]])
                    else:
                        fw.op("dve", lambda e, nv=nv: e.tensor_tensor(out=nv, in0=nv, in1=num_ps[:], op=ALU.add),
                              reads=[B["num_ps"], B["accn"]], writes=[B["accn"]])
                        fw.op("dve", lambda e, dv=dv: e.tensor_tensor(out=dv, in0=dv, in1=den_ps, op=ALU.add),
                              reads=[B["den_ps"], B["accd"]], writes=[B["accd"]])

                aj = []
                for j in range(16):
                    aj.append(acount % 2)
                    acount += 1
                scores(0, aj[0])
                for j in range(16):
                    if j + 1 < 16:
                        scores(j + 1, aj[j + 1])
                    pv(j, aj[j])
            B["wg"] = B["wq"]
            fw.dma("pool", wg_sb[:], w_s.rearrange("(kc p) n -> p kc n", p=128)[:, :, 3072 + hk * 512:3072 + (hk + 1) * 512],
                   writes=[B["wg"]])
            if hk == 0:
                load_wkv(1, 0)
                load_tab(0, (0, 128), tcount)
            for tt in range(4):
                c0 = tt * 512
                fw.op("dve", lambda e: e.reciprocal(out=rden_sb[:], in_=acc_den[:, c0:c0 + 512]),
                      reads=[B["accd"]], writes=[B["rden"]])
                for hh in range(4):
                    o = ocount % 2
                    ocount += 1
                    for kc in range(KC):
                        fw.op("pe", lambda e, kc=kc, hh=hh: e.matmul(pj_ps[0][:], lhsT=wg_sb[:, kc, hh * 128:(hh + 1) * 128],
                                                                     rhs=x1b[:, kc, CH + c0:CH + c0 + 512],
                                                                     start=(kc == 0), stop=(kc == KC - 1)),
                              reads=[B["wg"], B["x1b"]], writes=[B["pj_ps"][0]], inc=(kc == KC - 1))
                    fw.op("pe", lambda e, hh=hh: e.matmul(pj_ps[1][:], lhsT=F_sb[:, hh, :], rhs=rden_sb[:], start=True, stop=True),
                          reads=[B["F"], B["rden"]], writes=[B["pj_ps"][1]])
                    fw.op("act", lambda e, o=o: e.activation(out=sg_sb[o][:], in_=pj_ps[0][:], func=AF.Silu),
                          reads=[B["pj_ps"][0]], writes=[B["sg"][o]])
                    fw.op("dve", lambda e, o=o, hh=hh: e.tensor_tensor(out=tt_sb[o][:], in0=acc_num[:, hh, c0:c0 + 512], in1=pj_ps[1][:], op=ALU.mult),
                          reads=[B["accn"], B["pj_ps"][1]], writes=[B["tt"][o]])
                    fw.op("pool", lambda e, o=o: e.tensor_tensor(out=og2_sb[o][:], in0=tt_sb[o][:], in1=sg_sb[o][:], op=ALU.mult),
                          reads=[B["tt"][o], B["sg"][o]], writes=[B["og2"][o]])
                    fw.dma("sp", og2_v[:, 4 * hk + hh, c0:c0 + 512], og2_sb[o][:], reads=[B["og2"][o]], writes=[B_og2], owner=B["og2"][o])
            if hk == 0:
                load_wq(1, 0)
        fw.barrier()


def _sel_consts():
    k = np.arange(128)[:, None]
    m = np.arange(128)[None, :]
    esel = np.concatenate([np.broadcast_to((m // 32 == hh), (128, 128)).astype(np.float32) for hh in range(4)], axis=1)
    fsel = np.concatenate([np.broadcast_to((k // 32 == hh), (128, 128)).astype(np.float32) / 32.0 for hh in range(4)], axis=1)
    return np.ascontiguousarray(esel), np.ascontiguousarray(fsel)


def _fused_inputs(b, c, x, gla_w_in, gla_w_a2, gla_b_a2, gla_norm_g, gla_w_out, w_kv, swa_w_in, swa_w_out, ln_g, ln_b):
    cmask, ident = _consts()
    esel, fsel = _sel_consts()
    xT = np.zeros((D, 4 * CH), dtype=np.float32)
    lo = (c - 3) * CH
    src0 = max(lo, 0)
    xT[:, src0 - lo:] = x[b, src0:(c + 1) * CH].T
    lnp = np.ascontiguousarray(np.stack([ln_g[0].reshape(KC, 128).T, ln_b[0].reshape(KC, 128).T,
                                         ln_g[1].reshape(KC, 128).T, ln_b[1].reshape(KC, 128).T], axis=1)).astype(np.float32)
    im = {"xT": xT, "w_in": gla_w_in[0], "wa2": gla_w_a2[0],
          "ba2": np.ascontiguousarray(gla_b_a2[0].reshape(4, 128).T),
          "ngb": np.ascontiguousarray(np.broadcast_to(gla_norm_g[0][None, :], (128, 1024))),
          "cmask": np.ascontiguousarray(np.tile(cmask, (1, 4))), "ident": ident, "w_o1": gla_w_out[0], "w_o2": swa_w_out[0], "lnp": lnp,
          "w_kv": w_kv, "w_s": swa_w_in[0], "esel": esel, "fsel": fsel}
    for g, idx in enumerate(_l3_perm(c)):
        cosT, sinT = _rope_tables(idx)
        im["cos%d" % g] = cosT
        im["sin%d" % g] = sinT
    for nm, mk in zip(("mcur", "mprev", "mprevh"), _l3_masks(c)):
        im[nm] = np.ascontiguousarray((mk - 1.0) * 30000.0).astype(np.float32)
    return im


def kernel_fused(x, gla_w_in, gla_w_a2, gla_b_a2, gla_norm_g, gla_w_out, w_kv, swa_w_in, swa_w_out, ln_g, ln_b):
    f = lambda a: np.asarray(a, dtype=np.float32)
    args = list(map(f, (x, gla_w_in, gla_w_a2, gla_b_a2, gla_norm_g, gla_w_out, w_kv, swa_w_in, swa_w_out, ln_g, ln_b)))
    if "fused" not in _NC_CACHE:
        _NC_CACHE["fused"] = build_fused()
    in_maps = [_fused_inputs(core // 4, core % 4, *args) for core in range(8)]
    r = _run(_NC_CACHE["fused"], in_maps)
    out = np.zeros((BATCH, SEQ, D), dtype=np.float32)
    for core in range(8):
        b, c = core // 4, core % 4
        out[b, c * CH:(c + 1) * CH, :] = r[core]["outT"].T
    return out


def kernel(x, gla_w_in, gla_w_a2, gla_b_a2, gla_norm_g, gla_w_out, w_kv, swa_w_in, swa_w_out, ln_g, ln_b):
    return kernel_fused(x, gla_w_in, gla_w_a2, gla_b_a2, gla_norm_g, gla_w_out, w_kv, swa_w_in, swa_w_out, ln_g, ln_b)


def _gla_phase2(nc, fw, xT_v, w_in, wa2, ba2, ngb, cmask4, ident, ogT_v, B_ogT):
    TT = 512
    pes = contextlib.ExitStack()
    with pes:
        w_sb = _sb(nc, pes, "g_w_sb", [128, KC, 3088], BF16)
        wa2_sb = _sb(nc, pes, "g_wa2_sb", [16, 512], BF16)
        nb_sb = _sb(nc, pes, "g_nb_sb", [128, 4], F32)
        ng_sb = _sb(nc, pes, "g_ng_sb", [128, 1024], F32)
        cm_sb = _sb(nc, pes, "g_cm_sb", [128, 4, 128], F32)
        id_sb = _sb(nc, pes, "g_id_sb", [128, 128], BF16)
        rm_sb = _sb(nc, pes, "g_rm_sb", [128, TT], F32)
        S_all = _sb(nc, pes, "g_S_all", [128, 4, 256], F32)
        Sb_all = _sb(nc, pes, "g_Sb_all", [128, 4, 256], BF16)
        xTb = [_sb(nc, pes, "g_xTb%d" % i, [128, KC, TT], BF16) for i in range(2)]
        aT_sb = _sb(nc, pes, "g_aT_sb", [16, TT], BF16)
        e1_sb = _sb(nc, pes, "g_e1_sb", [128, TT], F32)
        sp_sb = _sb(nc, pes, "g_sp_sb", [128, TT], F32)
        cs_sb = _sb(nc, pes, "g_cs_sb", [128, TT], F32)
        ek_sbs = [_sb(nc, pes, "g_ek_sb%d" % i, [128, TT], F32) for i in range(2)]
        Kh = [_sb(nc, pes, "g_Kh%d" % i, [128, TT], BF16) for i in range(2)]
        eq_hp = [[_sb(nc, pes, "g_eq%d_%d" % (i, h), [128, TT], F32) for h in range(4)] for i in range(2)]
        Qt_h = [_sb(nc, pes, "g_Qt%d" % h, [128, TT], BF16) for h in range(4)]
        Kt_h = [_sb(nc, pes, "g_Kt%d" % h, [128, TT], BF16) for h in range(4)]
        Khtok_hp = [[_sb(nc, pes, "g_Khtok%d_%d" % (i, h), [128, 4, 128], BF16) for h in range(4)] for i in range(2)]
        v_hp = [[_sb(nc, pes, "g_v%d_%d" % (i, h), [128, 4, 256], BF16) for h in range(4)] for i in range(2)]
        gs_sb = [_sb(nc, pes, "g_gs_sb%d" % i, [128, 256], F32) for i in range(2)]
        gn_all = _sb(nc, pes, "g_gn_all", [128, 4, 4, 256], F32)
        att_sb = [_sb(nc, pes, "g_att_sb%d" % i, [128, 4, 128], BF16) for i in range(2)]
        o_sb = [_sb(nc, pes, "g_o_sb%d" % i, [128, 4, 256], F32) for i in range(2)]
        sq_sb = _sb(nc, pes, "g_sq_sb", [128, 4, 256], F32)
        tmp_sb = _sb(nc, pes, "g_tmp_sb", [128, 4, 256], F32)
        ss_sb = _sb(nc, pes, "g_ss_sb", [128, 4], F32)
        rs_sb = _sb(nc, pes, "g_rs_sb", [128, 4], F32)
        og_all = _sb(nc, pes, "g_og_all", [128, 4, 4, 256], BF16)
        ogT_sb = [_sb(nc, pes, "g_ogT_sb%d" % i, [128, KC, TT], BF16) for i in range(2)]
        pj_ps = [_ps(nc, pes, "g_pj_ps%d" % i, [128, 512], F32) for i in range(2)]
        tr_ps = _ps(nc, pes, "g_tr_ps", [128, 4, 256], BF16)
        at_ps = _ps(nc, pes, "g_at_ps", [128, 4, 128], F32)
        o_ps = [_ps(nc, pes, "g_o_ps%d" % i, [128, 2, 256], F32) for i in range(2)]
        kv_ps = [_ps(nc, pes, "g_kv_ps%d" % i, [128, 2, 256], F32) for i in range(2)]
        B = {}
        for n in ["w", "wa2", "nb", "ng", "cm", "id", "rm", "aT", "e1", "sp", "cs", "ek", "sq", "tmp", "ss", "rs",
                  "at_ps", "tr_ps", "Sb", "og"]:
            B[n] = fw.buf("g_" + n)
        for n in ["xTb", "Kh", "pj_ps", "gs", "att", "o_sb", "o_ps", "kv_ps", "ogT"]:
            B[n] = fw.bufs(2, "g_" + n)
        for n in ["Qt", "Kt", "S", "gn"]:
            B[n] = fw.bufs(4, "g_" + n)
        BP = {n: [fw.bufs(4, "g_%s%d_" % (n, i)) for i in range(2)] for n in ["eq", "Khtok", "v"]}
        BEK = fw.bufs(2, "g_ekb")
        w_v = w_in.rearrange("(kc p) n -> p kc n", p=128)
        WB = {}

        def load_wblk(key, c0, c1):
            WB[key] = fw.buf("g_w_" + key)
            fw.dma("pool", w_sb[:, :, c0:c1], w_v[:, :, c0:c1], writes=[WB[key]])

        load_wblk("a", 3072, 3088)
        for h in range(4):
            load_wblk("k%d" % h, 512 + h * 128, 512 + (h + 1) * 128)
            load_wblk("v%d" % h, 1024 + h * 256, 1024 + (h + 1) * 256)
        fw.dma("pool", wa2_sb[:], wa2[:, :], writes=[B["wa2"]])
        fw.dma("sp", nb_sb[:], ba2[:, :], writes=[B["nb"]])
        fw.dma("sp", ng_sb[:], ngb[:, :], writes=[B["ng"]])
        fw.dma("sp", cm_sb[:], cmask4.rearrange("p (h q) -> p h q", h=4), writes=[B["cm"]])
        fw.dma("pool", id_sb[:], ident[:, :], writes=[B["id"]])
        fw.op("dve", lambda e: e.tensor_scalar(out=nb_sb[:], in0=nb_sb[:], scalar1=-1.0, scalar2=None, op0=ALU.mult),
              reads=[B["nb"]], writes=[B["nb"]])
        fw.op("dve", lambda e: e.memset(rm_sb[:], 1.0), writes=[B["rm"]])
        for c in range(4):
            fw.op("dve", lambda e, c=c: e.memset(rm_sb[:, c * 128:c * 128 + 1], 0.0), writes=[B["rm"]])
        fw.op("dve", lambda e: e.memset(S_all[:], 0.0), writes=B["S"])
        fw.op("dve", lambda e: e.memset(Sb_all[:], 0.0), writes=[B["Sb"]])
        pjc = [0]
        khc = [0]
        gsc = [0]
        bc = [0]

        def proj(lhs_fn, rhs_fn, reads, cols, rows=128):
            a = pjc[0] % 2
            pjc[0] += 1
            for kc in range(KC):
                fw.op("pe", lambda e, kc=kc: e.matmul(pj_ps[a][0:rows, cols], lhsT=lhs_fn(kc), rhs=rhs_fn(kc),
                                                      start=(kc == 0), stop=(kc == KC - 1)),
                      reads=reads, writes=[B["pj_ps"][a]], inc=(kc == KC - 1))
            return a

        def load_x(t):
            fw.dma("pool", xTb[t % 2][:], xT_v[:, :, t * TT:(t + 1) * TT], writes=[B["xTb"][t % 2]])

        kq_ps = [kv_ps[i][:].rearrange("p a b -> p (a b)") for i in range(2)]
        pending = []
        at_bf = at_ps[:].bitcast(BF16)

        def og_round(tt_, h, half):
            xq = tt_ % 2
            tp, tb = (tr_ps, B["tr_ps"]) if half == 0 else (at_bf, B["at_ps"])
            for c in range(4):
                fw.op("pe", lambda e, c=c: e.transpose(tp[:, c, 0:128], og_all[:, c, h, half * 128:(half + 1) * 128], id_sb[:]),
                      reads=[B["og"], B["id"]], writes=[tb], inc=(c == 3))
            fw.op("act", lambda e: e.activation(out=ogT_sb[xq][:, 2 * h + half, :].rearrange("p (c i) -> p c i", c=4),
                                                in_=tp[:, :, 0:128], func=AF.Copy),
                  reads=[tb], writes=[B["ogT"][xq]])

        def og_store(tt_):
            xq = tt_ % 2
            c0 = (tt_ - 8) * TT
            fw.dma("sp", ogT_v[:, :, c0:c0 + TT], ogT_sb[xq][:], reads=[B["ogT"][xq]], writes=[B_ogT], owner=B["ogT"][xq])

        load_x(0)
        for h in range(4):
            load_wblk("q%d" % h, h * 128, (h + 1) * 128)
            load_wblk("g%d" % h, 2048 + h * 256, 2048 + (h + 1) * 256)
        hoisted = set()

        def vproj(tn, h):
            xq = tn % 2
            for c in range(4):
                va = pjc[0] % 2
                pjc[0] += 1
                for kc in range(KC):
                    fw.op("pe", lambda e, c=c, kc=kc, va=va: e.matmul(
                        pj_ps[va][:, 0:256], lhsT=xTb[xq][:, kc, c * 128:(c + 1) * 128],
                        rhs=w_sb[:, kc, 1024 + h * 256:1024 + (h + 1) * 256],
                        start=(kc == 0), stop=(kc == KC - 1)),
                        reads=[B["xTb"][xq], WB["v%d" % h]], writes=[B["pj_ps"][va]], inc=(kc == KC - 1))
                fw.op("dve", lambda e, c=c, va=va: e.tensor_copy(out=v_hp[xq][h][:, c, :], in_=pj_ps[va][:, 0:256]),
                      reads=[B["pj_ps"][va]], writes=[BP["v"][xq][h]])

        deferred = []
        for t in range(16):
            xp = t % 2
            full = t >= 8
            eq_h, Khtok_h, v_h = eq_hp[xp], Khtok_hp[xp], v_hp[xp]
            B["eq"], B["Khtok"], B["v"] = BP["eq"][xp], BP["Khtok"][xp], BP["v"][xp]
            xr = [B["xTb"][xp]]
            a = proj(lambda kc: w_sb[:, kc, 3072:3088], lambda kc: xTb[xp][:, kc, :], xr + [WB["a"]], slice(0, TT), rows=16)
            fw.op("act", lambda e, a=a: e.activation(out=aT_sb[:], in_=pj_ps[a][0:16, :], func=AF.Copy),
                  reads=[B["pj_ps"][a]], writes=[B["aT"]])
            if t + 1 < 16:
                load_x(t + 1)
            def gate(h, eq_h=eq_h, Beq=BP["eq"][xp]):
                az = pjc[0] % 2
                pjc[0] += 1
                fw.op("pe", lambda e: e.matmul(pj_ps[az][:], lhsT=wa2_sb[:, h * 128:(h + 1) * 128], rhs=aT_sb[:], start=True, stop=True),
                      reads=[B["wa2"], B["aT"]], writes=[B["pj_ps"][az]])
                fw.op("act", lambda e: e.activation(out=e1_sb[:], in_=pj_ps[az][:], func=AF.Exp, bias=nb_sb[:, h:h + 1], scale=-1.0),
                      reads=[B["pj_ps"][az], B["nb"]], writes=[B["e1"]])
                fw.op("act", lambda e: e.activation(out=sp_sb[:], in_=e1_sb[:], func=AF.Ln, bias=1.0, scale=1.0),
                      reads=[B["e1"]], writes=[B["sp"]])
                fw.op("dve", lambda e: e.tensor_tensor_scan(out=cs_sb[:], data0=rm_sb[:], data1=sp_sb[:], initial=0.0,
                                                            op0=ALU.mult, op1=ALU.add),
                      reads=[B["rm"], B["sp"]], writes=[B["cs"]])
                fw.op("act", lambda e: e.activation(out=eq_h[h][:], in_=cs_sb[:], func=AF.Exp, scale=-1.0 / 16.0),
                      reads=[B["cs"]], writes=[Beq[h]])
                fw.op("act", lambda e: e.activation(out=ek_sbs[h % 2][:], in_=cs_sb[:], func=AF.Exp, scale=1.0 / 16.0),
                      reads=[B["cs"]], writes=[BEK[h % 2]])

            gate(0)
            for h in range(4):
                ek_sb = ek_sbs[h % 2]
                B["ek"] = BEK[h % 2]
                for _ in range(2):
                    if pending:
                        og_round(*pending.pop(0))
                if deferred:
                    fn, cc = deferred.pop(0)
                    fn(cc)
                    B["eq"], B["Khtok"], B["v"] = BP["eq"][xp], BP["Khtok"][xp], BP["v"][xp]
                if h == 3 and t >= 9:
                    og_store(t - 1)
                for kc in range(KC):
                    fw.op("pe", lambda e, kc=kc, h=h: e.matmul(kq_ps[0], lhsT=w_sb[:, kc, 512 + h * 128:512 + (h + 1) * 128], rhs=xTb[xp][:, kc, :],
                                                               start=(kc == 0), stop=(kc == KC - 1)),
                          reads=xr + [WB["k%d" % h]], writes=[B["kv_ps"][0]], inc=(kc == KC - 1))
                if full:
                    for kc in range(KC):
                        fw.op("pe", lambda e, kc=kc, h=h: e.matmul(kq_ps[1], lhsT=w_sb[:, kc, h * 128:(h + 1) * 128], rhs=xTb[xp][:, kc, :],
                                                                   start=(kc == 0), stop=(kc == KC - 1)),
                              reads=xr + [WB["q%d" % h]], writes=[B["kv_ps"][1]], inc=(kc == KC - 1))
                vdone = t in hoisted
                for c in range(4):
                    va = pjc[0] % 2
                    pjc[0] += 1
                    for kc in range(KC):
                        if vdone:
                            break
                        fw.op("pe", lambda e, c=c, kc=kc, h=h, va=va: e.matmul(
                            pj_ps[va][:, 0:256], lhsT=xTb[xp][:, kc, c * 128:(c + 1) * 128],
                            rhs=w_sb[:, kc, 1024 + h * 256:1024 + (h + 1) * 256],
                            start=(kc == 0), stop=(kc == KC - 1)),
                            reads=xr + [WB["v%d" % h]], writes=[B["pj_ps"][va]], inc=(kc == KC - 1 and not full))
                    if full:
                        for kc in range(KC):
                            fw.op("pe", lambda e, c=c, kc=kc, h=h, va=va: e.matmul(
                                pj_ps[va][:, 256:512], lhsT=xTb[xp][:, kc, c * 128:(c + 1) * 128],
                                rhs=w_sb[:, kc, 2048 + h * 256:2048 + (h + 1) * 256],
                                start=(kc == 0), stop=(kc == KC - 1)),
                                reads=xr + [WB["g%d" % h]], writes=[B["pj_ps"][va]], inc=(kc == KC - 1))
                    if not vdone:
                        fw.op("act", lambda e, c=c, h=h, va=va: e.activation(out=v_h[h][:, c, :], in_=pj_ps[va][:, 0:256], func=AF.Copy),
                              reads=[B["pj_ps"][va]], writes=[B["v"][h]])
                    if full:
                        gq = gsc[0] % 2
                        gsc[0] += 1
                        fw.op("act", lambda e, va=va, gq=gq: e.activation(out=gs_sb[gq][:], in_=pj_ps[va][:, 256:512], func=AF.Silu),
                              reads=[B["pj_ps"][va]], writes=[B["gs"][gq]])
                        fw.op("pool", lambda e, c=c, h=h, gq=gq: e.tensor_tensor(out=gn_all[:, c, h, :], in0=gs_sb[gq][:],
                                                                                 in1=ng_sb[:, h * 256:(h + 1) * 256], op=ALU.mult),
                              reads=[B["gs"][gq], B["ng"]], writes=[B["gn"][c]])
                kp = khc[0] % 2
                khc[0] += 1
                if full:
                    fw.op("dve", lambda e, h=h: e.tensor_tensor(out=Kt_h[h][:], in0=kq_ps[0], in1=ek_sb[:], op=ALU.mult),
                          reads=[B["kv_ps"][0], B["ek"]], writes=[B["Kt"][h]])
                for c in range(4):
                    cl = c * 128 + 127
                    fw.op("dve", lambda e, c=c, cl=cl, h=h: e.scalar_tensor_tensor(
                        out=Kh[kp][:, c * 128:(c + 1) * 128], in0=kq_ps[0][:, c * 128:(c + 1) * 128],
                        scalar=eq_h[h][:, cl:cl + 1], in1=ek_sb[:, c * 128:(c + 1) * 128],
                        op0=ALU.mult, op1=ALU.mult),
                        reads=[B["kv_ps"][0], B["eq"][h], B["ek"]], writes=[B["Kh"][kp]])
                if full:
                    fw.op("dve", lambda e, h=h: e.scalar_tensor_tensor(out=Qt_h[h][:], in0=kq_ps[1], scalar=QS, in1=eq_h[h][:],
                                                                       op0=ALU.mult, op1=ALU.mult),
                          reads=[B["kv_ps"][1], B["eq"][h]], writes=[B["Qt"][h]])
                for c in range(4):
                    fw.op("pe", lambda e, c=c: e.transpose(tr_ps[:, c, 0:128], Kh[kp][:, c * 128:(c + 1) * 128], id_sb[:]),
                          reads=[B["Kh"][kp], B["id"]], writes=[B["tr_ps"]], inc=(c == 3))
                fw.op("act", lambda e, h=h: e.activation(out=Khtok_h[h][:], in_=tr_ps[:, :, 0:128], func=AF.Copy),
                      reads=[B["tr_ps"]], writes=[B["Khtok"][h]])
                if h < 3:
                    gate(h + 1)
            def norm(c, a):
                for h in range(4):
                    fw.op("act", lambda e, h=h: e.activation(out=sq_sb[:, h, :], in_=o_sb[a][:, h, :], func=AF.Square,
                                                             accum_out=ss_sb[:, h:h + 1]),
                          reads=[B["o_sb"][a]], writes=[B["sq"], B["ss"]])
                fw.op("act", lambda e: e.activation(out=ss_sb[:], in_=ss_sb[:], func=AF.Ln, bias=RMS_EPS, scale=1.0 / 256.0),
                      reads=[B["ss"]], writes=[B["ss"]])
                fw.op("act", lambda e: e.activation(out=rs_sb[:], in_=ss_sb[:], func=AF.Exp, scale=-0.5),
                      reads=[B["ss"]], writes=[B["rs"]])
                fw.op("pool", lambda e: e.tensor_tensor(out=tmp_sb[:], in0=o_sb[a][:], in1=gn_all[:, c, :, :], op=ALU.mult),
                      reads=[B["o_sb"][a], B["gn"][c]], writes=[B["tmp"]])
                fw.op("dve", lambda e: e.tensor_tensor(out=og_all[:, c, :, :], in0=tmp_sb[:],
                                                       in1=rs_sb[:, 0:4].unsqueeze(2).to_broadcast([128, 4, 256]), op=ALU.mult),
                      reads=[B["tmp"], B["rs"]], writes=[B["og"]])

            def stage_b(c, t=t, full=full, eq_h=eq_h, Khtok_h=Khtok_h, v_h=v_h,
                        Beq=B["eq"], BKh=B["Khtok"], Bv=B["v"]):
                B["eq"], B["Khtok"], B["v"] = Beq, BKh, Bv
                sl = slice(c * 128, (c + 1) * 128)
                cl = c * 128 + 127
                a = bc[0] % 2
                bc[0] += 1
                if full:
                    for h in range(4):
                        fw.op("pe", lambda e, h=h: e.matmul(at_ps[:, h, :], lhsT=Kt_h[h][:, sl], rhs=Qt_h[h][:, sl], start=True, stop=True),
                              reads=[B["Kt"][h], B["Qt"][h]], writes=[B["at_ps"]], inc=(h == 3))
                for h in range(4):
                    fw.op("pe", lambda e, h=h: e.matmul(kv_ps[h // 2][:, h % 2, :], lhsT=Khtok_h[h][:, c, :], rhs=v_h[h][:, c, :],
                                                        start=True, stop=True),
                          reads=[B["Khtok"][h], B["v"][h]], writes=[B["kv_ps"][h // 2]], inc=(h % 2 == 1))
                if full:
                    fw.op("dve", lambda e: e.tensor_tensor(out=att_sb[a][:], in0=at_ps[:], in1=cm_sb[:], op=ALU.mult),
                          reads=[B["at_ps"], B["cm"]], writes=[B["att"][a]])
                for h in range(4):
                    fw.op("dve", lambda e, h=h: e.scalar_tensor_tensor(out=S_all[:, h, :], in0=S_all[:, h, :], scalar=eq_h[h][:, cl:cl + 1],
                                                                       in1=kv_ps[h // 2][:, h % 2, :], op0=ALU.mult, op1=ALU.add),
                          reads=[B["S"][h], B["eq"][h], B["kv_ps"][h // 2]], writes=[B["S"][h]])
                if full and prevbox[0] is not None:
                    norm(*prevbox[0])
                if full:
                    for h in range(4):
                        fw.op("pe", lambda e, h=h: e.matmul(o_ps[h // 2][:, h % 2, :], lhsT=att_sb[a][:, h, :], rhs=v_h[h][:, c, :],
                                                            start=True, stop=False),
                              reads=[B["att"][a], B["v"][h]], writes=[B["o_ps"][h // 2]], inc=False)
                        fw.op("pe", lambda e, h=h: e.matmul(o_ps[h // 2][:, h % 2, :], lhsT=Qt_h[h][:, sl], rhs=Sb_all[:, h, :],
                                                            start=False, stop=True),
                              reads=[B["Qt"][h], B["Sb"]], writes=[B["o_ps"][h // 2]], inc=(h % 2 == 1))
                    for i in range(2):
                        fw.op("act", lambda e, i=i: e.activation(out=o_sb[a][:, 2 * i:2 * i + 2, :], in_=o_ps[i][:], func=AF.Copy),
                              reads=[B["o_ps"][i]], writes=[B["o_sb"][a]])
                if t >= 7:
                    fw.op("act", lambda e: e.activation(out=Sb_all[:], in_=S_all[:], func=AF.Copy),
                          reads=B["S"], writes=[B["Sb"]])
                if full:
                    prevbox[0] = (c, a)

            if full:
                prevbox = [None]
                for c in range(4):
                    stage_b(c)
                    if t + 1 < 16:
                        vproj(t + 1, c)
                if t + 1 < 16:
                    hoisted.add(t + 1)
            else:
                deferred.extend(stage_b for _ in range(0))
                deferred.extend([(stage_b, c) for c in range(4)])
            if full:
                norm(*prevbox[0])
                pending.extend((t, h, half) for h in range(4) for half in range(2))
                if t == 15:
                    while pending:
                        og_round(*pending.pop(0))
                    og_store(15)
        fw.barrier()
```

```python
import contextlib
import numpy as np
import ml_dtypes
import concourse.bass as bass
import concourse.mybir as mybir
from concourse.bass_utils import run_bass_kernel_spmd

F32 = mybir.dt.float32
BF16 = mybir.dt.bfloat16
ALU = mybir.AluOpType
AF = mybir.ActivationFunctionType
AX = mybir.AxisListType

D = 1024
SEQ = 8192
BATCH = 2
KC = 8
ALPHA = (2.0 * 2) ** 0.25
LN_EPS = 1e-5
RMS_EPS = 1e-5


class Buf:
    __slots__ = ("name", "w", "r", "dsem", "dcnt", "dgen")

    def __init__(self, name):
        self.name = name
        self.w = None
        self.r = {}
        self.dsem = None
        self.dcnt = 0
        self.dgen = -1


class FW:
    def __init__(self, nc, es):
        self.nc = nc
        self.es = es
        self.eng = {"pe": nc.tensor, "act": nc.scalar, "dve": nc.vector, "pool": nc.gpsimd, "sp": nc.sync}
        self.sems = []
        self.esem = {}
        self.cnt = {}
        self.known = {}
        for e in self.eng:
            self.esem[e] = self._newsem("p_" + e)
            self.cnt[e] = 0
            self.known[e] = {}
        self.nbuf = 0
        self.final = {}
        self.dtot = {}
        self.gen = 0
        self.free_dsems = {}
        self.dq = {}

    def _newsem(self, name):
        h = self.es.enter_context(self.nc.semaphore(name))
        self.sems.append(h)
        return len(self.sems) - 1

    def buf(self, name=None):
        self.nbuf += 1
        return Buf(name or ("b%d" % self.nbuf))

    def bufs(self, n, name=None):
        return [self.buf(None if name is None else "%s%d" % (name, i)) for i in range(n)]

    def _wait(self, e, ev):
        si, val = ev
        if e == "pe" and si == self.esem["pe"]:
            return
        k = self.known[e]
        if k.get(si, 0) >= val:
            return
        self.eng[e].wait_ge(self.sems[si], val)
        k[si] = val

    def _deps(self, e, reads, writes):
        evs = {}
        for b in reads:
            if b.w is not None:
                evs[b.w[0]] = max(evs.get(b.w[0], 0), b.w[1])
        for b in writes:
            if b.w is not None:
                evs[b.w[0]] = max(evs.get(b.w[0], 0), b.w[1])
            for si, v in b.r.items():
                evs[si] = max(evs.get(si, 0), v)
        for si, v in evs.items():
            self._wait(e, (si, v))

    def _mark(self, ev, reads, writes):
        for b in reads:
            b.r[ev[0]] = max(b.r.get(ev[0], 0), ev[1])
        for b in writes:
            b.w = ev
            b.r = {}

    def op(self, e, fn, reads=(), writes=(), inc=True):
        self._deps(e, reads, writes)
        ins = fn(self.eng[e])
        si = self.esem[e]
        if inc:
            self.cnt[e] += 1
            ins.then_inc(self.sems[si], 1)
            ev = (si, self.cnt[e])
            self.known[e][si] = max(self.known[e].get(si, 0), 0)
        else:
            ev = (si, self.cnt[e] + 1)
        self._mark(ev, reads, writes)
        return ins

    def dma(self, q, out, in_, reads=(), writes=(), owner=None, final=False, **kw):
        self._deps(q, reads, writes)
        if owner is None:
            owner = writes[0] if writes else reads[0]
        if owner.dsem is None or owner.dgen != self.gen:
            if self.free_dsems.get(q):
                owner.dsem = self.free_dsems[q].pop()
                owner.dcnt = self.dtot.get(owner.dsem, 0)
            else:
                owner.dsem = self._newsem("d_" + owner.name)
                owner.dcnt = 0
                self.dq[owner.dsem] = q
            owner.dgen = self.gen
        ins = self.eng[q].dma_start(out=out, in_=in_, **kw)
        owner.dcnt += 16
        ins.then_inc(self.sems[owner.dsem], 16)
        ev = (owner.dsem, owner.dcnt)
        self.dtot[owner.dsem] = owner.dcnt
        self._mark(ev, reads, writes)
        if final:
            self.final[ev[0]] = max(self.final.get(ev[0], 0), ev[1])
        return ins

    def barrier(self):
        evs = dict(self.dtot)
        for e in self.eng:
            if self.cnt[e] > 0:
                evs[self.esem[e]] = self.cnt[e]
        for e in self.eng:
            for si, v in evs.items():
                self._wait(e, (si, v))
        self.gen += 1
        self.free_dsems = {}
        for si in sorted(self.dtot.keys(), reverse=True):
            self.free_dsems.setdefault(self.dq[si], []).append(si)

    def finish(self, e="sp"):
        for si, v in self.final.items():
            self._wait(e, (si, v))


def _sb(nc, es, name, shape, dt):
    return es.enter_context(nc.sbuf_tensor(name, shape, dt))


def _ps(nc, es, name, shape, dt):
    return es.enter_context(nc.psum_tensor(name, shape, dt))


def build_l1(S=SEQ):
    TT = 512
    NT = S // TT
    nc = bass.Bass("TRN2", target_bir_lowering=False)
    xT = nc.dram_tensor("xT", [D, S], F32, kind="ExternalInput").ap()
    wsel = nc.dram_tensor("wsel", [D, 784], F32, kind="ExternalInput").ap()
    wa2 = nc.dram_tensor("wa2", [16, 128], F32, kind="ExternalInput").ap()
    ba2 = nc.dram_tensor("ba2", [128, 1], F32, kind="ExternalInput").ap()
    ngb = nc.dram_tensor("ngb", [128, 256], F32, kind="ExternalInput").ap()
    cmask = nc.dram_tensor("cmask", [128, 128], F32, kind="ExternalInput").ap()
    ident = nc.dram_tensor("ident", [128, 128], F32, kind="ExternalInput").ap()
    og = nc.dram_tensor("og", [S, 256], BF16, kind="ExternalOutput").ap()

    es = contextlib.ExitStack()
    with es:
        fw = FW(nc, es)
        w_sb = _sb(nc, es, "w_sb", [128, KC, 784], BF16)
        wa2_sb = _sb(nc, es, "wa2_sb", [16, 128], F32)
        nb_sb = _sb(nc, es, "nb_sb", [128, 1], F32)
        ng_sb = _sb(nc, es, "ng_sb", [128, 256], F32)
        cm_sb = _sb(nc, es, "cm_sb", [128, 128], F32)
        id_sb = _sb(nc, es, "id_sb", [128, 128], BF16)
        rm_sb = _sb(nc, es, "rm_sb", [128, TT], F32)
        S_sb = _sb(nc, es, "S_sb", [128, 256], F32)
        Sb_sb = _sb(nc, es, "Sb_sb", [128, 256], BF16)
        xTb = [_sb(nc, es, "xTb%d" % i, [128, KC, TT], BF16) for i in range(2)]
        aT_sb = _sb(nc, es, "aT_sb", [16, TT], F32)
        e1_sb = _sb(nc, es, "e1_sb", [128, TT], F32)
        sp_sb = _sb(nc, es, "sp_sb", [128, TT], F32)
        cs_sb = _sb(nc, es, "cs_sb", [128, TT], F32)
        eq_sb = [_sb(nc, es, "eq_sb%d" % i, [128, TT], F32) for i in range(2)]
        ek_sb = _sb(nc, es, "ek_sb", [128, TT], F32)
        Qt = [_sb(nc, es, "Qt%d" % i, [128, TT], BF16) for i in range(2)]
        Kt = [_sb(nc, es, "Kt%d" % i, [128, TT], BF16) for i in range(2)]
        Kh = [_sb(nc, es, "Kh%d" % i, [128, TT], BF16) for i in range(2)]
        Khtok = [_sb(nc, es, "Khtok%d" % i, [128, 4, 128], BF16) for i in range(2)]
        v_sb = [_sb(nc, es, "v_sb%d" % i, [128, 4, 256], BF16) for i in range(2)]
        gs_sb = [_sb(nc, es, "gs_sb%d" % i, [128, 4, 256], F32) for i in range(2)]
        gn_sb = [_sb(nc, es, "gn_sb%d" % i, [128, 4, 256], F32) for i in range(2)]
        og_sb = [_sb(nc, es, "og_sb%d" % i, [128, 4, 256], BF16) for i in range(2)]
        att_sb = [_sb(nc, es, "att_sb%d" % i, [128, 128], BF16) for i in range(2)]
        sq_sb = _sb(nc, es, "sq_sb", [128, 256], F32)
        ss_sb = _sb(nc, es, "ss_sb", [128, 1], F32)
        rs_sb = _sb(nc, es, "rs_sb", [128, 1], F32)
        q_ps = _ps(nc, es, "q_ps", [128, TT], F32)
        k_ps = _ps(nc, es, "k_ps", [128, TT], F32)
        az_ps = _ps(nc, es, "az_ps", [128, TT], F32)
        vg_ps1 = _ps(nc, es, "vg_ps", [128, 512], F32)
        vg_ps = [vg_ps1, vg_ps1]
        at_ps = _ps(nc, es, "at_ps", [128, 512], F32)
        tr_ps = _ps(nc, es, "tr_ps", [128, 4, 256], BF16)
        o_ps = _ps(nc, es, "o_ps", [128, 512], F32)
        kv_ps = _ps(nc, es, "kv_ps", [128, 512], F32)

        B = {}
        for n in ["w", "wa2", "nb", "ng", "cm", "id", "rm", "S", "Sb", "aT", "e1", "sp", "cs", "ek",
                  "sq", "ss", "rs", "q_ps", "k_ps", "az_ps", "at_ps", "o_ps", "kv_ps", "og_out"]:
            B[n] = fw.buf(n)
        for n in ["xTb", "eq", "Qt", "Kt", "Kh", "Khtok", "og", "att"]:
            B[n] = fw.bufs(2, n)
        _vg = fw.buf("vg_ps")
        B["vg_ps"] = [_vg, _vg]
        for n in ["v", "gs", "gn"]:
            B[n] = [fw.bufs(4, n + str(i) + "_") for i in range(2)]
        B["tr_ps"] = fw.buf("tr_ps")

        fw.dma("pool", w_sb[:], wsel.rearrange("(kc p) n -> p kc n", p=128), writes=[B["w"]])
        fw.dma("sp", wa2_sb[:], wa2[:, :], writes=[B["wa2"]])
        fw.dma("sp", nb_sb[:], ba2[:, :], writes=[B["nb"]])
        fw.dma("sp", ng_sb[:], ngb[:, :], writes=[B["ng"]])
        fw.dma("sp", cm_sb[:], cmask[:, :], writes=[B["cm"]])
        fw.dma("pool", id_sb[:], ident[:, :], writes=[B["id"]])
        fw.op("dve", lambda e: e.tensor_scalar(out=nb_sb[:], in0=nb_sb[:], scalar1=-1.0, scalar2=None, op0=ALU.mult),
              reads=[B["nb"]], writes=[B["nb"]])
        fw.op("dve", lambda e: e.memset(rm_sb[:], 1.0), writes=[B["rm"]])
        for c in range(4):
            fw.op("dve", lambda e, c=c: e.memset(rm_sb[:, c * 128:c * 128 + 1], 0.0), writes=[B["rm"]])
        fw.op("dve", lambda e: e.memset(S_sb[:], 0.0), writes=[B["S"]])
        fw.op("dve", lambda e: e.memset(Sb_sb[:], 0.0), writes=[B["Sb"]])

        xT_v = xT.rearrange("(kc p) t -> p kc t", p=128)
        og_v = og.rearrange("(n c p) e -> n p c e", c=4, p=128)
        QSCALE = 128.0 ** -0.5

        for t in range(NT):
            p = t % 2
            t0 = t * TT
            fw.dma("pool", xTb[p][:], xT_v[:, :, t0:t0 + TT], writes=[B["xTb"][p]])
            for (ps, pb, c0, m) in ((q_ps, B["q_ps"], 0, 128), (k_ps, B["k_ps"], 128, 128), (az_ps, B["az_ps"], 768, 16)):
                for kc in range(KC):
                    fw.op("pe", lambda e, ps=ps, c0=c0, m=m, kc=kc: e.matmul(
                        ps[0:m, :], lhsT=w_sb[:, kc, c0:c0 + m], rhs=xTb[p][:, kc, :],
                        start=(kc == 0), stop=(kc == KC - 1)),
                        reads=[B["w"], B["xTb"][p]], writes=[pb], inc=(kc == KC - 1))
            fw.op("act", lambda e: e.activation(out=aT_sb[:], in_=az_ps[0:16, :], func=AF.Copy),
                  reads=[B["az_ps"]], writes=[B["aT"]])
            fw.op("pe", lambda e: e.matmul(az_ps[:], lhsT=wa2_sb[:], rhs=aT_sb[:], start=True, stop=True),
                  reads=[B["wa2"], B["aT"]], writes=[B["az_ps"]])
            fw.op("act", lambda e: e.activation(out=e1_sb[:], in_=az_ps[:], func=AF.Exp, bias=nb_sb[:], scale=-1.0),
                  reads=[B["az_ps"], B["nb"]], writes=[B["e1"]])
            fw.op("act", lambda e: e.activation(out=sp_sb[:], in_=e1_sb[:], func=AF.Ln, bias=1.0, scale=1.0),
                  reads=[B["e1"]], writes=[B["sp"]])
            fw.op("dve", lambda e: e.tensor_tensor_scan(out=cs_sb[:], data0=rm_sb[:], data1=sp_sb[:], initial=0.0,
                                                        op0=ALU.mult, op1=ALU.add),
                  reads=[B["rm"], B["sp"]], writes=[B["cs"]])
            fw.op("act", lambda e: e.activation(out=eq_sb[p][:], in_=cs_sb[:], func=AF.Exp, scale=-1.0 / 16.0),
                  reads=[B["cs"]], writes=[B["eq"][p]])
            fw.op("act", lambda e: e.activation(out=ek_sb[:], in_=cs_sb[:], func=AF.Exp, scale=1.0 / 16.0),
                  reads=[B["cs"]], writes=[B["ek"]])
            fw.op("dve", lambda e: e.scalar_tensor_tensor(out=Qt[p][:], in0=q_ps[:], scalar=QSCALE, in1=eq_sb[p][:],
                                                          op0=ALU.mult, op1=ALU.mult),
                  reads=[B["q_ps"], B["eq"][p]], writes=[B["Qt"][p]])
            fw.op("dve", lambda e: e.tensor_tensor(out=Kt[p][:], in0=k_ps[:], in1=ek_sb[:], op=ALU.mult),
                  reads=[B["k_ps"], B["ek"]], writes=[B["Kt"][p]])
            for c in range(4):
                cl = c * 128 + 127
                fw.op("dve", lambda e, c=c, cl=cl: e.scalar_tensor_tensor(
                    out=Kh[p][:, c * 128:(c + 1) * 128], in0=k_ps[:, c * 128:(c + 1) * 128],
                    scalar=eq_sb[p][:, cl:cl + 1], in1=ek_sb[:, c * 128:(c + 1) * 128],
                    op0=ALU.mult, op1=ALU.mult),
                    reads=[B["k_ps"], B["eq"][p], B["ek"]], writes=[B["Kh"][p]])
            for c in range(4):
                fw.op("pe", lambda e, c=c: e.transpose(tr_ps[:, c, 0:128], Kh[p][:, c * 128:(c + 1) * 128], id_sb[:]),
                      reads=[B["Kh"][p], B["id"]], writes=[B["tr_ps"]], inc=(c == 3))
            fw.op("act", lambda e: e.activation(out=Khtok[p][:], in_=tr_ps[:, :, 0:128], func=AF.Copy),
                  reads=[B["tr_ps"]], writes=[B["Khtok"][p]])
            for c in range(4):
                vp = vg_ps[c % 2]
                vb = B["vg_ps"][c % 2]
                for kc in range(KC):
                    fw.op("pe", lambda e, c=c, kc=kc, vp=vp: e.matmul(
                        vp[:], lhsT=xTb[p][:, kc, c * 128:(c + 1) * 128], rhs=w_sb[:, kc, 256:768],
                        start=(kc == 0), stop=(kc == KC - 1)),
                        reads=[B["w"], B["xTb"][p]], writes=[vb], inc=(kc == KC - 1))
                fw.op("act", lambda e, c=c, vp=vp: e.activation(out=v_sb[p][:, c, :], in_=vp[:, 0:256], func=AF.Copy),
                      reads=[vb], writes=[B["v"][p][c]])
                fw.op("act", lambda e, c=c, vp=vp: e.activation(out=gs_sb[p][:, c, :], in_=vp[:, 256:512], func=AF.Silu),
                      reads=[vb], writes=[B["gs"][p][c]])
                fw.op("pool", lambda e, c=c: e.tensor_tensor(out=gn_sb[p][:, c, :], in0=gs_sb[p][:, c, :], in1=ng_sb[:],
                                                             op=ALU.mult),
                      reads=[B["gs"][p][c], B["ng"]], writes=[B["gn"][p][c]])
            for c in range(4):
                sl = slice(c * 128, (c + 1) * 128)
                cl = c * 128 + 127
                a = c % 2
                fw.op("pe", lambda e, sl=sl: e.matmul(at_ps[:, 0:128], lhsT=Kt[p][:, sl], rhs=Qt[p][:, sl],
                                                      start=True, stop=True),
                      reads=[B["Kt"][p], B["Qt"][p]], writes=[B["at_ps"]])
                fw.op("dve", lambda e, a=a: e.tensor_tensor(out=att_sb[a][:], in0=at_ps[:, 0:128], in1=cm_sb[:], op=ALU.mult),
                      reads=[B["at_ps"], B["cm"]], writes=[B["att"][a]])
                fw.op("pe", lambda e, a=a, c=c: e.matmul(o_ps[:, 0:256], lhsT=att_sb[a][:], rhs=v_sb[p][:, c, :],
                                                         start=True, stop=False),
                      reads=[B["att"][a], B["v"][p][c]], writes=[B["o_ps"]], inc=False)
                fw.op("pe", lambda e, sl=sl: e.matmul(o_ps[:, 0:256], lhsT=Qt[p][:, sl], rhs=Sb_sb[:],
                                                      start=False, stop=True),
                      reads=[B["Qt"][p], B["Sb"]], writes=[B["o_ps"]])
                fw.op("pe", lambda e, c=c: e.matmul(kv_ps[:, 0:256], lhsT=Khtok[p][:, c, :], rhs=v_sb[p][:, c, :],
                                                    start=True, stop=True),
                      reads=[B["Khtok"][p], B["v"][p][c]], writes=[B["kv_ps"]])
                fw.op("dve", lambda e, cl=cl: e.scalar_tensor_tensor(out=S_sb[:], in0=S_sb[:], scalar=eq_sb[p][:, cl:cl + 1],
                                                                     in1=kv_ps[:, 0:256], op0=ALU.mult, op1=ALU.add),
                      reads=[B["S"], B["eq"][p], B["kv_ps"]], writes=[B["S"]])
                fw.op("act", lambda e: e.activation(out=Sb_sb[:], in_=S_sb[:], func=AF.Copy),
                      reads=[B["S"]], writes=[B["Sb"]])
                fw.op("act", lambda e: e.activation(out=sq_sb[:], in_=o_ps[:, 0:256], func=AF.Square),
                      reads=[B["o_ps"]], writes=[B["sq"]])
                fw.op("dve", lambda e: e.reduce_sum(out=ss_sb[:], in_=sq_sb[:], axis=AX.X),
                      reads=[B["sq"]], writes=[B["ss"]])
                fw.op("act", lambda e: e.activation(out=ss_sb[:], in_=ss_sb[:], func=AF.Ln, bias=RMS_EPS, scale=1.0 / 256.0),
                      reads=[B["ss"]], writes=[B["ss"]])
                fw.op("act", lambda e: e.activation(out=rs_sb[:], in_=ss_sb[:], func=AF.Exp, scale=-0.5),
                      reads=[B["ss"]], writes=[B["rs"]])
                fw.op("dve", lambda e, c=c: e.scalar_tensor_tensor(out=og_sb[p][:, c, :], in0=o_ps[:, 0:256], scalar=rs_sb[:, 0:1],
                                                                   in1=gn_sb[p][:, c, :], op0=ALU.mult, op1=ALU.mult),
                      reads=[B["o_ps"], B["rs"], B["gn"][p][c]], writes=[B["og"][p]])
            fw.dma("sp", og_v[t], og_sb[p][:], reads=[B["og"][p]], owner=B["og"][p], final=True)
        fw.finish()
    return nc


def _consts():
    j = np.arange(128)[:, None]
    i = np.arange(128)[None, :]
    cmask = (i >= j).astype(np.float32)
    ident = np.eye(128, dtype=np.float32)
    return cmask, ident


def run_l1(x, gla_w_in, gla_w_a2, gla_b_a2, gla_norm_g, S=SEQ, trace=False):
    nc = build_l1(S)
    cmask, ident = _consts()
    in_maps = []
    w = gla_w_in[0]
    for core in range(8):
        b, h = core // 4, core % 4
        wsel = np.concatenate([w[:, h * 128:(h + 1) * 128], w[:, 512 + h * 128:512 + (h + 1) * 128],
                               w[:, 1024 + h * 256:1024 + (h + 1) * 256], w[:, 2048 + h * 256:2048 + (h + 1) * 256],
                               w[:, 3072:3088]], axis=1)
        in_maps.append({
            "xT": np.ascontiguousarray(x[b, :S].T),
            "wsel": np.ascontiguousarray(wsel),
            "wa2": np.ascontiguousarray(gla_w_a2[0][:, h * 128:(h + 1) * 128]),
            "ba2": np.ascontiguousarray(gla_b_a2[0][h * 128:(h + 1) * 128].reshape(128, 1)),
            "ngb": np.ascontiguousarray(np.broadcast_to(gla_norm_g[0][h * 256:(h + 1) * 256][None, :], (128, 256))),
            "cmask": cmask, "ident": ident,
        })
    res = run_bass_kernel_spmd(nc, in_maps, core_ids=list(range(8)), trace=trace)
    og = np.zeros((BATCH, S, D), dtype=ml_dtypes.bfloat16)
    for core in range(8):
        b, h = core // 4, core % 4
        og[b, :, h * 256:(h + 1) * 256] = res.results[core]["og"]
    return og, res


CH = 2048


def build_pl(kind):
    TT = 512
    NT = CH // TT
    nc = bass.Bass("TRN2", target_bir_lowering=False)
    w = nc.dram_tensor("w", [D, D], F32, kind="ExternalInput").ap()
    lnp = nc.dram_tensor("lnp", [128, 2, KC], F32, kind="ExternalInput").ap()
    resT = nc.dram_tensor("resT", [D, CH], F32, kind="ExternalInput").ap()
    outT = nc.dram_tensor("outT", [D, CH], F32, kind="ExternalOutput").ap()
    if kind == "l2":
        aT = nc.dram_tensor("aT", [D, CH], BF16, kind="ExternalInput").ap()
    else:
        numT = nc.dram_tensor("numT", [3, D, CH], F32, kind="ExternalInput").ap()
        denT = nc.dram_tensor("denT", [3, D, CH], F32, kind="ExternalInput").ap()
        wg = nc.dram_tensor("wg", [D, D], F32, kind="ExternalInput").ap()
    es = contextlib.ExitStack()
    with es:
        fw = FW(nc, es)
        w_sb = _sb(nc, es, "w_sb", [128, KC, D], BF16)
        ln_sb = _sb(nc, es, "ln_sb", [128, 2, KC], F32)
        ones_sb = _sb(nc, es, "ones_sb", [128, 128], BF16)
        a_sb = [_sb(nc, es, "a_sb%d" % i, [128, KC, TT], BF16) for i in range(2)]
        res_sb = [_sb(nc, es, "res_sb%d" % i, [128, KC, TT], F32) for i in range(2)]
        r_sb = _sb(nc, es, "r_sb", [128, KC, TT], F32)
        rb_sb = _sb(nc, es, "rb_sb", [128, KC, TT], BF16)
        rq_sb = _sb(nc, es, "rq_sb", [128, KC, TT], BF16)
        o_sb = [_sb(nc, es, "o_sb%d" % i, [128, KC, TT], F32) for i in range(2)]
        mean_sb = _sb(nc, es, "mean_sb", [128, TT], F32)
        msq_sb = _sb(nc, es, "msq_sb", [128, TT], F32)
        var_sb = _sb(nc, es, "var_sb", [128, TT], F32)
        rstd_sb = _sb(nc, es, "rstd_sb", [128, TT], F32)
        nmr_sb = _sb(nc, es, "nmr_sb", [128, TT], F32)
        t_sb = [_sb(nc, es, "t_sb%d" % i, [128, TT], F32) for i in range(2)]
        y_ps = [_ps(nc, es, "y_ps%d" % i, [128, TT], F32) for i in range(2)]
        sum_ps = _ps(nc, es, "sum_ps", [128, TT], F32)
        sq_ps = _ps(nc, es, "sq_ps", [128, TT], F32)
        B = {}
        for n in ["w", "ln", "ones", "r", "rb", "rq", "mean", "msq", "var", "rstd", "nmr", "sum_ps", "sq_ps", "wg"]:
            B[n] = fw.buf(n)
        for n in ["a", "res", "o", "t", "y_ps", "xb", "g_ps"]:
            B[n] = fw.bufs(2, n)
        for n in ["rr", "rbb", "rqq"]:
            B[n] = fw.bufs(KC, n)
        if kind == "l4":
            wg_sb = _sb(nc, es, "wg_sb", [128, KC, D], BF16)
            xb_sb = [_sb(nc, es, "xb_sb%d" % i, [128, KC, TT], BF16) for i in range(2)]
            g_ps = [_ps(nc, es, "g_ps%d" % i, [128, TT], F32) for i in range(2)]
            n_sb = [[_sb(nc, es, "n_sb%d_%d" % (i, g), [128, TT], F32) for g in range(3)] for i in range(2)]
            d_sb = [[_sb(nc, es, "d_sb%d_%d" % (i, g), [128, TT], F32) for g in range(3)] for i in range(2)]
            sg_sb = [_sb(nc, es, "sg_sb%d" % i, [128, TT], F32) for i in range(2)]
            B["n"] = [fw.bufs(3, "n%d_" % i) for i in range(2)]
            B["d"] = [fw.bufs(3, "d%d_" % i) for i in range(2)]
            B["sg"] = fw.bufs(2, "sg")
            fw.dma("pool", wg_sb[:], wg.rearrange("(kc p) n -> p kc n", p=128), writes=[B["wg"]])
            num_v = numT.rearrange("g (h p) t -> g p h t", p=128)
            den_v = denT.rearrange("g (h p) t -> g p h t", p=128)
        fw.dma("pool", w_sb[:], w.rearrange("(kc p) n -> p kc n", p=128), writes=[B["w"]])
        fw.dma("sp", ln_sb[:], lnp[:, :, :], writes=[B["ln"]])
        fw.op("dve", lambda e: e.memset(ones_sb[:], 1.0), writes=[B["ones"]])
        res_v = resT.rearrange("(kc p) t -> p kc t", p=128)
        out_v = outT.rearrange("(kc p) t -> p kc t", p=128)
        if kind == "l2":
            a_v = aT.rearrange("(kc p) t -> p kc t", p=128)
        hcount = 0
        for t in range(NT):
            p = t % 2
            t0 = t * TT
            fw.dma("sp", res_sb[p][:], res_v[:, :, t0:t0 + TT], writes=[B["res"][p]])
            if kind == "l2":
                fw.dma("sp", a_sb[p][:], a_v[:, :, t0:t0 + TT], writes=[B["a"][p]])
            else:
                fw.dma("pool", xb_sb[p][:], res_v[:, :, t0:t0 + TT], writes=[B["xb"][p]])
                for h in range(8):
                    q = hcount % 2
                    hcount += 1
                    for g in range(3):
                        fw.dma("sp", n_sb[q][g][:], num_v[g, :, h, t0:t0 + TT], writes=[B["n"][q][g]])
                        fw.dma("sp", d_sb[q][g][:], den_v[g, :, h, t0:t0 + TT], writes=[B["d"][q][g]])
                    for kc in range(KC):
                        fw.op("pe", lambda e, kc=kc, h=h, q=q: e.matmul(
                            g_ps[q][:], lhsT=wg_sb[:, kc, h * 128:(h + 1) * 128], rhs=xb_sb[p][:, kc, :],
                            start=(kc == 0), stop=(kc == KC - 1)),
                            reads=[B["wg"], B["xb"][p]], writes=[B["g_ps"][q]], inc=(kc == KC - 1))
                    fw.op("act", lambda e, q=q: e.activation(out=sg_sb[q][:], in_=g_ps[q][:], func=AF.Silu),
                          reads=[B["g_ps"][q]], writes=[B["sg"][q]])
                    fw.op("pool", lambda e, q=q: e.tensor_tensor(out=n_sb[q][0][:], in0=n_sb[q][0][:], in1=n_sb[q][1][:], op=ALU.add),
                          reads=[B["n"][q][0], B["n"][q][1]], writes=[B["n"][q][0]])
                    fw.op("pool", lambda e, q=q: e.tensor_tensor(out=n_sb[q][0][:], in0=n_sb[q][0][:], in1=n_sb[q][2][:], op=ALU.add),
                          reads=[B["n"][q][0], B["n"][q][2]], writes=[B["n"][q][0]])
                    fw.op("pool", lambda e, q=q: e.tensor_tensor(out=d_sb[q][0][:], in0=d_sb[q][0][:], in1=d_sb[q][1][:], op=ALU.add),
                          reads=[B["d"][q][0], B["d"][q][1]], writes=[B["d"][q][0]])
                    fw.op("pool", lambda e, q=q: e.tensor_tensor(out=d_sb[q][0][:], in0=d_sb[q][0][:], in1=d_sb[q][2][:], op=ALU.add),
                          reads=[B["d"][q][0], B["d"][q][2]], writes=[B["d"][q][0]])
                    fw.op("dve", lambda e, q=q: e.reciprocal(out=d_sb[q][1][:], in_=d_sb[q][0][:]),
                          reads=[B["d"][q][0]], writes=[B["d"][q][1]])
                    fw.op("dve", lambda e, q=q: e.tensor_tensor(out=n_sb[q][1][:], in0=n_sb[q][0][:], in1=d_sb[q][1][:], op=ALU.mult),
                          reads=[B["n"][q][0], B["d"][q][1]], writes=[B["n"][q][1]])
                    fw.op("dve", lambda e, q=q, h=h: e.tensor_tensor(out=a_sb[p][:, h, :], in0=n_sb[q][1][:], in1=sg_sb[q][:], op=ALU.mult),
                          reads=[B["n"][q][1], B["sg"][q]], writes=[B["a"][p]])
            for fc in range(KC):
                yq = fc % 2
                for kc in range(KC):
                    fw.op("pe", lambda e, kc=kc, fc=fc, yq=yq: e.matmul(
                        y_ps[yq][:], lhsT=w_sb[:, kc, fc * 128:(fc + 1) * 128], rhs=a_sb[p][:, kc, :],
                        start=(kc == 0), stop=(kc == KC - 1)),
                        reads=[B["w"], B["a"][p]], writes=[B["y_ps"][yq]], inc=(kc == KC - 1))
                fw.op("dve", lambda e, fc=fc, yq=yq: e.scalar_tensor_tensor(
                    out=r_sb[:, fc, :], in0=res_sb[p][:, fc, :], scalar=float(ALPHA), in1=y_ps[yq][:], op0=ALU.mult, op1=ALU.add),
                    reads=[B["res"][p], B["y_ps"][yq]], writes=[B["rr"][fc]])
                fw.op("act", lambda e, fc=fc: e.activation(out=rb_sb[:, fc, :], in_=r_sb[:, fc, :], func=AF.Copy),
                      reads=[B["rr"][fc]], writes=[B["rbb"][fc]])
                fw.op("act", lambda e, fc=fc: e.activation(out=rq_sb[:, fc, :], in_=r_sb[:, fc, :], func=AF.Square),
                      reads=[B["rr"][fc]], writes=[B["rqq"][fc]])
            for fc in range(KC):
                fw.op("pe", lambda e, fc=fc: e.matmul(sum_ps[:], lhsT=ones_sb[:], rhs=rb_sb[:, fc, :],
                                                      start=(fc == 0), stop=(fc == KC - 1)),
                      reads=[B["ones"], B["rbb"][fc]], writes=[B["sum_ps"]], inc=(fc == KC - 1))
            for fc in range(KC):
                fw.op("pe", lambda e, fc=fc: e.matmul(sq_ps[:], lhsT=ones_sb[:], rhs=rq_sb[:, fc, :],
                                                      start=(fc == 0), stop=(fc == KC - 1)),
                      reads=[B["ones"], B["rqq"][fc]], writes=[B["sq_ps"]], inc=(fc == KC - 1))
            fw.op("dve", lambda e: e.tensor_scalar(out=mean_sb[:], in0=sum_ps[:], scalar1=1.0 / D, scalar2=None, op0=ALU.mult),
                  reads=[B["sum_ps"]], writes=[B["mean"]])
            fw.op("dve", lambda e: e.tensor_tensor(out=msq_sb[:], in0=mean_sb[:], in1=mean_sb[:], op=ALU.mult),
                  reads=[B["mean"]], writes=[B["msq"]])
            fw.op("dve", lambda e: e.scalar_tensor_tensor(out=var_sb[:], in0=sq_ps[:], scalar=1.0 / D, in1=msq_sb[:],
                                                          op0=ALU.mult, op1=ALU.subtract),
                  reads=[B["sq_ps"], B["msq"]], writes=[B["var"]])
            fw.op("act", lambda e: e.activation(out=var_sb[:], in_=var_sb[:], func=AF.Ln, bias=LN_EPS, scale=1.0),
                  reads=[B["var"]], writes=[B["var"]])
            fw.op("act", lambda e: e.activation(out=rstd_sb[:], in_=var_sb[:], func=AF.Exp, scale=-0.5),
                  reads=[B["var"]], writes=[B["rstd"]])
            fw.op("dve", lambda e: e.tensor_tensor(out=nmr_sb[:], in0=mean_sb[:], in1=rstd_sb[:], op=ALU.mult),
                  reads=[B["mean"], B["rstd"]], writes=[B["nmr"]])
            for fc in range(KC):
                tq = fc % 2
                fw.op("dve", lambda e, fc=fc, tq=tq: e.tensor_tensor(out=t_sb[tq][:], in0=r_sb[:, fc, :], in1=rstd_sb[:], op=ALU.mult),
                      reads=[B["rr"][fc], B["rstd"]], writes=[B["t"][tq]])
                fw.op("dve", lambda e, tq=tq: e.tensor_tensor(out=t_sb[tq][:], in0=t_sb[tq][:], in1=nmr_sb[:], op=ALU.subtract),
                      reads=[B["t"][tq], B["nmr"]], writes=[B["t"][tq]])
                fw.op("act", lambda e, fc=fc, tq=tq: e.activation(out=o_sb[p][:, fc, :], in_=t_sb[tq][:], func=AF.Identity,
                                                                  bias=ln_sb[:, 1, fc:fc + 1], scale=ln_sb[:, 0, fc:fc + 1]),
                      reads=[B["t"][tq], B["ln"]], writes=[B["o"][p]])
            fw.dma("sp", out_v[:, :, t0:t0 + TT], o_sb[p][:], reads=[B["o"][p]], owner=B["o"][p], final=True)
        fw.finish()
    return nc


_VERBOSE = False


def _run(nc, in_maps):
    if _VERBOSE:
        import time
        t0 = time.time()
        print("launch: input MB", sum(v.nbytes for m in in_maps for v in m.values()) / 1e6, flush=True)
    r = run_bass_kernel_spmd(nc, in_maps, core_ids=list(range(8))).results
    if _VERBOSE:
        print("launch done", time.time() - t0, flush=True)
    return r


def _ln_layout(g, b):
    return np.ascontiguousarray(np.stack([g.reshape(KC, 128).T, b.reshape(KC, 128).T], axis=1)).astype(np.float32)


SWA_GROUPS = ((128, 1), (512, 4), (2048, 16))
QS = 128.0 ** -0.5


def build_l3():
    nc = bass.Bass("TRN2", target_bir_lowering=False)
    G = []
    for g, (_, d) in enumerate(SWA_GROUPS):
        NP = (d + 16) * 128
        G.append(dict(
            d=d, NP=NP, HL=d * 128,
            x=nc.dram_tensor("xg%d" % g, [D, NP], F32, kind="ExternalInput").ap(),
            wq=nc.dram_tensor("wq%d" % g, [D, 1024], F32, kind="ExternalInput").ap(),
            wkv=nc.dram_tensor("wkv%d" % g, [D, 512], F32, kind="ExternalInput").ap(),
            cos=nc.dram_tensor("cos%d" % g, [128, NP], F32, kind="ExternalInput").ap(),
            sin=nc.dram_tensor("sin%d" % g, [128, NP], F32, kind="ExternalInput").ap(),
            num=nc.dram_tensor("numT%d" % g, [D, CH], F32, kind="ExternalOutput").ap(),
            den=nc.dram_tensor("denT%d" % g, [D, CH], F32, kind="ExternalOutput").ap(),
        ))
    mcur = nc.dram_tensor("mcur", [128, 512], F32, kind="ExternalInput").ap()
    mprev = nc.dram_tensor("mprev", [128, 512], F32, kind="ExternalInput").ap()
    mprevh = nc.dram_tensor("mprevh", [128, 512], F32, kind="ExternalInput").ap()
    es = contextlib.ExitStack()
    with es:
        fw = FW(nc, es)
        wq_sb = _sb(nc, es, "wq_sb", [128, KC, 1024], BF16)
        wkv_sb = _sb(nc, es, "wkv_sb", [128, KC, 512], BF16)
        xt = [_sb(nc, es, "xt%d" % i, [128, KC, 512], BF16) for i in range(2)]
        cs_sb = [_sb(nc, es, "cs_sb%d" % i, [128, 512], F32) for i in range(2)]
        sn_sb = [_sb(nc, es, "sn_sb%d" % i, [128, 512], F32) for i in range(2)]
        Kt_sb = _sb(nc, es, "Kt_sb", [128, 2, 4096], BF16)
        V_sb = _sb(nc, es, "V_sb", [128, 32, 256], BF16)
        Qt_sb = _sb(nc, es, "Qt_sb", [128, 8, CH], BF16)
        t1_sb = [_sb(nc, es, "t1_sb%d" % i, [128, 512], F32) for i in range(2)]
        t2_sb = [_sb(nc, es, "t2_sb%d" % i, [128, 512], F32) for i in range(2)]
        pe_sb = [[_sb(nc, es, "pe_sb%d_%d" % (k, i), [128, 4, 128], BF16) for i in range(2)] for k in range(2)]
        pm_sb = [[_sb(nc, es, "pm_sb%d_%d" % (k, i), [128, 4, 128], BF16) for i in range(2)] for k in range(2)]
        num_sb = [_sb(nc, es, "num_sb%d" % i, [128, 8, 128], F32) for i in range(2)]
        den_sb = [_sb(nc, es, "den_sb%d" % i, [128, 8, 128], F32) for i in range(2)]
        m_sb = [_sb(nc, es, "m_sb%d" % i, [128, 4, 128], BF16) for i in range(3)]
        ones_sb = _sb(nc, es, "ones_sb", [128, 128], BF16)
        pj_ps = [_ps(nc, es, "pj_ps%d" % i, [128, 512], F32) for i in range(2)]
        v_ps = _ps(nc, es, "v_ps", [128, 512], F32)
        s_ps = [_ps(nc, es, "s_ps%d" % i, [128, 4, 128], F32) for i in range(2)]
        num_ps = _ps(nc, es, "num_ps", [128, 4, 128], F32)
        den_ps = _ps(nc, es, "den_ps", [128, 4, 128], F32)
        B = {}
        for n in ["wq", "wkv", "Kt", "V", "Qt", "ones", "v_ps", "num_ps", "den_ps"]:
            B[n] = fw.buf(n)
        for n in ["xt", "cs", "sn", "t1", "t2", "pj_ps", "s_ps", "num", "den"]:
            B[n] = fw.bufs(2, n)
        B["m"] = fw.bufs(3, "m")
        B["pe"] = [fw.bufs(2, "pe%d_" % k) for k in range(2)]
        B["pm"] = [fw.bufs(2, "pm%d_" % k) for k in range(2)]
        for i, mm in enumerate((mcur, mprev, mprevh)):
            fw.dma("pool", m_sb[i][:], mm.rearrange("p (h q) -> p h q", h=4), writes=[B["m"][i]])
        fw.op("dve", lambda e: e.memset(ones_sb[:], 1.0), writes=[B["ones"]])
        pjc = [0]

        def rope(ps, wd, xp, out_ap, out_buf):
            a = pjc[0] % 2
            fw.op("dve", lambda e: e.tensor_tensor(out=t1_sb[a][:, 0:wd], in0=ps[:, 0:wd], in1=cs_sb[xp][:, 0:wd], op=ALU.mult),
                  reads=[B["pj_ps"][a], B["cs"][xp]], writes=[B["t1"][a]])
            fw.op("dve", lambda e: e.tensor_tensor(out=t2_sb[a][0:64, 0:wd], in0=ps[64:128, 0:wd], in1=sn_sb[xp][64:128, 0:wd], op=ALU.mult),
                  reads=[B["pj_ps"][a], B["sn"][xp]], writes=[B["t2"][a]])
            fw.op("dve", lambda e: e.tensor_tensor(out=t2_sb[a][64:128, 0:wd], in0=ps[0:64, 0:wd], in1=sn_sb[xp][0:64, 0:wd], op=ALU.mult),
                  reads=[B["pj_ps"][a], B["sn"][xp]], writes=[B["t2"][a]])
            fw.op("pool", lambda e: e.tensor_tensor(out=out_ap, in0=t1_sb[a][:, 0:wd], in1=t2_sb[a][:, 0:wd], op=ALU.add),
                  reads=[B["t1"][a], B["t2"][a]], writes=[out_buf])

        def proj(w_sb_, wb, c0, xp, wd):
            a = pjc[0] % 2
            for kc in range(KC):
                fw.op("pe", lambda e, kc=kc: e.matmul(pj_ps[a][:, 0:wd], lhsT=w_sb_[:, kc, c0:c0 + 128], rhs=xt[xp][:, kc, 0:wd],
                                                      start=(kc == 0), stop=(kc == KC - 1)),
                      reads=[wb, B["xt"][xp]], writes=[B["pj_ps"][a]], inc=(kc == KC - 1))
            return pj_ps[a]

        tcount = 0
        jcount = 0
        acount = 0
        for g, gi in enumerate(G):
            d, NP, HL = gi["d"], gi["NP"], gi["HL"]
            fw.dma("pool", wq_sb[:], gi["wq"].rearrange("(kc p) n -> p kc n", p=128), writes=[B["wq"]])
            fw.dma("pool", wkv_sb[:], gi["wkv"].rearrange("(kc p) n -> p kc n", p=128), writes=[B["wkv"]])
            x_v = gi["x"].rearrange("(kc p) n -> p kc n", p=128)
            if d == 1:
                tiles = [(0, 128)]
            else:
                tiles = [(i * 512, 512) for i in range(HL // 512)]
            tiles += [(HL + i * 512, 512) for i in range(4)]
            for (p0, wd) in tiles:
                xp = tcount % 2
                tcount += 1
                fw.dma("pool", xt[xp][:, :, 0:wd], x_v[:, :, p0:p0 + wd], writes=[B["xt"][xp]])
                fw.dma("sp", cs_sb[xp][:, 0:wd], gi["cos"][:, p0:p0 + wd], writes=[B["cs"][xp]])
                fw.dma("sp", sn_sb[xp][:, 0:wd], gi["sin"][:, p0:p0 + wd], writes=[B["sn"][xp]])
                for hk in range(2):
                    ps = proj(wkv_sb, B["wkv"], hk * 128, xp, wd)
                    rope(ps, wd, xp, Kt_sb[:, hk, p0:p0 + wd], B["Kt"])
                    pjc[0] += 1
                for blk in range(wd // 128):
                    for kc in range(KC):
                        fw.op("pe", lambda e, kc=kc, blk=blk: e.matmul(
                            v_ps[:, 0:256], lhsT=xt[xp][:, kc, blk * 128:(blk + 1) * 128], rhs=wkv_sb[:, kc, 256:512],
                            start=(kc == 0), stop=(kc == KC - 1)),
                            reads=[B["wkv"], B["xt"][xp]], writes=[B["v_ps"]], inc=(kc == KC - 1))
                    fw.op("act", lambda e, blk=blk: e.activation(out=V_sb[:, p0 // 128 + blk, :], in_=v_ps[:, 0:256], func=AF.Copy),
                          reads=[B["v_ps"]], writes=[B["V"]])
                if p0 >= HL:
                    q0 = p0 - HL
                    for h in range(8):
                        ps = proj(wq_sb, B["wq"], h * 128, xp, wd)
                        rope(ps, wd, xp, Qt_sb[:, h, q0:q0 + wd], B["Qt"])
                        pjc[0] += 1
            nbr = 16 // d
            num_v = gi["num"].rearrange("(h p) n -> p h n", p=128)
            den_v = gi["den"].rearrange("(h p) n -> p h n", p=128)
            for j in range(16):
                jb = jcount % 2
                jcount += 1
                if j % nbr == 0:
                    prevpos, pm_i = (j // nbr) * 128, 2
                else:
                    prevpos, pm_i = HL + (j - 1) * 128, 1
                curpos = HL + j * 128
                for hk in range(2):
                    a = acount % 2
                    acount += 1
                    for kb, (kpos, mi) in enumerate(((prevpos, pm_i), (curpos, 0))):
                        fw.op("pe", lambda e, kb=kb, kpos=kpos: e.matmul(
                            s_ps[kb][:], lhsT=Kt_sb[:, hk, kpos:kpos + 128], rhs=Qt_sb[:, 4 * hk:4 * hk + 4, j * 128:(j + 1) * 128],
                            start=True, stop=True),
                            reads=[B["Kt"], B["Qt"]], writes=[B["s_ps"][kb]])
                        fw.op("act", lambda e, kb=kb: e.activation(out=pe_sb[kb][a][:], in_=s_ps[kb][:], func=AF.Exp, scale=QS),
                              reads=[B["s_ps"][kb]], writes=[B["pe"][kb][a]])
                        fw.op("pool" if kb == 0 else "dve", lambda e, kb=kb, mi=mi: e.tensor_tensor(
                            out=pm_sb[kb][a][:], in0=pe_sb[kb][a][:], in1=m_sb[mi][:], op=ALU.mult),
                            reads=[B["pe"][kb][a], B["m"][mi]], writes=[B["pm"][kb][a]])
                    for kb, kpos in enumerate((prevpos, curpos)):
                        fw.op("pe", lambda e, kb=kb, kpos=kpos: e.matmul(
                            num_ps[:], lhsT=V_sb[:, kpos // 128, hk * 128:(hk + 1) * 128], rhs=pm_sb[kb][a][:],
                            start=(kb == 0), stop=(kb == 1)),
                            reads=[B["V"], B["pm"][kb][a]], writes=[B["num_ps"]], inc=(kb == 1))
                    for kb in range(2):
                        fw.op("pe", lambda e, kb=kb: e.matmul(den_ps[:], lhsT=ones_sb[:], rhs=pm_sb[kb][a][:],
                                                              start=(kb == 0), stop=(kb == 1)),
                              reads=[B["ones"], B["pm"][kb][a]], writes=[B["den_ps"]], inc=(kb == 1))
                    fw.op("act", lambda e: e.activation(out=num_sb[jb][:, 4 * hk:4 * hk + 4, :], in_=num_ps[:], func=AF.Copy),
                          reads=[B["num_ps"]], writes=[B["num"][jb]])
                    fw.op("dve", lambda e: e.tensor_copy(out=den_sb[jb][:, 4 * hk:4 * hk + 4, :], in_=den_ps[:]),
                          reads=[B["den_ps"]], writes=[B["den"][jb]])
                fw.dma("sp", num_v[:, :, j * 128:(j + 1) * 128], num_sb[jb][:], reads=[B["num"][jb]], owner=B["num"][jb], final=True)
                fw.dma("sp", den_v[:, :, j * 128:(j + 1) * 128], den_sb[jb][:], reads=[B["den"][jb]], owner=B["den"][jb], final=True)
        fw.finish()
    return nc


def _l3_perm(c):
    out = []
    for (_, d) in SWA_GROUPS:
        n = CH // d
        idx = []
        for r in range(d):
            i = np.arange(n - 128, n)
            t = (c - 1) * CH + r + d * i
            idx.append(t if c > 0 else np.full(128, -1))
        for r in range(d):
            i = np.arange(n)
            idx.append(c * CH + r + d * i)
        out.append(np.concatenate(idx))
    return out


def _rope_tables(pos):
    half = 64
    inv = (10000.0 ** (-(np.arange(half, dtype=np.float32) * 2.0) / 128.0)).astype(np.float32)
    ang = pos.astype(np.float32)[None, :] * inv[:, None]
    cos = np.cos(ang).astype(np.float32)
    sin = np.sin(ang).astype(np.float32)
    cosT = np.concatenate([cos, cos], axis=0)
    sinT = np.concatenate([sin, -sin], axis=0)
    return np.ascontiguousarray(cosT), np.ascontiguousarray(sinT)


def _l3_masks(c):
    k = np.arange(128)[:, None]
    q = np.arange(128)[None, :]
    mcur = np.tile((k <= q).astype(np.float32), (1, 4))
    mprev = np.tile((k >= q).astype(np.float32), (1, 4))
    mprevh = mprev if c > 0 else np.zeros_like(mprev)
    return mcur, mprev, np.ascontiguousarray(mprevh)


def _l3_inputs(x1T_b, c, swa_w_in, w_kv):
    im = {}
    perms = _l3_perm(c)
    for g, idx in enumerate(perms):
        xg = x1T_b[:, np.maximum(idx, 0)]
        xg[:, idx < 0] = 0.0
        im["xg%d" % g] = np.ascontiguousarray(xg)
        im["wq%d" % g] = np.ascontiguousarray(swa_w_in[:, g * 1024:(g + 1) * 1024])
        im["wkv%d" % g] = np.ascontiguousarray(np.concatenate(
            [w_kv[:, g * 256:(g + 1) * 256], w_kv[:, 768 + g * 256:768 + (g + 1) * 256]], axis=1))
        cosT, sinT = _rope_tables(idx)
        im["cos%d" % g] = cosT
        im["sin%d" % g] = sinT
    im["mcur"], im["mprev"], im["mprevh"] = _l3_masks(c)
    return im, perms


_NC_CACHE = {}


def _get_nc(name):
    if name not in _NC_CACHE:
        if name == "l1":
            _NC_CACHE[name] = build_l1(SEQ)
        elif name == "l3":
            _NC_CACHE[name] = build_l3()
        else:
            _NC_CACHE[name] = build_pl(name)
    return _NC_CACHE[name]


def kernel_unfused(x, gla_w_in, gla_w_a2, gla_b_a2, gla_norm_g, gla_w_out, w_kv, swa_w_in, swa_w_out, ln_g, ln_b):
    f = lambda a: np.asarray(a, dtype=np.float32)
    x, gla_w_in, gla_w_a2, gla_b_a2, gla_norm_g, gla_w_out = map(f, (x, gla_w_in, gla_w_a2, gla_b_a2, gla_norm_g, gla_w_out))
    w_kv, swa_w_in, swa_w_out, ln_g, ln_b = map(f, (w_kv, swa_w_in, swa_w_out, ln_g, ln_b))
    cmask, ident = _consts()
    w = gla_w_in[0]
    xT = [np.ascontiguousarray(x[b].T) for b in range(BATCH)]
    in_maps = []
    for core in range(8):
        b, h = core // 4, core % 4
        wsel = np.concatenate([w[:, h * 128:(h + 1) * 128], w[:, 512 + h * 128:512 + (h + 1) * 128],
                               w[:, 1024 + h * 256:1024 + (h + 1) * 256], w[:, 2048 + h * 256:2048 + (h + 1) * 256],
                               w[:, 3072:3088]], axis=1)
        in_maps.append({
            "xT": xT[b], "wsel": np.ascontiguousarray(wsel),
            "wa2": np.ascontiguousarray(gla_w_a2[0][:, h * 128:(h + 1) * 128]),
            "ba2": np.ascontiguousarray(gla_b_a2[0][h * 128:(h + 1) * 128].reshape(128, 1)),
            "ngb": np.ascontiguousarray(np.broadcast_to(gla_norm_g[0][h * 256:(h + 1) * 256][None, :], (128, 256))),
            "cmask": cmask, "ident": ident,
        })
    r1 = _run(_get_nc("l1"), in_maps)
    ogT = np.zeros((BATCH, D, SEQ), dtype=ml_dtypes.bfloat16)
    for core in range(8):
        b, h = core // 4, core % 4
        ogT[b, h * 256:(h + 1) * 256, :] = r1[core]["og"].T
    lnp0 = _ln_layout(ln_g[0], ln_b[0])
    lnp1 = _ln_layout(ln_g[1], ln_b[1])
    in_maps = []
    for core in range(8):
        b, c = core // 4, core % 4
        sl = slice(c * CH, (c + 1) * CH)
        in_maps.append({"w": gla_w_out[0], "lnp": lnp0, "resT": np.ascontiguousarray(xT[b][:, sl]),
                        "aT": np.ascontiguousarray(ogT[b][:, sl])})
    r2 = _run(_get_nc("l2"), in_maps)
    x1T = np.zeros((BATCH, D, SEQ), dtype=np.float32)
    for core in range(8):
        b, c = core // 4, core % 4
        x1T[b][:, c * CH:(c + 1) * CH] = r2[core]["outT"]
    in_maps, perms_all = [], []
    for core in range(8):
        b, c = core // 4, core % 4
        im, perms = _l3_inputs(x1T[b], c, swa_w_in[0], w_kv)
        in_maps.append(im)
        perms_all.append(perms)
    r3 = _run(_get_nc("l3"), in_maps)
    in_maps = []
    wg = np.ascontiguousarray(swa_w_in[0][:, 3072:4096])
    for core in range(8):
        b, c = core // 4, core % 4
        numT = np.zeros((3, D, CH), dtype=np.float32)
        denT = np.zeros((3, D, CH), dtype=np.float32)
        for g, (_, d) in enumerate(SWA_GROUPS):
            own = perms_all[core][g][d * 128:] - c * CH
            numT[g][:, own] = r3[core]["numT%d" % g]
            denT[g][:, own] = r3[core]["denT%d" % g]
        in_maps.append({"w": swa_w_out[0], "lnp": lnp1, "resT": np.ascontiguousarray(x1T[b][:, c * CH:(c + 1) * CH]),
                        "numT": numT, "denT": denT, "wg": wg})
    r4 = _run(_get_nc("l4"), in_maps)
    out = np.zeros((BATCH, SEQ, D), dtype=np.float32)
    for core in range(8):
        b, c = core // 4, core % 4
        out[b, c * CH:(c + 1) * CH, :] = r4[core]["outT"].T
    return out


def _pl_phase(nc, fw, w_dram, ln_sb, lidx, a_v, res_v, res_off, ntiles, first_own, x1b, out_v, out_buf_final, tag):
    TT = 512
    pes = contextlib.ExitStack()
    with pes:
        w_sb = _sb(nc, pes, tag + "w_sb", [128, KC, D], BF16)
        ones_sb = _sb(nc, pes, tag + "ones_sb", [128, 128], BF16)
        a_sb = [_sb(nc, pes, tag + "a_sb%d" % i, [128, KC, TT], BF16) for i in range(2)]
        res_sb = [_sb(nc, pes, tag + "res_sb%d" % i, [128, KC, TT], F32) for i in range(2)]
        rb_sbs = [_sb(nc, pes, tag + "rb_sb%d" % i, [128, KC, TT], BF16) for i in range(2)]
        rq_sbs = [_sb(nc, pes, tag + "rq_sb%d" % i, [128, KC, TT], BF16) for i in range(2)]
        o_sb = [_sb(nc, pes, tag + "o_sb%d" % i, [128, KC, TT], F32) for i in range(2)]
        mean_sb = _sb(nc, pes, tag + "mean_sb", [128, TT], F32)
        msq_sb = _sb(nc, pes, tag + "msq_sb", [128, TT], F32)
        var_sb = _sb(nc, pes, tag + "var_sb", [128, TT], F32)
        rstd_sb = _sb(nc, pes, tag + "rstd_sb", [128, TT], F32)
        nmr_sb = _sb(nc, pes, tag + "nmr_sb", [128, TT], F32)
        t_sb = [_sb(nc, pes, tag + "t_sb%d" % i, [128, TT], F32) for i in range(2)]
        y_ps = [_ps(nc, pes, tag + "y_ps%d" % i, [128, TT], F32) for i in range(2)]
        sum_ps = _ps(nc, pes, tag + "sum_ps", [128, TT], F32)
        sq_ps = _ps(nc, pes, tag + "sq_ps", [128, TT], F32)
        B = {}
        for n in ["w", "ones", "mean", "msq", "var", "rstd", "nmr", "sum_ps", "sq_ps", "x1b"]:
            B[n] = fw.buf(tag + n)
        for n in ["a", "res", "t", "y_ps"]:
            B[n] = fw.bufs(2, tag + n)
        B["o"] = fw.bufs(2, tag + "o")
        RR = [fw.bufs(KC, tag + "rr%d_" % i) for i in range(2)]
        RB = [fw.bufs(KC, tag + "rbb%d_" % i) for i in range(2)]
        RQ = [fw.bufs(KC, tag + "rqq%d_" % i) for i in range(2)]
        w_v = w_dram.rearrange("(kc p) n -> p kc n", p=128)
        WB = fw.bufs(KC, tag + "wblk")
        for fc in range(KC):
            fw.dma("pool", w_sb[:, :, fc * 128:(fc + 1) * 128], w_v[:, :, fc * 128:(fc + 1) * 128], writes=[WB[fc]])
        fw.op("dve", lambda e: e.memset(ones_sb[:], 1.0), writes=[B["ones"]])
        def load_a(t):
            p = t % 2
            t0 = t * TT
            fw.dma("sp", a_sb[p][:], a_v[:, :, t0:t0 + TT], writes=[B["a"][p]])

        RES = [fw.bufs(KC, tag + "res%d_" % i) for i in range(2)]

        def load_res_fc(t, fc):
            p = t % 2
            t0 = t * TT
            fw.dma("sp", res_sb[p][:, fc, :], res_v[:, fc, res_off + t0:res_off + t0 + TT], writes=[RES[p][fc], RR[p][fc]])

        def load_res(t):
            for fc in range(KC):
                load_res_fc(t, fc)

        def normalize_fc(t, fc):
            p = t % 2
            t0 = t * TT
            own = t >= first_own
            r_sb = res_sb[p]
            tq = fc % 2
            last = t == ntiles - 1
            fw.op("dve" if (fc % 4 or last) else "pool", lambda e: e.tensor_tensor(out=t_sb[tq][:], in0=r_sb[:, fc, :], in1=rstd_sb[:], op=ALU.mult),
                  reads=[RR[p][fc], B["rstd"]], writes=[B["t"][tq]])
            fw.op("dve", lambda e: e.tensor_tensor(out=t_sb[tq][:], in0=t_sb[tq][:], in1=nmr_sb[:], op=ALU.subtract),
                  reads=[B["t"][tq], B["nmr"]], writes=[B["t"][tq]])
            if own:
                fw.op("act", lambda e: e.activation(out=o_sb[p][:, fc, :], in_=t_sb[tq][:], func=AF.Identity,
                                                    bias=ln_sb[:, 2 * lidx + 1, fc:fc + 1], scale=ln_sb[:, 2 * lidx, fc:fc + 1]),
                      reads=[B["t"][tq]], writes=[B["o"][p]])
                if x1b is not None and last:
                    fw.op("act", lambda e: e.activation(out=x1b[:, fc, t0:t0 + TT], in_=t_sb[tq][:], func=AF.Identity,
                                                        bias=ln_sb[:, 2 * lidx + 1, fc:fc + 1], scale=ln_sb[:, 2 * lidx, fc:fc + 1]),
                          reads=[B["t"][tq]], writes=[B["x1b"]])
                elif x1b is not None:
                    fw.op("pool", lambda e: e.tensor_copy(out=x1b[:, fc, t0:t0 + TT], in_=o_sb[p][:, fc, :]),
                          reads=[B["o"][p]], writes=[B["x1b"]])
            elif x1b is not None:
                fw.op("act", lambda e: e.activation(out=x1b[:, fc, t0:t0 + TT], in_=t_sb[tq][:], func=AF.Identity,
                                                    bias=ln_sb[:, 2 * lidx + 1, fc:fc + 1], scale=ln_sb[:, 2 * lidx, fc:fc + 1]),
                      reads=[B["t"][tq]], writes=[B["x1b"]])
            if own and fc == KC - 1:
                c0 = (t - first_own) * TT
                wr = [out_buf_final] if out_buf_final is not None else []
                fw.dma("sp", out_v[:, :, c0:c0 + TT], o_sb[p][:], reads=[B["o"][p]], writes=wr, owner=B["o"][p],
                       final=(out_buf_final is None))

        load_a(0)
        load_res(0)
        for t in range(ntiles + 1):
            if t < ntiles:
                p = t % 2
                r_sb, rb_sb, rq_sb = res_sb[p], rb_sbs[p], rq_sbs[p]
                if t + 1 < ntiles:
                    load_a(t + 1)
            for fc in range(KC):
                if t < ntiles:
                    yq = fc % 2
                    for kc in range(KC):
                        fw.op("pe", lambda e, kc=kc, fc=fc, yq=yq: e.matmul(
                            y_ps[yq][:], lhsT=w_sb[:, kc, fc * 128:(fc + 1) * 128], rhs=a_sb[p][:, kc, :],
                            start=(kc == 0), stop=(kc == KC - 1)),
                            reads=[WB[fc], B["a"][p]], writes=[B["y_ps"][yq]], inc=(kc == KC - 1))
                    fw.op("dve", lambda e, fc=fc, yq=yq: e.scalar_tensor_tensor(
                        out=r_sb[:, fc, :], in0=res_sb[p][:, fc, :], scalar=float(ALPHA), in1=y_ps[yq][:], op0=ALU.mult, op1=ALU.add),
                        reads=[RES[p][fc], B["y_ps"][yq]], writes=[RR[p][fc]])
                    fw.op("act", lambda e, fc=fc: e.activation(out=rb_sb[:, fc, :], in_=r_sb[:, fc, :], func=AF.Copy),
                          reads=[RR[p][fc]], writes=[RB[p][fc]])
                    fw.op("act", lambda e, fc=fc: e.activation(out=rq_sb[:, fc, :], in_=r_sb[:, fc, :], func=AF.Square),
                          reads=[RR[p][fc]], writes=[RQ[p][fc]])
                if t >= 1:
                    normalize_fc(t - 1, fc)
                if t + 1 < ntiles and t >= 1:
                    load_res_fc(t + 1, fc)
            if t == 0 and ntiles > 1:
                load_res(1)
            if t < ntiles:
                for fc in range(KC):
                    fw.op("pe", lambda e, fc=fc: e.matmul(sum_ps[:], lhsT=ones_sb[:], rhs=rb_sb[:, fc, :],
                                                          start=(fc == 0), stop=(fc == KC - 1)),
                          reads=[B["ones"], RB[p][fc]], writes=[B["sum_ps"]], inc=(fc == KC - 1))
                for fc in range(KC):
                    fw.op("pe", lambda e, fc=fc: e.matmul(sq_ps[:], lhsT=ones_sb[:], rhs=rq_sb[:, fc, :],
                                                          start=(fc == 0), stop=(fc == KC - 1)),
                          reads=[B["ones"], RQ[p][fc]], writes=[B["sq_ps"]], inc=(fc == KC - 1))
                fw.op("dve", lambda e: e.tensor_scalar(out=mean_sb[:], in0=sum_ps[:], scalar1=1.0 / D, scalar2=None, op0=ALU.mult),
                      reads=[B["sum_ps"]], writes=[B["mean"]])
                fw.op("dve", lambda e: e.tensor_tensor(out=msq_sb[:], in0=mean_sb[:], in1=mean_sb[:], op=ALU.mult),
                      reads=[B["mean"]], writes=[B["msq"]])
                fw.op("dve", lambda e: e.scalar_tensor_tensor(out=var_sb[:], in0=sq_ps[:], scalar=1.0 / D, in1=msq_sb[:],
                                                              op0=ALU.mult, op1=ALU.subtract),
                      reads=[B["sq_ps"], B["msq"]], writes=[B["var"]])
                fw.op("act", lambda e: e.activation(out=var_sb[:], in_=var_sb[:], func=AF.Ln, bias=LN_EPS, scale=1.0),
                      reads=[B["var"]], writes=[B["var"]])
                fw.op("act", lambda e: e.activation(out=rstd_sb[:], in_=var_sb[:], func=AF.Exp, scale=-0.5),
                      reads=[B["var"]], writes=[B["rstd"]])
                fw.op("dve", lambda e: e.tensor_tensor(out=nmr_sb[:], in0=mean_sb[:], in1=rstd_sb[:], op=ALU.mult),
                      reads=[B["mean"], B["rstd"]], writes=[B["nmr"]])
        fw.barrier()

def build_fused():
    TT = 512
    NT1 = 16
    nc = bass.Bass("TRN2", target_bir_lowering=False)
    xT = nc.dram_tensor("xT", [D, 4 * CH], F32, kind="ExternalInput").ap()
    w_in = nc.dram_tensor("w_in", [D, 3088], F32, kind="ExternalInput").ap()
    wa2 = nc.dram_tensor("wa2", [16, 512], F32, kind="ExternalInput").ap()
    ba2 = nc.dram_tensor("ba2", [128, 4], F32, kind="ExternalInput").ap()
    ngb = nc.dram_tensor("ngb", [128, 1024], F32, kind="ExternalInput").ap()
    cmask = nc.dram_tensor("cmask", [128, 512], F32, kind="ExternalInput").ap()
    ident = nc.dram_tensor("ident", [128, 128], F32, kind="ExternalInput").ap()
    w_o1 = nc.dram_tensor("w_o1", [D, D], F32, kind="ExternalInput").ap()
    w_o2 = nc.dram_tensor("w_o2", [D, D], F32, kind="ExternalInput").ap()
    lnp = nc.dram_tensor("lnp", [128, 4, KC], F32, kind="ExternalInput").ap()
    w_kv = nc.dram_tensor("w_kv", [D, 1536], F32, kind="ExternalInput").ap()
    w_s = nc.dram_tensor("w_s", [D, 4096], F32, kind="ExternalInput").ap()
    cosd, sind = [], []
    for g, (_, d) in enumerate(SWA_GROUPS):
        NP = (d + 16) * 128
        cosd.append(nc.dram_tensor("cos%d" % g, [128, NP], F32, kind="ExternalInput").ap())
        sind.append(nc.dram_tensor("sin%d" % g, [128, NP], F32, kind="ExternalInput").ap())
    mcur = nc.dram_tensor("mcur", [128, 512], F32, kind="ExternalInput").ap()
    mprev = nc.dram_tensor("mprev", [128, 512], F32, kind="ExternalInput").ap()
    mprevh = nc.dram_tensor("mprevh", [128, 512], F32, kind="ExternalInput").ap()
    esel = nc.dram_tensor("esel", [128, 512], F32, kind="ExternalInput").ap()
    fsel = nc.dram_tensor("fsel", [128, 512], F32, kind="ExternalInput").ap()
    outT = nc.dram_tensor("outT", [D, CH], F32, kind="ExternalOutput").ap()
    ogT_s = nc.dram_tensor("ogT_s", [D, 2 * CH], BF16).ap()
    x1_s = nc.dram_tensor("x1_s", [D, CH], F32).ap()
    og2T_s = nc.dram_tensor("og2T_s", [D, CH], BF16).ap()

    es = contextlib.ExitStack()
    with es:
        fw = FW(nc, es)
        ln_sb = _sb(nc, es, "ln_sb", [128, 4, KC], F32)
        Bln = fw.buf("ln")
        fw.dma("sp", ln_sb[:], lnp[:, :, :], writes=[Bln])
        B_ogT = fw.buf("ogT_s")
        B_x1s = fw.buf("x1_s")
        B_og2 = fw.buf("og2T_s")
        xT_v = xT.rearrange("(kc p) t -> p kc t", p=128)
        ogT_v = ogT_s.rearrange("(kc p) t -> p kc t", p=128)
        x1s_v = x1_s.rearrange("(kc p) t -> p kc t", p=128)
        og2_v = og2T_s.rearrange("(kc p) t -> p kc t", p=128)
        out_v = outT.rearrange("(kc p) t -> p kc t", p=128)

        _gla_phase2(nc, fw, xT_v, w_in, wa2, ba2, ngb, cmask, ident, ogT_v, B_ogT)

        xes = contextlib.ExitStack()
        with xes:
            x1b = _sb(nc, xes, "x1b", [128, KC, 2 * CH], BF16)
            _pl_phase(nc, fw, w_o1, ln_sb, 0, ogT_v, xT_v, 2 * CH, 8, 4, x1b, x1s_v, B_x1s, "p2_")
            _attn_phase(nc, fw, x1b, w_kv, w_s, cosd, sind, (mcur, mprev, mprevh), esel, fsel, ident, og2_v, B_og2)
        _pl_phase(nc, fw, w_o2, ln_sb, 1, og2_v, x1s_v, 0, 4, 0, None, out_v, None, "p4_")
        fw.finish()
    return nc


def _attn_phase(nc, fw, x1b, w_kv, w_s, cosd, sind, masks, esel, fsel, ident, og2_v, B_og2):
    pes = contextlib.ExitStack()
    with pes:
        wq_sb = _sb(nc, pes, "a_wq_sb", [128, KC, 512], BF16)
        wk_sb = _sb(nc, pes, "a_wk_sb", [128, KC, 128], BF16)
        wv_sb = _sb(nc, pes, "a_wv_sb", [128, KC, 128], BF16)
        wg_sb = wq_sb
        xs_sb = [_sb(nc, pes, "a_xs_sb%d" % i, [128, KC, 512], BF16) for i in range(2)]
        cs_sb = [_sb(nc, pes, "a_cs_sb%d" % i, [128, 512], F32) for i in range(2)]
        sn_sb = [_sb(nc, pes, "a_sn_sb%d" % i, [128, 512], F32) for i in range(2)]
        Kt_sb = _sb(nc, pes, "a_Kt_sb", [128, 4096], BF16)
        V_sb = _sb(nc, pes, "a_V_sb", [128, 32, 128], BF16)
        Qt_sb = _sb(nc, pes, "a_Qt_sb", [128, 4, CH], BF16)
        t1_sb = [_sb(nc, pes, "a_t1_sb%d" % i, [128, 512], F32) for i in range(2)]
        t2_sb = [_sb(nc, pes, "a_t2_sb%d" % i, [128, 512], F32) for i in range(2)]
        pe_sb = [[_sb(nc, pes, "a_pe_sb%d_%d" % (k, i), [128, 4, 128], BF16) for i in range(2)] for k in range(2)]
        pm_sb = [[_sb(nc, pes, "a_pm_sb%d_%d" % (k, i), [128, 4, 128], BF16) for i in range(2)] for k in range(2)]
        m_sb = [_sb(nc, pes, "a_m_sb%d" % i, [128, 4, 128], BF16) for i in range(3)]
        aid_sb = _sb(nc, pes, "a_id_sb", [128, 128], BF16)
        E_sb = _sb(nc, pes, "a_E_sb", [128, 4, 128], BF16)
        F_sb = _sb(nc, pes, "a_F_sb", [128, 4, 128], F32)
        acc_num = _sb(nc, pes, "a_acc_num", [128, 4, CH], F32)
        acc_den = _sb(nc, pes, "a_acc_den", [128, CH], F32)
        rden_sb = _sb(nc, pes, "a_rden_sb", [128, 512], F32)
        sg_sb = [_sb(nc, pes, "a_sg_sb%d" % i, [128, 512], F32) for i in range(2)]
        tt_sb = [_sb(nc, pes, "a_tt_sb%d" % i, [128, 512], F32) for i in range(2)]
        og2_sb = [_sb(nc, pes, "a_og2_sb%d" % i, [128, 512], BF16) for i in range(2)]
        pj_ps = [_ps(nc, pes, "a_pj_ps%d" % i, [128, 512], F32) for i in range(2)]
        vd_ps = _ps(nc, pes, "a_vd_ps", [128, 512], F32)
        v_ps = vd_ps[:, 0:128]
        den_ps = vd_ps[:, 128:256]
        s_ps2 = [[_ps(nc, pes, "a_s_ps%d_%d" % (st, i), [128, 4, 128], F32) for i in range(2)] for st in range(2)]
        num_ps = _ps(nc, pes, "a_num_ps", [128, 4, 128], F32)
        B = {}
        for n in ["wq", "wk", "wv", "wg", "Kt", "V", "Qt", "E", "F", "v_ps", "num_ps", "den_ps", "accn", "accd", "rden", "x1b"]:
            B[n] = fw.buf("a_" + n)
        for n in ["cs", "sn", "t1", "t2", "pj_ps", "sg", "tt", "og2", "xs"]:
            B[n] = fw.bufs(2, "a_" + n)
        SPS = [fw.bufs(2, "a_s_ps%d_" % st) for st in range(2)]
        B["den_ps"] = B["v_ps"]
        B["m"] = fw.bufs(3, "a_m")
        B["pe"] = [fw.bufs(2, "a_pe%d_" % k) for k in range(2)]
        B["pm"] = [fw.bufs(2, "a_pm%d_" % k) for k in range(2)]
        B["aid"] = fw.buf("a_id")

        def load_consts():
            for i, mm in enumerate(masks):
                fw.dma("pool", m_sb[i][:], mm.rearrange("p (h q) -> p h q", h=4), writes=[B["m"][i]])
            fw.dma("pool", E_sb[:], esel.rearrange("p (h q) -> p h q", h=4), writes=[B["E"]])
            fw.dma("pool", aid_sb[:], ident[:, :], writes=[B["aid"]])
            fw.dma("sp", F_sb[:], fsel.rearrange("p (h q) -> p h q", h=4), writes=[B["F"]])
        pjc = [0]

        def rope(ps, wd, xp, out_ap, out_buf):
            a = pjc[0] % 2
            fw.op("dve", lambda e: e.tensor_tensor(out=t1_sb[a][:, 0:wd], in0=ps[:, 0:wd], in1=cs_sb[xp][:, 0:wd], op=ALU.mult),
                  reads=[B["pj_ps"][a], B["cs"][xp]], writes=[B["t1"][a]])
            fw.op("dve", lambda e: e.tensor_tensor(out=t2_sb[a][0:64, 0:wd], in0=ps[64:128, 0:wd], in1=sn_sb[xp][64:128, 0:wd], op=ALU.mult),
                  reads=[B["pj_ps"][a], B["sn"][xp]], writes=[B["t2"][a]])
            fw.op("dve", lambda e: e.tensor_tensor(out=t2_sb[a][64:128, 0:wd], in0=ps[0:64, 0:wd], in1=sn_sb[xp][0:64, 0:wd], op=ALU.mult),
                  reads=[B["pj_ps"][a], B["sn"][xp]], writes=[B["t2"][a]])
            fw.op("pool", lambda e: e.tensor_tensor(out=out_ap, in0=t1_sb[a][:, 0:wd], in1=t2_sb[a][:, 0:wd], op=ALU.add),
                  reads=[B["t1"][a], B["t2"][a]], writes=[out_buf])

        def load_wkv(hk_, g_):
            kcol = (2 * g_ + hk_) * 128
            fw.dma("pool", wk_sb[:], w_kv.rearrange("(kc p) n -> p kc n", p=128)[:, :, kcol:kcol + 128], writes=[B["wk"]])
            fw.dma("pool", wv_sb[:], w_kv.rearrange("(kc p) n -> p kc n", p=128)[:, :, 768 + kcol:768 + kcol + 128], writes=[B["wv"]])

        def load_wq(hk_, g_):
            fw.dma("pool", wq_sb[:], w_s.rearrange("(kc p) n -> p kc n", p=128)[:, :, g_ * 1024 + hk_ * 512:g_ * 1024 + (hk_ + 1) * 512],
                   writes=[B["wq"]])

        def load_w(hk_, g_):
            load_wkv(hk_, g_)
            load_wq(hk_, g_)

        def load_tab(g_, tile, cnt):
            p0_, wd_ = tile
            xp_ = cnt % 2
            fw.dma("sp", cs_sb[xp_][:, 0:wd_], cosd[g_][:, p0_:p0_ + wd_], writes=[B["cs"][xp_]])
            fw.dma("sp", sn_sb[xp_][:, 0:wd_], sind[g_][:, p0_:p0_ + wd_], writes=[B["sn"][xp_]])

        tcount = 0
        acount = 0
        ocount = 0
        for hk in range(2):
            for g, (_, d) in enumerate(SWA_GROUPS):
                HL = d * 128
                n = CH // d
                halo_v = lambda kc: x1b[:, kc, CH - 128 * d:CH].rearrange("p (i r) -> p r i", r=d)
                own_v = lambda kc: x1b[:, kc, CH:2 * CH].rearrange("p (i r) -> p r i", r=d)

                def xsrc(kc, p0, wd):
                    if p0 < HL:
                        r0 = p0 // 128
                        nr = wd // 128
                        return halo_v(kc)[:, r0:r0 + nr, :], nr
                    q0 = p0 - HL
                    if n >= wd:
                        r, i0 = q0 // n, q0 % n
                        return own_v(kc)[:, r, i0:i0 + wd], 1
                    r0 = q0 // n
                    nr = wd // n
                    return own_v(kc)[:, r0:r0 + nr, :], nr

                cur = {}

                def stage(tile, cnt, share=False):
                    if d == 1:
                        return
                    p0_, wd_ = tile
                    xq = cnt % 2
                    for kc in range(KC):
                        src, nr = xsrc(kc, p0_, wd_)
                        dst = xs_sb[xq][:, kc, 0:wd_]
                        if len(src.shape) == 3:
                            dst = dst.rearrange("p (r i) -> p r i", r=nr)
                        if share and kc % 2 == 1:
                            fw.op("dve", lambda e, src=src, dst=dst: e.tensor_copy(out=dst, in_=src),
                                  reads=[B["x1b"]], writes=[B["xs"][xq]])
                        else:
                            fw.op("act", lambda e, src=src, dst=dst: e.activation(out=dst, in_=src, func=AF.Copy),
                                  reads=[B["x1b"]], writes=[B["xs"][xq]])

                def xview(kc, p0, wd):
                    if d == 1:
                        return xsrc(kc, p0, wd)
                    return xs_sb[cur["xq"]][:, kc, 0:wd], 1

                def xblock(kc, pb):
                    if d == 1:
                        if pb < HL:
                            return halo_v(kc)[:, pb // 128, :]
                        q = pb - HL
                        return own_v(kc)[:, q // n, (q % n):(q % n) + 128]
                    off = pb - cur["p0"]
                    return xs_sb[cur["xq"]][:, kc, off:off + 128]

                def proj(w_ap_fn, wb, p0, wd):
                    a = pjc[0] % 2
                    for kc in range(KC):
                        rhs, nr = xview(kc, p0, wd)
                        out = pj_ps[a][:, 0:wd] if nr == 1 and len(rhs.shape) == 2 else pj_ps[a][:, 0:wd].rearrange("p (r i) -> p r i", r=nr)
                        fw.op("pe", lambda e, kc=kc, rhs=rhs, out=out: e.matmul(out, lhsT=w_ap_fn(kc), rhs=rhs,
                                                                                 start=(kc == 0), stop=(kc == KC - 1)),
                              reads=[wb, B["x1b"], B["xs"][cur.get("xq", 0)]], writes=[B["pj_ps"][a]], inc=(kc == KC - 1))
                    return pj_ps[a]

                if hk == 0 and g == 0:
                    load_w(0, 0)
                    load_consts()
                if d == 1:
                    tiles = [(0, 128)]
                else:
                    tiles = [(i * 512, 512) for i in range(HL // 512)]
                tiles += [(HL + i * 512, 512) for i in range(4)]
                if hk == 0 and g == 0:
                    load_tab(g, tiles[0], tcount)
                stage(tiles[0], tcount, share=True)
                for ti, (p0, wd) in enumerate(tiles):
                    xp = tcount % 2
                    cur["xq"] = xp
                    cur["p0"] = p0
                    tcount += 1
                    if ti + 1 < len(tiles):
                        load_tab(g, tiles[ti + 1], tcount)
                        stage(tiles[ti + 1], tcount, share=(p0 < HL))

                    def vblock(blk):
                        for kc in range(KC):
                            fw.op("pe", lambda e, kc=kc: e.matmul(
                                v_ps, lhsT=xblock(kc, p0 + blk * 128), rhs=wv_sb[:, kc, :],
                                start=(kc == 0), stop=(kc == KC - 1)),
                                reads=[B["wv"], B["x1b"], B["xs"][cur.get("xq", 0)]], writes=[B["v_ps"]], inc=(kc == KC - 1))
                        fw.op("act", lambda e: e.activation(out=V_sb[:, p0 // 128 + blk, :], in_=v_ps, func=AF.Copy),
                              reads=[B["v_ps"]], writes=[B["V"]])

                    ps = proj(lambda kc: wk_sb[:, kc, :], B["wk"], p0, wd)
                    rope(ps, wd, xp, Kt_sb[:, p0:p0 + wd], B["Kt"])
                    pjc[0] += 1
                    nblk = wd // 128
                    if p0 >= HL:
                        q0 = p0 - HL
                        for hh in range(4):
                            if hh < nblk:
                                vblock(hh)
                            ps = proj(lambda kc, hh=hh: wq_sb[:, kc, hh * 128:(hh + 1) * 128], B["wq"], p0, wd)
                            rope(ps, wd, xp, Qt_sb[:, hh, q0:q0 + wd], B["Qt"])
                            pjc[0] += 1
                    else:
                        for blk in range(nblk):
                            vblock(blk)
                nxt = (hk, g + 1) if g + 1 < 3 else None
                if nxt is not None:
                    load_w(*nxt)
                    d2 = SWA_GROUPS[nxt[1]][1]
                    load_tab(nxt[1], (0, 128) if d2 == 1 else (0, 512), tcount)
                nbr = 16 // d
                accn_v = acc_num[:, :, :].rearrange("p h (i r) -> p h r i", r=d)
                accd_v = acc_den[:, :].rearrange("p (i r) -> p r i", r=d)
                def blk_info(j):
                    if j % nbr == 0:
                        prevpos, pm_i = (j // nbr) * 128, 2
                    else:
                        prevpos, pm_i = HL + (j - 1) * 128, 1
                    return prevpos, pm_i, HL + j * 128

                def scores(j, a):
                    prevpos, pm_i, curpos = blk_info(j)
                    for kb, (kpos, mi) in enumerate(((prevpos, pm_i), (curpos, 0))):
                        fw.op("pe", lambda e, kb=kb, kpos=kpos: e.matmul(
                            s_ps2[a][kb][:], lhsT=Kt_sb[:, kpos:kpos + 128], rhs=Qt_sb[:, :, j * 128:(j + 1) * 128],
                            start=True, stop=False),
                            reads=[B["Kt"], B["Qt"]], writes=[SPS[a][kb]], inc=False)
                        fw.op("pe", lambda e, kb=kb, mi=mi: e.matmul(
                            s_ps2[a][kb][:], lhsT=aid_sb[:], rhs=m_sb[mi][:], start=False, stop=True),
                            reads=[B["aid"], B["m"][mi]], writes=[SPS[a][kb]])
                        fw.op("act", lambda e, kb=kb: e.activation(out=pm_sb[kb][a][:], in_=s_ps2[a][kb][:], func=AF.Exp, scale=QS),
                              reads=[SPS[a][kb]], writes=[B["pm"][kb][a]])

                def pv(j, a):
                    prevpos, pm_i, curpos = blk_info(j)
                    r_j, ib = j // nbr, j % nbr
                    for kb, kpos in enumerate((prevpos, curpos)):
                        fw.op("pe", lambda e, kb=kb, kpos=kpos: e.matmul(
                            num_ps[:], lhsT=V_sb[:, kpos // 128, :], rhs=pm_sb[kb][a][:],
                            start=(kb == 0), stop=(kb == 1)),
                            reads=[B["V"], B["pm"][kb][a]], writes=[B["num_ps"]], inc=(kb == 1))
                    for hh in range(4):
                        for kb in range(2):
                            fw.op("pe", lambda e, kb=kb, hh=hh: e.matmul(den_ps, lhsT=E_sb[:, hh, :], rhs=pm_sb[kb][a][:, hh, :],
                                                                         start=(hh == 0 and kb == 0), stop=(hh == 3 and kb == 1)),
                                  reads=[B["E"], B["pm"][kb][a]], writes=[B["den_ps"]], inc=(hh == 3 and kb == 1))
                    nv = accn_v[:, :, r_j, ib * 128:(ib + 1) * 128]
                    dv = accd_v[:, r_j, ib * 128:(ib + 1) * 128]
                    if g == 0:
                        fw.op("act", lambda e, nv=nv: e.activation(out=nv, in_=num_ps[:], func=AF.Copy),
                              reads=[B["num_ps"]], writes=[B["accn"]])
                        fw.op("dve", lambda e, dv=dv: e.tensor_copy(out=dv, in_=den_ps),
                              reads=[B["den_ps"]], writes=[B["accd"]])
                    else:
                        fw.op("dve", lambda e, nv=nv: e.tensor_tensor(out=nv, in0=nv, in1=num_ps[:], op=ALU.add),
                              reads=[B["num_ps"], B["accn"]], writes=[B["accn"]])
                        fw.op("dve", lambda e, dv=dv: e.tensor_tensor(out=dv, in0=dv, in1=den_ps, op=ALU.add),
                              reads=[B["den_ps"], B["accd"]], writes=[B["accd"]])

                aj = []
                for j in range(16):
                    aj.append(acount % 2)
                    acount += 1
                scores(0, aj[0])
                for j in range(16):
                    if j + 1 < 16:
                        scores(j + 1, aj[j + 1])
                    pv(j, aj[j])
            B["wg"] = B["wq"]
            fw.dma("pool", wg_sb[:], w_s.rearrange("(kc p) n -> p kc n", p=128)[:, :, 3072 + hk * 512:3072 + (hk + 1) * 512],
                   writes=[B["wg"]])
            if hk == 0:
                load_wkv(1, 0)
                load_tab(0, (0, 128), tcount)
            for tt in range(4):
                c0 = tt * 512
                fw.op("dve", lambda e: e.reciprocal(out=rden_sb[:], in_=acc_den[:, c0:c0 + 512]),
                      reads=[B["accd"]], writes=[B["rden"]])
                for hh in range(4):
                    o = ocount % 2
                    ocount += 1
                    for kc in range(KC):
                        fw.op("pe", lambda e, kc=kc, hh=hh: e.matmul(pj_ps[0][:], lhsT=wg_sb[:, kc, hh * 128:(hh + 1) * 128],
                                                                     rhs=x1b[:, kc, CH + c0:CH + c0 + 512],
                                                                     start=(kc == 0), stop=(kc == KC - 1)),
                              reads=[B["wg"], B["x1b"]], writes=[B["pj_ps"][0]], inc=(kc == KC - 1))
                    fw.op("pe", lambda e, hh=hh: e.matmul(pj_ps[1][:], lhsT=F_sb[:, hh, :], rhs=rden_sb[:], start=True, stop=True),
                          reads=[B["F"], B["rden"]], writes=[B["pj_ps"][1]])
                    fw.op("act", lambda e, o=o: e.activation(out=sg_sb[o][:], in_=pj_ps[0][:], func=AF.Silu),
                          reads=[B["pj_ps"][0]], writes=[B["sg"][o]])
                    fw.op("dve", lambda e, o=o, hh=hh: e.tensor_tensor(out=tt_sb[o][:], in0=acc_num[:, hh, c0:c0 + 512], in1=pj_ps[1][:], op=ALU.mult),
                          reads=[B["accn"], B["pj_ps"][1]], writes=[B["tt"][o]])
                    fw.op("pool", lambda e, o=o: e.tensor_tensor(out=og2_sb[o][:], in0=tt_sb[o][:], in1=sg_sb[o][:], op=ALU.mult),
                          reads=[B["tt"][o], B["sg"][o]], writes=[B["og2"][o]])
                    fw.dma("sp", og2_v[:, 4 * hk + hh, c0:c0 + 512], og2_sb[o][:], reads=[B["og2"][o]], writes=[B_og2], owner=B["og2"][o])
            if hk == 0:
                load_wq(1, 0)
        fw.barrier()


def _sel_consts():
    k = np.arange(128)[:, None]
    m = np.arange(128)[None, :]
    esel = np.concatenate([np.broadcast_to((m // 32 == hh), (128, 128)).astype(np.float32) for hh in range(4)], axis=1)
    fsel = np.concatenate([np.broadcast_to((k // 32 == hh), (128, 128)).astype(np.float32) / 32.0 for hh in range(4)], axis=1)
    return np.ascontiguousarray(esel), np.ascontiguousarray(fsel)


def _fused_inputs(b, c, x, gla_w_in, gla_w_a2, gla_b_a2, gla_norm_g, gla_w_out, w_kv, swa_w_in, swa_w_out, ln_g, ln_b):
    cmask, ident = _consts()
    esel, fsel = _sel_consts()
    xT = np.zeros((D, 4 * CH), dtype=np.float32)
    lo = (c - 3) * CH
    src0 = max(lo, 0)
    xT[:, src0 - lo:] = x[b, src0:(c + 1) * CH].T
    lnp = np.ascontiguousarray(np.stack([ln_g[0].reshape(KC, 128).T, ln_b[0].reshape(KC, 128).T,
                                         ln_g[1].reshape(KC, 128).T, ln_b[1].reshape(KC, 128).T], axis=1)).astype(np.float32)
    im = {"xT": xT, "w_in": gla_w_in[0], "wa2": gla_w_a2[0],
          "ba2": np.ascontiguousarray(gla_b_a2[0].reshape(4, 128).T),
          "ngb": np.ascontiguousarray(np.broadcast_to(gla_norm_g[0][None, :], (128, 1024))),
          "cmask": np.ascontiguousarray(np.tile(cmask, (1, 4))), "ident": ident, "w_o1": gla_w_out[0], "w_o2": swa_w_out[0], "lnp": lnp,
          "w_kv": w_kv, "w_s": swa_w_in[0], "esel": esel, "fsel": fsel}
    for g, idx in enumerate(_l3_perm(c)):
        cosT, sinT = _rope_tables(idx)
        im["cos%d" % g] = cosT
        im["sin%d" % g] = sinT
    for nm, mk in zip(("mcur", "mprev", "mprevh"), _l3_masks(c)):
        im[nm] = np.ascontiguousarray((mk - 1.0) * 30000.0).astype(np.float32)
    return im


def kernel_fused(x, gla_w_in, gla_w_a2, gla_b_a2, gla_norm_g, gla_w_out, w_kv, swa_w_in, swa_w_out, ln_g, ln_b):
    f = lambda a: np.asarray(a, dtype=np.float32)
    args = list(map(f, (x, gla_w_in, gla_w_a2, gla_b_a2, gla_norm_g, gla_w_out, w_kv, swa_w_in, swa_w_out, ln_g, ln_b)))
    if "fused" not in _NC_CACHE:
        _NC_CACHE["fused"] = build_fused()
    in_maps = [_fused_inputs(core // 4, core % 4, *args) for core in range(8)]
    r = _run(_NC_CACHE["fused"], in_maps)
    out = np.zeros((BATCH, SEQ, D), dtype=np.float32)
    for core in range(8):
        b, c = core // 4, core % 4
        out[b, c * CH:(c + 1) * CH, :] = r[core]["outT"].T
    return out


def kernel(x, gla_w_in, gla_w_a2, gla_b_a2, gla_norm_g, gla_w_out, w_kv, swa_w_in, swa_w_out, ln_g, ln_b):
    return kernel_fused(x, gla_w_in, gla_w_a2, gla_b_a2, gla_norm_g, gla_w_out, w_kv, swa_w_in, swa_w_out, ln_g, ln_b)


def _gla_phase2(nc, fw, xT_v, w_in, wa2, ba2, ngb, cmask4, ident, ogT_v, B_ogT):
    TT = 512
    pes = contextlib.ExitStack()
    with pes:
        w_sb = _sb(nc, pes, "g_w_sb", [128, KC, 3088], BF16)
        wa2_sb = _sb(nc, pes, "g_wa2_sb", [16, 512], BF16)
        nb_sb = _sb(nc, pes, "g_nb_sb", [128, 4], F32)
        ng_sb = _sb(nc, pes, "g_ng_sb", [128, 1024], F32)
        cm_sb = _sb(nc, pes, "g_cm_sb", [128, 4, 128], F32)
        id_sb = _sb(nc, pes, "g_id_sb", [128, 128], BF16)
        rm_sb = _sb(nc, pes, "g_rm_sb", [128, TT], F32)
        S_all = _sb(nc, pes, "g_S_all", [128, 4, 256], F32)
        Sb_all = _sb(nc, pes, "g_Sb_all", [128, 4, 256], BF16)
        xTb = [_sb(nc, pes, "g_xTb%d" % i, [128, KC, TT], BF16) for i in range(2)]
        aT_sb = _sb(nc, pes, "g_aT_sb", [16, TT], BF16)
        e1_sb = _sb(nc, pes, "g_e1_sb", [128, TT], F32)
        sp_sb = _sb(nc, pes, "g_sp_sb", [128, TT], F32)
        cs_sb = _sb(nc, pes, "g_cs_sb", [128, TT], F32)
        ek_sb = _sb(nc, pes, "g_ek_sb", [128, TT], F32)
        Kh = [_sb(nc, pes, "g_Kh%d" % i, [128, TT], BF16) for i in range(2)]
        eq_hp = [[_sb(nc, pes, "g_eq%d_%d" % (i, h), [128, TT], F32) for h in range(4)] for i in range(2)]
        Qt_h = [_sb(nc, pes, "g_Qt%d" % h, [128, TT], BF16) for h in range(4)]
        Kt_h = [_sb(nc, pes, "g_Kt%d" % h, [128, TT], BF16) for h in range(4)]
        Khtok_hp = [[_sb(nc, pes, "g_Khtok%d_%d" % (i, h), [128, 4, 128], BF16) for h in range(4)] for i in range(2)]
        v_hp = [[_sb(nc, pes, "g_v%d_%d" % (i, h), [128, 4, 256], BF16) for h in range(4)] for i in range(2)]
        gs_sb = [_sb(nc, pes, "g_gs_sb%d" % i, [128, 256], F32) for i in range(2)]
        gn_all = _sb(nc, pes, "g_gn_all", [128, 4, 4, 256], F32)
        att_sb = [_sb(nc, pes, "g_att_sb%d" % i, [128, 4, 128], BF16) for i in range(2)]
        o_sb = [_sb(nc, pes, "g_o_sb%d" % i, [128, 4, 256], F32) for i in range(2)]
        sq_sb = _sb(nc, pes, "g_sq_sb", [128, 4, 256], F32)
        tmp_sb = _sb(nc, pes, "g_tmp_sb", [128, 4, 256], F32)
        ss_sb = _sb(nc, pes, "g_ss_sb", [128, 4], F32)
        rs_sb = _sb(nc, pes, "g_rs_sb", [128, 4], F32)
        og_all = _sb(nc, pes, "g_og_all", [128, 4, 4, 256], BF16)
        ogT_sb = [_sb(nc, pes, "g_ogT_sb%d" % i, [128, KC, TT], BF16) for i in range(2)]
        pj_ps = [_ps(nc, pes, "g_pj_ps%d" % i, [128, 512], F32) for i in range(2)]
        tr_ps = _ps(nc, pes, "g_tr_ps", [128, 4, 256], BF16)
        at_ps = _ps(nc, pes, "g_at_ps", [128, 4, 128], F32)
        o_ps = [_ps(nc, pes, "g_o_ps%d" % i, [128, 2, 256], F32) for i in range(2)]
        kv_ps = [_ps(nc, pes, "g_kv_ps%d" % i, [128, 2, 256], F32) for i in range(2)]
        B = {}
        for n in ["w", "wa2", "nb", "ng", "cm", "id", "rm", "aT", "e1", "sp", "cs", "ek", "sq", "tmp", "ss", "rs",
                  "at_ps", "tr_ps", "Sb", "og"]:
            B[n] = fw.buf("g_" + n)
        for n in ["xTb", "Kh", "pj_ps", "gs", "att", "o_sb", "o_ps", "kv_ps", "ogT"]:
            B[n] = fw.bufs(2, "g_" + n)
        for n in ["Qt", "Kt", "S", "gn"]:
            B[n] = fw.bufs(4, "g_" + n)
        BP = {n: [fw.bufs(4, "g_%s%d_" % (n, i)) for i in range(2)] for n in ["eq", "Khtok", "v"]}
        w_v = w_in.rearrange("(kc p) n -> p kc n", p=128)
        WB = {}

        def load_wblk(key, c0, c1):
            WB[key] = fw.buf("g_w_" + key)
            fw.dma("pool", w_sb[:, :, c0:c1], w_v[:, :, c0:c1], writes=[WB[key]])

        load_wblk("a", 3072, 3088)
        for h in range(4):
            load_wblk("k%d" % h, 512 + h * 128, 512 + (h + 1) * 128)
            load_wblk("v%d" % h, 1024 + h * 256, 1024 + (h + 1) * 256)
        fw.dma("pool", wa2_sb[:], wa2[:, :], writes=[B["wa2"]])
        fw.dma("sp", nb_sb[:], ba2[:, :], writes=[B["nb"]])
        fw.dma("sp", ng_sb[:], ngb[:, :], writes=[B["ng"]])
        fw.dma("sp", cm_sb[:], cmask4.rearrange("p (h q) -> p h q", h=4), writes=[B["cm"]])
        fw.dma("pool", id_sb[:], ident[:, :], writes=[B["id"]])
        fw.op("dve", lambda e: e.tensor_scalar(out=nb_sb[:], in0=nb_sb[:], scalar1=-1.0, scalar2=None, op0=ALU.mult),
              reads=[B["nb"]], writes=[B["nb"]])
        fw.op("dve", lambda e: e.memset(rm_sb[:], 1.0), writes=[B["rm"]])
        for c in range(4):
            fw.op("dve", lambda e, c=c: e.memset(rm_sb[:, c * 128:c * 128 + 1], 0.0), writes=[B["rm"]])
        fw.op("dve", lambda e: e.memset(S_all[:], 0.0), writes=B["S"])
        fw.op("dve", lambda e: e.memset(Sb_all[:], 0.0), writes=[B["Sb"]])
        pjc = [0]
        khc = [0]
        gsc = [0]
        bc = [0]

        def proj(lhs_fn, rhs_fn, reads, cols, rows=128):
            a = pjc[0] % 2
            pjc[0] += 1
            for kc in range(KC):
                fw.op("pe", lambda e, kc=kc: e.matmul(pj_ps[a][0:rows, cols], lhsT=lhs_fn(kc), rhs=rhs_fn(kc),
                                                      start=(kc == 0), stop=(kc == KC - 1)),
                      reads=reads, writes=[B["pj_ps"][a]], inc=(kc == KC - 1))
            return a

        def load_x(t):
            fw.dma("pool", xTb[t % 2][:], xT_v[:, :, t * TT:(t + 1) * TT], writes=[B["xTb"][t % 2]])

        kq_ps = [kv_ps[i][:].rearrange("p a b -> p (a b)") for i in range(2)]
        pending = []
        at_bf = at_ps[:].bitcast(BF16)

        def og_round(tt_, h, half):
            xq = tt_ % 2
            tp, tb = (tr_ps, B["tr_ps"]) if half == 0 else (at_bf, B["at_ps"])
            for c in range(4):
                fw.op("pe", lambda e, c=c: e.transpose(tp[:, c, 0:128], og_all[:, c, h, half * 128:(half + 1) * 128], id_sb[:]),
                      reads=[B["og"], B["id"]], writes=[tb], inc=(c == 3))
            fw.op("dve", lambda e: e.tensor_copy(out=ogT_sb[xq][:, 2 * h + half, :].rearrange("p (c i) -> p c i", c=4),
                                                 in_=tp[:, :, 0:128]),
                  reads=[tb], writes=[B["ogT"][xq]])

        def og_store(tt_):
            xq = tt_ % 2
            c0 = (tt_ - 8) * TT
            fw.dma("sp", ogT_v[:, :, c0:c0 + TT], ogT_sb[xq][:], reads=[B["ogT"][xq]], writes=[B_ogT], owner=B["ogT"][xq])

        load_x(0)
        for h in range(4):
            load_wblk("q%d" % h, h * 128, (h + 1) * 128)
            load_wblk("g%d" % h, 2048 + h * 256, 2048 + (h + 1) * 256)
        hoisted = set()

        def vproj(tn, h):
            xq = tn % 2
            for c in range(4):
                va = pjc[0] % 2
                pjc[0] += 1
                for kc in range(KC):
                    fw.op("pe", lambda e, c=c, kc=kc, va=va: e.matmul(
                        pj_ps[va][:, 0:256], lhsT=xTb[xq][:, kc, c * 128:(c + 1) * 128],
                        rhs=w_sb[:, kc, 1024 + h * 256:1024 + (h + 1) * 256],
                        start=(kc == 0), stop=(kc == KC - 1)),
                        reads=[B["xTb"][xq], WB["v%d" % h]], writes=[B["pj_ps"][va]], inc=(kc == KC - 1))
                fw.op("dve", lambda e, c=c, va=va: e.tensor_copy(out=v_hp[xq][h][:, c, :], in_=pj_ps[va][:, 0:256]),
                      reads=[B["pj_ps"][va]], writes=[BP["v"][xq][h]])

        deferred = []
        for t in range(16):
            xp = t % 2
            full = t >= 8
            eq_h, Khtok_h, v_h = eq_hp[xp], Khtok_hp[xp], v_hp[xp]
            B["eq"], B["Khtok"], B["v"] = BP["eq"][xp], BP["Khtok"][xp], BP["v"][xp]
            xr = [B["xTb"][xp]]
            a = proj(lambda kc: w_sb[:, kc, 3072:3088], lambda kc: xTb[xp][:, kc, :], xr + [WB["a"]], slice(0, TT), rows=16)
            fw.op("act", lambda e, a=a: e.activation(out=aT_sb[:], in_=pj_ps[a][0:16, :], func=AF.Copy),
                  reads=[B["pj_ps"][a]], writes=[B["aT"]])
            if t + 1 < 16:
                load_x(t + 1)
            for h in range(4):
                for _ in range(2):
                    if pending:
                        og_round(*pending.pop(0))
                if deferred:
                    fn, cc = deferred.pop(0)
                    fn(cc)
                    B["eq"], B["Khtok"], B["v"] = BP["eq"][xp], BP["Khtok"][xp], BP["v"][xp]
                if h == 3 and t >= 9:
                    og_store(t - 1)
                az = pjc[0] % 2
                pjc[0] += 1
                fw.op("pe", lambda e, h=h, az=az: e.matmul(pj_ps[az][:], lhsT=wa2_sb[:, h * 128:(h + 1) * 128], rhs=aT_sb[:], start=True, stop=True),
                      reads=[B["wa2"], B["aT"]], writes=[B["pj_ps"][az]])
                for kc in range(KC):
                    fw.op("pe", lambda e, kc=kc, h=h: e.matmul(kq_ps[0], lhsT=w_sb[:, kc, 512 + h * 128:512 + (h + 1) * 128], rhs=xTb[xp][:, kc, :],
                                                               start=(kc == 0), stop=(kc == KC - 1)),
                          reads=xr + [WB["k%d" % h]], writes=[B["kv_ps"][0]], inc=(kc == KC - 1))
                if full:
                    for kc in range(KC):
                        fw.op("pe", lambda e, kc=kc, h=h: e.matmul(kq_ps[1], lhsT=w_sb[:, kc, h * 128:(h + 1) * 128], rhs=xTb[xp][:, kc, :],
                                                                   start=(kc == 0), stop=(kc == KC - 1)),
                              reads=xr + [WB["q%d" % h]], writes=[B["kv_ps"][1]], inc=(kc == KC - 1))
                fw.op("act", lambda e, h=h, az=az: e.activation(out=e1_sb[:], in_=pj_ps[az][:], func=AF.Exp, bias=nb_sb[:, h:h + 1], scale=-1.0),
                      reads=[B["pj_ps"][az], B["nb"]], writes=[B["e1"]])
                fw.op("act", lambda e: e.activation(out=sp_sb[:], in_=e1_sb[:], func=AF.Ln, bias=1.0, scale=1.0),
                      reads=[B["e1"]], writes=[B["sp"]])
                fw.op("dve", lambda e: e.tensor_tensor_scan(out=cs_sb[:], data0=rm_sb[:], data1=sp_sb[:], initial=0.0,
                                                            op0=ALU.mult, op1=ALU.add),
                      reads=[B["rm"], B["sp"]], writes=[B["cs"]])
                fw.op("act", lambda e, h=h: e.activation(out=eq_h[h][:], in_=cs_sb[:], func=AF.Exp, scale=-1.0 / 16.0),
                      reads=[B["cs"]], writes=[B["eq"][h]])
                fw.op("act", lambda e: e.activation(out=ek_sb[:], in_=cs_sb[:], func=AF.Exp, scale=1.0 / 16.0),
                      reads=[B["cs"]], writes=[B["ek"]])
                vdone = t in hoisted
                for c in range(4):
                    va = pjc[0] % 2
                    pjc[0] += 1
                    for kc in range(KC):
                        if vdone:
                            break
                        fw.op("pe", lambda e, c=c, kc=kc, h=h, va=va: e.matmul(
                            pj_ps[va][:, 0:256], lhsT=xTb[xp][:, kc, c * 128:(c + 1) * 128],
                            rhs=w_sb[:, kc, 1024 + h * 256:1024 + (h + 1) * 256],
                            start=(kc == 0), stop=(kc == KC - 1)),
                            reads=xr + [WB["v%d" % h]], writes=[B["pj_ps"][va]], inc=(kc == KC - 1 and not full))
                    if full:
                        for kc in range(KC):
                            fw.op("pe", lambda e, c=c, kc=kc, h=h, va=va: e.matmul(
                                pj_ps[va][:, 256:512], lhsT=xTb[xp][:, kc, c * 128:(c + 1) * 128],
                                rhs=w_sb[:, kc, 2048 + h * 256:2048 + (h + 1) * 256],
                                start=(kc == 0), stop=(kc == KC - 1)),
                                reads=xr + [WB["g%d" % h]], writes=[B["pj_ps"][va]], inc=(kc == KC - 1))
                    if not vdone:
                        fw.op("act", lambda e, c=c, h=h, va=va: e.activation(out=v_h[h][:, c, :], in_=pj_ps[va][:, 0:256], func=AF.Copy),
                              reads=[B["pj_ps"][va]], writes=[B["v"][h]])
                    if full:
                        gq = gsc[0] % 2
                        gsc[0] += 1
                        fw.op("act", lambda e, va=va, gq=gq: e.activation(out=gs_sb[gq][:], in_=pj_ps[va][:, 256:512], func=AF.Silu),
                              reads=[B["pj_ps"][va]], writes=[B["gs"][gq]])
                        fw.op("pool", lambda e, c=c, h=h, gq=gq: e.tensor_tensor(out=gn_all[:, c, h, :], in0=gs_sb[gq][:],
                                                                                 in1=ng_sb[:, h * 256:(h + 1) * 256], op=ALU.mult),
                              reads=[B["gs"][gq], B["ng"]], writes=[B["gn"][c]])
                kp = khc[0] % 2
                khc[0] += 1
                if full:
                    fw.op("dve", lambda e, h=h: e.tensor_tensor(out=Kt_h[h][:], in0=kq_ps[0], in1=ek_sb[:], op=ALU.mult),
                          reads=[B["kv_ps"][0], B["ek"]], writes=[B["Kt"][h]])
                for c in range(4):
                    cl = c * 128 + 127
                    fw.op("dve", lambda e, c=c, cl=cl, h=h: e.scalar_tensor_tensor(
                        out=Kh[kp][:, c * 128:(c + 1) * 128], in0=kq_ps[0][:, c * 128:(c + 1) * 128],
                        scalar=eq_h[h][:, cl:cl + 1], in1=ek_sb[:, c * 128:(c + 1) * 128],
                        op0=ALU.mult, op1=ALU.mult),
                        reads=[B["kv_ps"][0], B["eq"][h], B["ek"]], writes=[B["Kh"][kp]])
                if full:
                    fw.op("dve", lambda e, h=h: e.scalar_tensor_tensor(out=Qt_h[h][:], in0=kq_ps[1], scalar=QS, in1=eq_h[h][:],
                                                                       op0=ALU.mult, op1=ALU.mult),
                          reads=[B["kv_ps"][1], B["eq"][h]], writes=[B["Qt"][h]])
                for c in range(4):
                    fw.op("pe", lambda e, c=c: e.transpose(tr_ps[:, c, 0:128], Kh[kp][:, c * 128:(c + 1) * 128], id_sb[:]),
                          reads=[B["Kh"][kp], B["id"]], writes=[B["tr_ps"]], inc=(c == 3))
                fw.op("dve", lambda e, h=h: e.tensor_copy(out=Khtok_h[h][:], in_=tr_ps[:, :, 0:128]),
                      reads=[B["tr_ps"]], writes=[B["Khtok"][h]])
            def norm(c, a):
                for h in range(4):
                    fw.op("act", lambda e, h=h: e.activation(out=sq_sb[:, h, :], in_=o_sb[a][:, h, :], func=AF.Square,
                                                             accum_out=ss_sb[:, h:h + 1]),
                          reads=[B["o_sb"][a]], writes=[B["sq"], B["ss"]])
                fw.op("act", lambda e: e.activation(out=ss_sb[:], in_=ss_sb[:], func=AF.Ln, bias=RMS_EPS, scale=1.0 / 256.0),
                      reads=[B["ss"]], writes=[B["ss"]])
                fw.op("act", lambda e: e.activation(out=rs_sb[:], in_=ss_sb[:], func=AF.Exp, scale=-0.5),
                      reads=[B["ss"]], writes=[B["rs"]])
                fw.op("pool", lambda e: e.tensor_tensor(out=tmp_sb[:], in0=o_sb[a][:], in1=gn_all[:, c, :, :], op=ALU.mult),
                      reads=[B["o_sb"][a], B["gn"][c]], writes=[B["tmp"]])
                fw.op("dve", lambda e: e.tensor_tensor(out=og_all[:, c, :, :], in0=tmp_sb[:],
                                                       in1=rs_sb[:, 0:4].unsqueeze(2).to_broadcast([128, 4, 256]), op=ALU.mult),
                      reads=[B["tmp"], B["rs"]], writes=[B["og"]])

            def stage_b(c, t=t, full=full, eq_h=eq_h, Khtok_h=Khtok_h, v_h=v_h,
                        Beq=B["eq"], BKh=B["Khtok"], Bv=B["v"]):
                B["eq"], B["Khtok"], B["v"] = Beq, BKh, Bv
                sl = slice(c * 128, (c + 1) * 128)
                cl = c * 128 + 127
                a = bc[0] % 2
                bc[0] += 1
                if full:
                    for h in range(4):
                        fw.op("pe", lambda e, h=h: e.matmul(at_ps[:, h, :], lhsT=Kt_h[h][:, sl], rhs=Qt_h[h][:, sl], start=True, stop=True),
                              reads=[B["Kt"][h], B["Qt"][h]], writes=[B["at_ps"]], inc=(h == 3))
                for h in range(4):
                    fw.op("pe", lambda e, h=h: e.matmul(kv_ps[h // 2][:, h % 2, :], lhsT=Khtok_h[h][:, c, :], rhs=v_h[h][:, c, :],
                                                        start=True, stop=True),
                          reads=[B["Khtok"][h], B["v"][h]], writes=[B["kv_ps"][h // 2]], inc=(h % 2 == 1))
                if full:
                    fw.op("dve", lambda e: e.tensor_tensor(out=att_sb[a][:], in0=at_ps[:], in1=cm_sb[:], op=ALU.mult),
                          reads=[B["at_ps"], B["cm"]], writes=[B["att"][a]])
                for h in range(4):
                    fw.op("dve", lambda e, h=h: e.scalar_tensor_tensor(out=S_all[:, h, :], in0=S_all[:, h, :], scalar=eq_h[h][:, cl:cl + 1],
                                                                       in1=kv_ps[h // 2][:, h % 2, :], op0=ALU.mult, op1=ALU.add),
                          reads=[B["S"][h], B["eq"][h], B["kv_ps"][h // 2]], writes=[B["S"][h]])
                if full and prevbox[0] is not None:
                    norm(*prevbox[0])
                if full:
                    for h in range(4):
                        fw.op("pe", lambda e, h=h: e.matmul(o_ps[h // 2][:, h % 2, :], lhsT=att_sb[a][:, h, :], rhs=v_h[h][:, c, :],
                                                            start=True, stop=False),
                              reads=[B["att"][a], B["v"][h]], writes=[B["o_ps"][h // 2]], inc=False)
                        fw.op("pe", lambda e, h=h: e.matmul(o_ps[h // 2][:, h % 2, :], lhsT=Qt_h[h][:, sl], rhs=Sb_all[:, h, :],
                                                            start=False, stop=True),
                              reads=[B["Qt"][h], B["Sb"]], writes=[B["o_ps"][h // 2]], inc=(h % 2 == 1))
                    for i in range(2):
                        fw.op("act", lambda e, i=i: e.activation(out=o_sb[a][:, 2 * i:2 * i + 2, :], in_=o_ps[i][:], func=AF.Copy),
                              reads=[B["o_ps"][i]], writes=[B["o_sb"][a]])
                if t >= 7:
                    fw.op("act", lambda e: e.activation(out=Sb_all[:], in_=S_all[:], func=AF.Copy),
                          reads=B["S"], writes=[B["Sb"]])
                if full:
                    prevbox[0] = (c, a)

            if full:
                prevbox = [None]
                for c in range(4):
                    stage_b(c)
                    if t + 1 < 16:
                        vproj(t + 1, c)
                if t + 1 < 16:
                    hoisted.add(t + 1)
            else:
                deferred.extend(stage_b for _ in range(0))
                deferred.extend([(stage_b, c) for c in range(4)])
            if full:
                norm(*prevbox[0])
                pending.extend((t, h, half) for h in range(4) for half in range(2))
                if t == 15:
                    while pending:
                        og_round(*pending.pop(0))
                    og_store(15)
        fw.barrier()
```
